# Optimizing a Trainium2 kernel written in Bass

```python
import math
import jax, jax.numpy as jnp
from jax import lax
import numpy as np

D_MODEL = 2048
BATCH = 2
SEQ = 16384
DEPTH = 2

GRID_W = 64
CTX_LEN = 256
N_EVEN = (DEPTH + 1) // 2
N_ODD = DEPTH // 2
CONV_W = 3 * D_MODEL // 4
CONV_K = 31
SSM_W = D_MODEL // 4
SSM_P = 16
SSM_G = SSM_W // SSM_P
SSM_N = 64
EVEN_IN = 2 * CONV_W + SSM_W
EVEN_MIX = CONV_W + SSM_W
SGU_W = D_MODEL
SGU_CHUNK = 128
SGU_HEADS = 8
SGU_HD = SGU_W // SGU_HEADS
D_FF = 2 * D_MODEL
RMS_EPS = 1e-6
LN_EPS = 1e-5

kernel_name = 'hybrid_conv_s5_sgu_dit_block'


def rms_norm(x, g):
    x32 = x.astype(jnp.float32)
    y = x32 * lax.rsqrt(jnp.mean(x32 * x32, axis=-1, keepdims=True) + RMS_EPS)
    return (y * g.astype(jnp.float32)).astype(x.dtype)


def layer_norm(x, g, b):
    x32 = x.astype(jnp.float32)
    mu = jnp.mean(x32, axis=-1, keepdims=True)
    xc = x32 - mu
    var = jnp.mean(xc * xc, axis=-1, keepdims=True)
    return (xc * lax.rsqrt(var + LN_EPS) * g.astype(jnp.float32) + b.astype(jnp.float32)).astype(x.dtype)


def modulate(h, shift, scale):
    return h * (1.0 + scale[:, None, :]) + shift[:, None, :]


def dwconv1d(x, w, b):
    k = w.shape[0]
    y = lax.conv_general_dilated(x, w[:, None, :].astype(x.dtype), window_strides=(1,),
                                 padding=[(k // 2, k // 2)],
                                 dimension_numbers=('NWC', 'WIO', 'NWC'),
                                 feature_group_count=x.shape[-1])
    return y + b.astype(x.dtype)


def dwconv_grid(x, w, b):
    bn, length, ch = x.shape
    rows = length // GRID_W
    xg = x.reshape(bn, rows, GRID_W, ch)
    y = lax.conv_general_dilated(xg, w[:, :, None, :].astype(x.dtype), window_strides=(1, 1),
                                 padding=[(1, 1), (1, 1)],
                                 dimension_numbers=('NHWC', 'HWIO', 'NHWC'),
                                 feature_group_count=ch)
    return y.reshape(bn, length, ch) + b.astype(x.dtype)


def conformer_conv(a_val, a_gate, conv_w, conv_b, ln_g, ln_b):
    z = a_val * jax.nn.sigmoid(a_gate)
    z = dwconv1d(z, conv_w, conv_b)
    z = layer_norm(z, ln_g, ln_b)
    return jax.nn.silu(z)


def s5_discretize(a_re, a_im, log_step, b_re, b_im):
    a_re = a_re.astype(jnp.float32)
    a_im = a_im.astype(jnp.float32)
    b_re = b_re.astype(jnp.float32)
    b_im = b_im.astype(jnp.float32)
    dt = jnp.exp(log_step.astype(jnp.float32))[:, None]
    mag = jnp.exp(a_re * dt)
    ang = a_im * dt
    ab_re = mag * jnp.cos(ang)
    ab_im = mag * jnp.sin(ang)
    den = a_re * a_re + a_im * a_im
    num_re = ab_re - 1.0
    f_re = (num_re * a_re + ab_im * a_im) / den
    f_im = (ab_im * a_re - num_re * a_im) / den
    bb_re = f_re[..., None] * b_re - f_im[..., None] * b_im
    bb_im = f_re[..., None] * b_im + f_im[..., None] * b_re
    return ab_re, ab_im, bb_re, bb_im


def _linear_recurrence_combine(e_i, e_j):
    ai_re, ai_im, bi_re, bi_im = e_i
    aj_re, aj_im, bj_re, bj_im = e_j
    return (aj_re * ai_re - aj_im * ai_im,
            aj_re * ai_im + aj_im * ai_re,
            aj_re * bi_re - aj_im * bi_im + bj_re,
            aj_re * bi_im + aj_im * bi_re + bj_im)


def s5_states(u, a_re, a_im, log_step, b_re, b_im, h0, reverse):
    ab_re, ab_im, bb_re, bb_im = s5_discretize(a_re, a_im, log_step, b_re, b_im)
    bu_re = jnp.einsum('blgp,gnp->lbgn', u, bb_re)
    bu_im = jnp.einsum('blgp,gnp->lbgn', u, bb_im)
    length = u.shape[1]
    if h0 is not None:
        idx = length - 1 if reverse else 0
        h0_re, h0_im = h0
        bu_re = bu_re.at[idx].add(ab_re * h0_re - ab_im * h0_im)
        bu_im = bu_im.at[idx].add(ab_re * h0_im + ab_im * h0_re)
    a_re_l = jnp.broadcast_to(ab_re, (length, 1) + ab_re.shape)
    a_im_l = jnp.broadcast_to(ab_im, (length, 1) + ab_im.shape)
    _, _, h_re, h_im = lax.associative_scan(_linear_recurrence_combine,
                                            (a_re_l, a_im_l, bu_re, bu_im),
                                            reverse=reverse, axis=0)
    return h_re, h_im


def s5_bidir_states(u, a_re, a_im, log_step, b_re, b_im, h0s):
    states = []
    for d in range(2):
        h0 = None if h0s is None else h0s[d]
        states.append(s5_states(u, a_re[d], a_im[d], log_step[d], b_re[d], b_im[d], h0, reverse=(d == 1)))
    return states


def s5_finals(states):
    (f_re, f_im), (b_re, b_im) = states
    return [(f_re[-1], f_im[-1]), (b_re[0], b_im[0])]


def s5_readout(states, u, c_re, c_im, d_skip, glu_w, glu_b):
    bn, length = u.shape[:2]
    y = d_skip.astype(jnp.float32) * u.reshape(bn, length, SSM_W)
    for d, (h_re, h_im) in enumerate(states):
        y_d = (jnp.einsum('lbgn,gpn->blgp', h_re, c_re[d].astype(jnp.float32))
               - jnp.einsum('lbgn,gpn->blgp', h_im, c_im[d].astype(jnp.float32)))
        y = y + y_d.reshape(bn, length, SSM_W)
    y = jax.nn.gelu(y)
    return y * jax.nn.sigmoid(y @ glu_w.astype(jnp.float32) + glu_b.astype(jnp.float32))


def even_mixer(hn, w_in, conv_w, conv_b, ln_g, ln_b, ssm_in, ssm_out, w_out, h0s, want_finals):
    bn, length, _ = hn.shape
    proj = hn @ w_in
    a_val, a_gate, u = jnp.split(proj, [CONV_W, 2 * CONV_W], axis=-1)
    y_conv = conformer_conv(a_val, a_gate, conv_w, conv_b, ln_g, ln_b)
    u32 = u.astype(jnp.float32).reshape(bn, length, SSM_G, SSM_P)
    states = s5_bidir_states(u32, *ssm_in, h0s)
    y_ssm = s5_readout(states, u32, *ssm_out).astype(hn.dtype)
    out = jnp.concatenate([y_conv, y_ssm], axis=-1) @ w_out
    finals = s5_finals(states) if want_finals else None
    return out, finals


def chunk_mlp_mixer(hn, w_in, ln_g, ln_b, sgu_w, sgu_b, w_out):
    bn, length, _ = hn.shape
    z = jax.nn.gelu(hn @ w_in)
    u, v = jnp.split(z, 2, axis=-1)
    v = layer_norm(v, ln_g, ln_b)
    v = v.reshape(bn, length // SGU_CHUNK, SGU_CHUNK, SGU_HEADS, SGU_HD)
    s = jnp.einsum('hqk,bnkhc->bnqhc', sgu_w.astype(v.dtype), v)
    s = s + jnp.transpose(sgu_b).astype(v.dtype)[None, None, :, :, None]
    return (u * s.reshape(bn, length, SGU_W)) @ w_out


def conv_ffn(hn, w_in, conv_w, conv_b, w_out, on_grid):
    z = hn @ w_in
    z = dwconv_grid(z, conv_w, conv_b) if on_grid else dwconv1d(z, conv_w[1], conv_b)
    a, g = jnp.split(z, 2, axis=-1)
    return (a * jax.nn.silu(g)) @ w_out


def setup_inputs(seed: int = 0) -> dict:
    key = jax.random.key(seed)
    ks = iter(jax.random.split(key, 48))

    def nrm(shape, scale):
        return jax.random.normal(next(ks), shape, jnp.float32) * scale

    d = D_MODEL
    inputs = {}
    inputs['x'] = nrm((BATCH, SEQ, d), 1.0)
    inputs['c'] = nrm((BATCH, d), 1.0)
    inputs['ctx'] = nrm((BATCH, CTX_LEN, d), 1.0)
    inputs['c_ctx'] = nrm((d,), 1.0)
    inputs['ada_w'] = nrm((DEPTH, d, 6 * d), 0.5 * d ** -0.5)
    inputs['ada_b'] = nrm((DEPTH, 6 * d), 0.02)
    inputs['norm_g'] = 1.0 + nrm((DEPTH, 2, d), 0.02)
    inputs['final_g'] = 1.0 + nrm((d,), 0.02)
    inputs['e_w_in'] = nrm((N_EVEN, d, EVEN_IN), d ** -0.5)
    inputs['e_conv_w'] = nrm((N_EVEN, CONV_K, CONV_W), CONV_K ** -0.5)
    inputs['e_conv_b'] = nrm((N_EVEN, CONV_W), 0.02)
    inputs['e_ln_g'] = 1.0 + nrm((N_EVEN, CONV_W), 0.02)
    inputs['e_ln_b'] = nrm((N_EVEN, CONV_W), 0.02)
    inputs['s5_a_re'] = -0.5 + nrm((N_EVEN, 2, SSM_G, SSM_N), 0.01)
    inputs['s5_a_im'] = jnp.broadcast_to(jnp.pi * jnp.arange(SSM_N, dtype=jnp.float32),
                                         (N_EVEN, 2, SSM_G, SSM_N))
    inputs['s5_log_step'] = jax.random.uniform(next(ks), (N_EVEN, 2, SSM_G), jnp.float32,
                                               minval=math.log(1e-3), maxval=math.log(1e-1))
    inputs['s5_b_re'] = nrm((N_EVEN, 2, SSM_G, SSM_N, SSM_P), (2 * SSM_P) ** -0.5)
    inputs['s5_b_im'] = nrm((N_EVEN, 2, SSM_G, SSM_N, SSM_P), (2 * SSM_P) ** -0.5)
    inputs['s5_c_re'] = nrm((N_EVEN, 2, SSM_G, SSM_P, SSM_N), SSM_N ** -0.5)
    inputs['s5_c_im'] = nrm((N_EVEN, 2, SSM_G, SSM_P, SSM_N), SSM_N ** -0.5)
    inputs['s5_d'] = nrm((N_EVEN, SSM_W), 1.0)
    inputs['s5_glu_w'] = nrm((N_EVEN, SSM_W, SSM_W), SSM_W ** -0.5)
    inputs['s5_glu_b'] = nrm((N_EVEN, SSM_W), 0.02)
    inputs['e_w_out'] = nrm((N_EVEN, EVEN_MIX, d), EVEN_MIX ** -0.5)
    inputs['o_w_in'] = nrm((N_ODD, d, 2 * SGU_W), d ** -0.5)
    inputs['o_ln_g'] = 1.0 + nrm((N_ODD, SGU_W), 0.02)
    inputs['o_ln_b'] = nrm((N_ODD, SGU_W), 0.02)
    inputs['o_sgu_w'] = nrm((N_ODD, SGU_HEADS, SGU_CHUNK, SGU_CHUNK), SGU_CHUNK ** -0.5)
    inputs['o_sgu_b'] = 1.0 + nrm((N_ODD, SGU_HEADS, SGU_CHUNK), 0.1)
    inputs['o_w_out'] = nrm((N_ODD, SGU_W, d), SGU_W ** -0.5)
    inputs['f_w_in'] = nrm((DEPTH, d, 2 * D_FF), d ** -0.5)
    inputs['f_conv_w'] = nrm((DEPTH, 3, 3, 2 * D_FF), 1.0 / 3.0)
    inputs['f_conv_b'] = nrm((DEPTH, 2 * D_FF), 0.02)
    inputs['f_w_out'] = nrm((DEPTH, D_FF, d), D_FF ** -0.5)
    return inputs


def reference(x, c, ctx, c_ctx, ada_w, ada_b, norm_g, final_g,
              e_w_in, e_conv_w, e_conv_b, e_ln_g, e_ln_b,
              s5_a_re, s5_a_im, s5_log_step, s5_b_re, s5_b_im, s5_c_re, s5_c_im,
              s5_d, s5_glu_w, s5_glu_b, e_w_out,
              o_w_in, o_ln_g, o_ln_b, o_sgu_w, o_sgu_b, o_w_out,
              f_w_in, f_conv_w, f_conv_b, f_w_out):
    d = D_MODEL
    h_lat = x
    h_ctx = ctx
    silu_c = jax.nn.silu(c)
    silu_cc = jax.nn.silu(c_ctx)[None, :]
    for layer in range(DEPTH):
        is_even = layer % 2 == 0
        li = layer // 2
        ctx_out_needed = any(j % 2 == 0 for j in range(layer + 1, DEPTH))
        n_ctx_mod = 6 if ctx_out_needed else (2 if is_even else 0)

        sh1, sc1, g1, sh2, sc2, g2 = jnp.split(silu_c @ ada_w[layer] + ada_b[layer], 6, axis=-1)
        if n_ctx_mod:
            cmod = jnp.split(silu_cc @ ada_w[layer][:, :n_ctx_mod * d] + ada_b[layer][:n_ctx_mod * d],
                             n_ctx_mod, axis=-1)
            cn = modulate(rms_norm(h_ctx, norm_g[layer, 0]), cmod[0], cmod[1])
        hn = modulate(rms_norm(h_lat, norm_g[layer, 0]), sh1, sc1)

        if is_even:
            ssm_in = (s5_a_re[li], s5_a_im[li], s5_log_step[li], s5_b_re[li], s5_b_im[li])
            ssm_out = (s5_c_re[li], s5_c_im[li], s5_d[li], s5_glu_w[li], s5_glu_b[li])
            if ctx_out_needed:
                c_mix, h0s = even_mixer(cn, e_w_in[li], e_conv_w[li], e_conv_b[li], e_ln_g[li], e_ln_b[li],
                                        ssm_in, ssm_out, e_w_out[li], None, True)
            else:
                u_c = (cn @ e_w_in[li][:, 2 * CONV_W:]).astype(jnp.float32)
                u_c = u_c.reshape(u_c.shape[0], u_c.shape[1], SSM_G, SSM_P)
                h0s = s5_finals(s5_bidir_states(u_c, *ssm_in, None))
            mix, _ = even_mixer(hn, e_w_in[li], e_conv_w[li], e_conv_b[li], e_ln_g[li], e_ln_b[li],
                                ssm_in, ssm_out, e_w_out[li], h0s, False)
        else:
            mix = chunk_mlp_mixer(hn, o_w_in[li], o_ln_g[li], o_ln_b[li], o_sgu_w[li], o_sgu_b[li], o_w_out[li])
            if ctx_out_needed:
                c_mix = chunk_mlp_mixer(cn, o_w_in[li], o_ln_g[li], o_ln_b[li], o_sgu_w[li], o_sgu_b[li],
                                        o_w_out[li])

        h_lat = h_lat + g1[:, None, :] * mix
        hn2 = modulate(rms_norm(h_lat, norm_g[layer, 1]), sh2, sc2)
        h_lat = h_lat + g2[:, None, :] * conv_ffn(hn2, f_w_in[layer], f_conv_w[layer], f_conv_b[layer],
                                                  f_w_out[layer], True)
        if ctx_out_needed:
            h_ctx = h_ctx + cmod[2][:, None, :] * c_mix
            cn2 = modulate(rms_norm(h_ctx, norm_g[layer, 1]), cmod[3], cmod[4])
            h_ctx = h_ctx + cmod[5][:, None, :] * conv_ffn(cn2, f_w_in[layer], f_conv_w[layer], f_conv_b[layer],
                                                           f_w_out[layer], False)
    return rms_norm(h_lat, final_g)
```

```python
import numpy as np
import concourse.bass as bass
import concourse.mybir as mybir
from concourse.bass_utils import run_bass_kernel_spmd
from contextlib import ExitStack

F32 = mybir.dt.float32
BF16 = mybir.dt.bfloat16
AF = mybir.ActivationFunctionType
ALU = mybir.AluOpType

D = 2048
TE = 4608
NT = 9
FAR = 11776
LAG = 64
ENGS = ['pe', 'act', 'dve', 'pool', 'sp']
SAME_ENG_WAIT = True
POOL_SCALE = True


class Sched:
    def __init__(self, nc, es):
        self.nc = nc
        self.es = es
        self.q = {e: [] for e in ENGS}
        self.sem = {}
        self.cnt = {}
        self.known = {e: {} for e in ENGS}
        self.lastw = {}
        self.readers = {}
        self.nops = 0
        self.deferred = None
        self.nodrain = set()

    def defer_begin(self):
        self.deferred = []

    def defer_end(self):
        d = self.deferred
        self.deferred = None
        return d

    def replay(self, lst, n):
        for _ in range(min(n, len(lst))):
            e, fn, reads, writes, dma = lst.pop(0)
            self.op(e, fn, reads, writes, dma)

    def _sem(self, name):
        if name not in self.sem:
            self.sem[name] = self.es.enter_context(self.nc.semaphore(name))
            self.cnt[name] = 0
        return self.sem[name]

    def op(self, e, fn, reads=(), writes=(), dma=None):
        if self.deferred is not None:
            self.deferred.append((e, fn, list(reads), list(writes), dma))
            return None
        deps = []
        raw = set()
        for r in reads:
            if r in self.lastw:
                deps.append(self.lastw[r])
                raw.add(self.lastw[r])
        for w in writes:
            if w in self.lastw:
                deps.append(self.lastw[w])
            deps.extend(self.readers.get(w, {}).items())
        if dma is None:
            sname, inc = 'S_' + e, 1
        else:
            sname, inc = 'D_' + dma, 16
        self._sem(sname)
        waits = {}
        for (sn, v) in deps:
            if sn == 'S_' + e and (e == 'pe' or not SAME_ENG_WAIT or (sn, v) not in raw):
                continue
            if self.known[e].get(sn, 0) >= v:
                continue
            waits[sn] = max(waits.get(sn, 0), v)
        for sn, v in waits.items():
            self.known[e][sn] = v
        self.cnt[sname] += inc
        tok = (sname, self.cnt[sname])
        self.q[e].append((list(waits.items()), fn, sname, inc))
        for w in writes:
            self.lastw[w] = tok
            self.readers[w] = {}
        for r in reads:
            d = self.readers.setdefault(r, {})
            d[tok[0]] = max(d.get(tok[0], 0), tok[1])
        self.nops += 1
        return tok

    def drain(self, e='sp'):
        waits = []
        for sn, v in self.cnt.items():
            if v > 0 and self.known[e].get(sn, 0) < v and sn != 'S_' + e and sn not in self.nodrain:
                waits.append((sn, v))
                self.known[e][sn] = v
        self.q[e].append((waits, None, None, 0))

    def emit(self):
        nc = self.nc
        engobj = {'pe': 'tensor', 'act': 'scalar', 'dve': 'vector', 'pool': 'gpsimd', 'sp': 'sync'}
        with nc.Block() as block:
            for e in ENGS:
                def body(eng, e=e):
                    for waits, fn, sname, inc in self.q[e]:
                        for sn, v in waits:
                            eng.wait_ge(self.sem[sn], v)
                        if fn is not None:
                            ins = fn(eng)
                            ins.then_inc(self.sem[sname], inc)
                getattr(block, engobj[e])(body)
        self.q = {e: [] for e in ENGS}


PARAMS = [
    ('ada_w', [2, 2048, 12288]), ('ada_b', [2, 12288]), ('norm_g', [2, 2, 2048]), ('final_g', [2048]),
    ('e_w_in', [1, 2048, 3584]), ('e_conv_w', [1, 31, 1536]), ('e_conv_b', [1, 1536]),
    ('e_ln_g', [1, 1536]), ('e_ln_b', [1, 1536]),
    ('s5_a_re', [1, 2, 32, 64]), ('s5_a_im', [1, 2, 32, 64]), ('s5_log_step', [1, 2, 32]),
    ('s5_b_re', [1, 2, 32, 64, 16]), ('s5_b_im', [1, 2, 32, 64, 16]),
    ('s5_c_re', [1, 2, 32, 16, 64]), ('s5_c_im', [1, 2, 32, 16, 64]),
    ('s5_d', [1, 512]), ('s5_glu_w', [1, 512, 512]), ('s5_glu_b', [1, 512]), ('e_w_out', [1, 2048, 2048]),
    ('o_w_in', [1, 2048, 4096]), ('o_ln_g', [1, 2048]), ('o_ln_b', [1, 2048]),
    ('o_sgu_w', [1, 8, 128, 128]), ('o_sgu_b', [1, 8, 128]), ('o_w_out', [1, 2048, 2048]),
    ('f_w_in', [2, 2048, 8192]), ('f_conv_w', [2, 3, 3, 8192]), ('f_conv_b', [2, 8192]),
    ('f_w_out', [2, 4096, 2048]),
]


def build(stages=('setup', 'l0mix', 'l0ffn', 'l1mix', 'l1ffn'), debug=False):
    nc = bass.Bass("TRN2", target_bir_lowering=False)

    def dram(name, shape, dtype=F32, kind="ExternalInput"):
        return nc.dram_tensor(name, list(shape), dtype, kind=kind).ap()

    xe = dram("xe", [TE, D])
    xfar = dram("xfar", [FAR, D])
    ctxb = dram("ctxb", [256, D])
    cvec = dram("cvec", [2, D])
    ohf = dram("ohf", [1, 384])
    ohb = dram("ohb", [1, 384])
    P = {n: dram(n, s) for n, s in PARAMS}
    out = dram("out", [TE, D], kind="ExternalOutput")
    hk = "ExternalOutput" if debug else "Internal"
    h1 = dram("h1", [TE, D], kind=hk)
    h2 = dram("h2", [TE, D], kind=hk)
    h3 = dram("h3", [TE, D], kind=hk)
    Wb_ein = dram("Wb_ein", [24, 128, 16, 128], BF16, kind="Internal")
    Wb_fin = [dram(f"Wb_fin{l}", [64, 128, 16, 128], BF16, kind="Internal") for l in range(2)]
    Wb_oinu = dram("Wb_oinu", [16, 128, 16, 128], BF16, kind="Internal")
    Wb_oinv = dram("Wb_oinv", [4, 128, 16, 512], BF16, kind="Internal")
    Wb_eout = dram("Wb_eout", [16, 128, 2048], BF16, kind="Internal")
    Wb_oout = dram("Wb_oout", [16, 128, 2048], BF16, kind="Internal")
    Wb_fout = [dram(f"Wb_fout{l}", [32, 128, 2048], BF16, kind="Internal") for l in range(2)]
    moddram = dram("moddram", [2, 2, 12288], kind="Internal")
    Ys = dram("Ys", [512, 8, 656], kind=hk)
    Us_own = dram("Us_own", [8, 512, 576], kind=hk)
    Us_far = dram("Us_far", [8, 512, 1536], kind=hk)

    es_g = ExitStack()
    with es_g:
        S = Sched(nc, es_g)

        def sbg(name, shape, dt=F32):
            return es_g.enter_context(nc.sbuf_tensor(name, list(shape), dt))

        PS = [es_g.enter_context(nc.psum_tensor(f"ps{i}", [128, 512], F32)) for i in range(8)]
        PK = [f"ps{i}" for i in range(8)]

        ident = sbg("ident", [128, 128])
        identb = sbg("identb", [128, 128], BF16)
        onesb = sbg("onesb", [128, 128], BF16)
        onesf = sbg("onesf", [128, 128])
        mhalf = sbg("mhalf", [128, 1])
        modc = sbg("modc", [128, 2, 96])
        modx = sbg("modx", [128, 32])
        A1 = sbg("A1", [128, 2, 16]); A2 = sbg("A2", [128, 2, 16]); Ac = sbg("Ac", [128, 16])
        fgc = sbg("fgc", [128, 16])
        cwc = sbg("cwc", [128, 31, 12]); cbc = sbg("cbc", [128, 12]); lgc = sbg("lgc", [128, 12]); lbc = sbg("lbc", [128, 12])
        fcw = sbg("fcw", [128, 2, 9, 64]); fcb = sbg("fcb", [128, 2, 64])
        glub = sbg("glub", [128, 4])
        olg = sbg("olg", [128, 16]); olb = sbg("olb", [128, 16])

        uid = [0]

        def uname(name):
            uid[0] += 1
            return f"{name}_u{uid[0]}"

        def emit_stage():
            S.drain('sp')
            S.emit()

        def cast_blocks(dst, src, nblk, tag):
            sv = src.rearrange("(kt p) (ot c) -> ot p kt c", p=128, c=128)
            return [((lambda e, ot=ot: e.dma_start(out=dst[ot], in_=sv[ot])), tag) for ot in range(nblk)]

        def cast_rows(dst, src, nkt, tag):
            return [((lambda e, kt=kt: e.dma_start(out=dst[kt], in_=src[kt * 128:(kt + 1) * 128, :])), tag) for kt in range(nkt)]

        def emit_casts(lst, n):
            for _ in range(min(n, len(lst))):
                fn, tag = lst.pop(0)
                S.op('pool', fn, writes=[tag], dma=tag)

        svv_ = P['o_w_in'][0][:, 2048:4096].rearrange("(kt p) (cg c) -> cg p kt c", p=128, c=512)
        cast0 = cast_blocks(Wb_ein, P['e_w_in'][0][:, 0:3072], 24, 'Wb_ein') + cast_rows(Wb_eout, P['e_w_out'][0], 16, 'Wb_eout')
        castA = cast_blocks(Wb_fin[0], P['f_w_in'][0], 64, 'Wb_fin0') + cast_rows(Wb_fout[0], P['f_w_out'][0], 32, 'Wb_fout0')
        castB = (cast_blocks(Wb_oinu, P['o_w_in'][0][:, 0:2048], 16, 'Wb_oinu')
                 + [((lambda e, cg=cg: e.dma_start(out=Wb_oinv[cg], in_=svv_[cg])), 'Wb_oinv') for cg in range(4)]
                 + cast_rows(Wb_oout, P['o_w_out'][0], 16, 'Wb_oout'))
        castC = cast_blocks(Wb_fin[1], P['f_w_in'][1], 64, 'Wb_fin1') + cast_rows(Wb_fout[1], P['f_w_out'][1], 32, 'Wb_fout1')
        for t_ in ('Wb_ein', 'Wb_oinu', 'Wb_oinv', 'Wb_fin0', 'Wb_fin1', 'Wb_eout', 'Wb_oout', 'Wb_fout0', 'Wb_fout1'):
            S.nodrain.add('D_' + t_)

        if 'setup' in stages:
            with ExitStack() as es:
                def sb(name, shape, dt=F32):
                    return es.enter_context(nc.sbuf_tensor(uname(name), list(shape), dt))

                S.op('pool', lambda e: e.memset(ident[:], 1.0), writes=['ident'])
                S.op('pool', lambda e: e.affine_select(out=ident[:], in_=ident[:], pattern=[[-1, 128]], compare_op=ALU.is_equal, fill=0.0, base=0, channel_multiplier=1), reads=['ident'], writes=['ident'])
                S.op('pool', lambda e: e.tensor_copy(out=identb[:], in_=ident[:]), reads=['ident'], writes=['identb'])
                S.op('pool', lambda e: e.memset(onesb[:], 1.0), writes=['onesb'])
                S.op('pool', lambda e: e.memset(onesf[:], 1.0), writes=['onesf'])
                S.op('pool', lambda e: e.memset(mhalf[:], -0.5), writes=['mhalf'])

                for t_ in ('Wb_ein', 'Wb_oinu', 'Wb_oinv', 'Wb_fin0', 'Wb_fin1', 'Wb_eout', 'Wb_oout', 'Wb_fout0', 'Wb_fout1'):
                    S.nodrain.add('D_' + t_)
                emit_casts(cast0, 10 ** 6)

                stg = sb("stg", [128, 128])
                pcol = PS[7]
                ncl = [0]

                def col_load(src2d, T, dst, dkey, post=None, rkey=None):
                    i = ncl[0]; ncl[0] += 1
                    S.op('sp', lambda e: e.dma_start(out=stg[0:T, :], in_=src2d), reads=([rkey] if rkey else []), writes=['stg'], dma='stg')
                    S.op('pe', lambda e: e.transpose(pcol[:, 0:T], stg[0:T, :], ident[0:T, 0:T]), reads=['stg', 'ident'], writes=[PK[7]])
                    S.op('dve', lambda e: e.tensor_copy(out=dst, in_=pcol[:, 0:T]), reads=[PK[7]], writes=[dkey])

                r128 = lambda ap1d: ap1d.rearrange("(t p) -> t p", p=128)
                ngc = sb("ngc", [128, 4, 16])
                for l in range(2):
                    for w in range(2):
                        col_load(r128(P['norm_g'][l, w]), 16, ngc[:, l * 2 + w, :], 'ngc')
                col_load(r128(P['final_g']), 16, fgc[:], 'fgc')
                adb = sb("adb", [128, 2, 96])
                for l in range(2):
                    col_load(r128(P['ada_b'][l]), 96, adb[:, l, :], 'adb')
                for k in range(31):
                    col_load(r128(P['e_conv_w'][0, k]), 12, cwc[:, k, :], 'cwc')
                col_load(r128(P['e_conv_b'][0]), 12, cbc[:], 'cbc')
                col_load(r128(P['e_ln_g'][0]), 12, lgc[:], 'lgc')
                col_load(r128(P['e_ln_b'][0]), 12, lbc[:], 'lbc')
                for l in range(2):
                    for t in range(9):
                        col_load(r128(P['f_conv_w'][l, t // 3, t % 3]), 64, fcw[:, l, t, :], 'fcw')
                    col_load(r128(P['f_conv_b'][l]), 64, fcb[:, l, :], 'fcb')
                col_load(r128(P['s5_glu_b'][0]), 4, glub[:], 'glub')
                col_load(r128(P['o_ln_g'][0]), 16, olg[:], 'olg')
                col_load(r128(P['o_ln_b'][0]), 16, olb[:], 'olb')
                craw = sb("craw", [128, 2, 16])
                for j in range(2):
                    col_load(r128(cvec[j]), 16, craw[:, j, :], 'craw')
                sc = sb("sc", [128, 16, 2])
                S.op('act', lambda e: e.activation(out=sc[:].rearrange("p k j -> p j k"), in_=craw[:], func=AF.Silu), reads=['craw'], writes=['sc'])

                wst = [sb(f"wst{i}", [128, 16, 512]) for i in range(3)]
                mrow = [sb(f"mrow{i}", [2, 512]) for i in range(2)]
                wsb = [sb(f"wsb{i}", [128, 16, 512], BF16) for i in range(2)]
                scb = sb("scb", [128, 16, 2], BF16)
                S.op('dve', lambda e: e.tensor_copy(out=scb[:], in_=sc[:]), reads=['sc'], writes=['scb'])
                nbb = 0
                pm = PS[5]
                nw = 0
                for l in range(2):
                    wv_ = P['ada_w'][l].rearrange("(kt p) c -> p kt c", p=128)
                    for j in range(24):
                        b = nw % 3; nw += 1
                        wk = f"wst{b}"
                        for hf in range(2):
                            S.op('sp' if hf == 0 else 'act', lambda e, b=b, j=j, hf=hf, wv_=wv_: e.dma_start(out=wst[b][:, hf * 8:(hf + 1) * 8, :], in_=wv_[:, hf * 8:(hf + 1) * 8, j * 512:(j + 1) * 512]), writes=[f"{wk}_{hf}"], dma=f"{wk}_{hf}")
                        bb = nbb % 2; nbb += 1
                        S.op('dve', lambda e, b=b, bb=bb: e.tensor_copy(out=wsb[bb][:, 0:6, :], in_=wst[b][:, 0:6, :]), reads=[f"{wk}_0"], writes=[f"wsb{bb}_a"])
                        S.op('pool', lambda e, b=b, bb=bb: e.tensor_copy(out=wsb[bb][:, 6:10, :], in_=wst[b][:, 6:10, :]), reads=[f"{wk}_0", f"{wk}_1"], writes=[f"wsb{bb}_b"])
                        S.op('act', lambda e, b=b, bb=bb: e.copy(out=wsb[bb][:, 10:16, :], in_=wst[b][:, 10:16, :]), reads=[f"{wk}_1"], writes=[f"wsb{bb}_c"])
                        for kt in range(16):
                            ck = f"wsb{bb}_a" if kt < 6 else (f"wsb{bb}_b" if kt < 10 else f"wsb{bb}_c")
                            S.op('pe', lambda e, bb=bb, kt=kt: e.matmul(pm[0:2, :], lhsT=scb[:, kt, :], rhs=wsb[bb][:, kt, :], start=(kt == 0), stop=(kt == 15)), reads=[ck, 'scb'], writes=[PK[5]])
                        S.op('act', lambda e, b=b: e.copy(out=mrow[b % 2][:], in_=pm[0:2, :]), reads=[PK[5]], writes=[f"mrow{b % 2}"])
                        S.op('act', lambda e, b=b, l=l, j=j: e.dma_start(out=moddram[l, :, j * 512:(j + 1) * 512], in_=mrow[b % 2][:]), reads=[f"mrow{b % 2}"], writes=['moddram'], dma='st_mod')
                macc = sb("macc", [128, 2, 96]); maccx = sb("maccx", [128, 32])
                for l in range(2):
                    col_load(r128(moddram[l, 0]), 96, macc[:, l, :], 'macc', rkey='moddram')
                col_load(r128(moddram[0, 1][0:4096]), 32, maccx[:], 'maccx', rkey='moddram')
                S.op('dve', lambda e: e.tensor_tensor(out=modc[:], in0=macc[:], in1=adb[:], op=ALU.add), reads=['macc', 'adb'], writes=['modc'])
                S.op('dve', lambda e: e.tensor_tensor(out=modx[:], in0=maccx[:], in1=adb[:, 0, 0:32], op=ALU.add), reads=['maccx', 'adb'], writes=['modx'])
                for l in range(2):
                    S.op('dve', lambda e, l=l: e.scalar_tensor_tensor(out=A1[:, l, :], in0=modc[:, l, 16:32], scalar=1.0, in1=ngc[:, l * 2, :], op0=ALU.add, op1=ALU.mult), reads=['modc', 'ngc'], writes=['A1'])
                    S.op('dve', lambda e, l=l: e.scalar_tensor_tensor(out=A2[:, l, :], in0=modc[:, l, 64:80], scalar=1.0, in1=ngc[:, l * 2 + 1, :], op0=ALU.add, op1=ALU.mult), reads=['modc', 'ngc'], writes=['A2'])
                S.op('dve', lambda e: e.scalar_tensor_tensor(out=Ac[:], in0=modx[:, 16:32], scalar=1.0, in1=ngc[:, 0, :], op0=ALU.add, op1=ALU.mult), reads=['modx', 'ngc'], writes=['Ac'])

                emit_stage()

        def norm_pre(es_bufs, src, tok0, blk, use_pow=False):
            xt, st, mvv, rs = es_bufs
            b = blk % 2
            xk = f"xt{b}"
            r0 = tok0 + blk * 128
            S.op('sp', lambda e, b=b, r0=r0: e.dma_start(out=xt[b][:], in_=src[r0:r0 + 128, :]), writes=[xk], dma=xk)
            for q4 in range(4):
                S.op('dve', lambda e, b=b, q4=q4: e.bn_stats(out=st[b][:, q4 * 6:(q4 + 1) * 6], in_=xt[b][:, q4 * 512:(q4 + 1) * 512]), reads=[xk], writes=[f"st{b}"])
            S.op('dve', lambda e, b=b: e.bn_aggr(out=mvv[b][:], in_=st[b][:]), reads=[f"st{b}"], writes=[f"mv{b}"])
            S.op('dve', lambda e, b=b: e.scalar_tensor_tensor(out=rs[b][:, 0:1], in0=mvv[b][:, 0:1], scalar=mvv[b][:, 0:1], in1=mvv[b][:, 1:2], op0=ALU.mult, op1=ALU.add), reads=[f"mv{b}"], writes=[f"rs{b}"])
            S.op('dve', lambda e, b=b: e.tensor_scalar(out=rs[b][:, 0:1], in0=rs[b][:, 0:1], scalar1=1e-6, scalar2=None, op0=ALU.add), reads=[f"rs{b}"], writes=[f"rs{b}"])
            if use_pow:
                S.op('pool', lambda e, b=b: e.tensor_tensor(out=rs[b][:, 2:3], in0=rs[b][:, 0:1], in1=mhalf[:], op=ALU.pow), reads=[f"rs{b}", 'mhalf'], writes=[f"rs{b}"])
            else:
                S.op('act', lambda e, b=b: e.activation(out=rs[b][:, 1:2], in_=rs[b][:, 0:1], func=AF.Sqrt), reads=[f"rs{b}"], writes=[f"rs{b}"])
                S.op('dve', lambda e, b=b: e.reciprocal(out=rs[b][:, 2:3], in_=rs[b][:, 1:2]), reads=[f"rs{b}"], writes=[f"rs{b}"])
            if POOL_SCALE:
                S.op('pool', lambda e, b=b: e.tensor_scalar(out=xt[b][:], in0=xt[b][:], scalar1=rs[b][:, 2:3], scalar2=0.0, op0=ALU.mult, op1=ALU.add), reads=[xk, f"rs{b}"], writes=[xk])
            else:
                S.op('act', lambda e, b=b: e.activation(out=xt[b][:], in_=xt[b][:], func=AF.Copy, scale=rs[b][:, 2:3]), reads=[xk, f"rs{b}"], writes=[xk])

        def norm_post(es_bufs, blk, Acol, Bcol, akey, bkey, hn, hnkey, evac_eng=None):
            xt, st, mvv, rs = es_bufs
            b = blk % 2
            xk = f"xt{b}"
            for g4 in range(4):
                pb = 6 + (g4 % 2)
                for i in range(4):
                    ct = g4 * 4 + i
                    S.op('pe', lambda e, b=b, ct=ct, i=i, pb=pb: e.transpose(PS[pb][:, i * 128:(i + 1) * 128], xt[b][:, ct * 128:(ct + 1) * 128], ident[:]), reads=[xk, 'ident'], writes=[PK[pb]])
                for i in range(4):
                    ct = g4 * 4 + i
                    eng = evac_eng if evac_eng else ('dve' if blk % 2 == 0 else 'act')
                    if eng == 'dve':
                        S.op('dve', lambda e, ct=ct, i=i, pb=pb, blk=blk: e.tensor_scalar(out=hn[:, ct, blk * 128:(blk + 1) * 128], in0=PS[pb][:, i * 128:(i + 1) * 128], scalar1=Acol[:, ct:ct + 1], scalar2=Bcol[:, ct:ct + 1], op0=ALU.mult, op1=ALU.add), reads=[PK[pb], akey, bkey], writes=[hnkey])
                    else:
                        S.op('act', lambda e, ct=ct, i=i, pb=pb, blk=blk: e.activation(out=hn[:, ct, blk * 128:(blk + 1) * 128], in_=PS[pb][:, i * 128:(i + 1) * 128], func=AF.Identity, scale=Acol[:, ct:ct + 1], bias=Bcol[:, ct:ct + 1]), reads=[PK[pb], akey, bkey], writes=[hnkey])

        def norm_block(es_bufs, src, tok0, blk, Acol, Bcol, akey, bkey, hn, hnkey):
            norm_post(es_bufs, blk, Acol, Bcol, akey, bkey, hn, hnkey)
            if blk + 1 < 4:
                norm_pre(es_bufs, src, tok0, blk + 1)

        def norm_tile(es_bufs, src, tok0, Acol, Bcol, akey, bkey, hn, hnkey, nblk=4):
            norm_pre(es_bufs, src, tok0, 0)
            for blk in range(nblk):
                norm_post(es_bufs, blk, Acol, Bcol, akey, bkey, hn, hnkey)
                if blk + 1 < nblk:
                    norm_pre(es_bufs, src, tok0, blk + 1)

        def norm_bufs(es):
            xt = [es.enter_context(nc.sbuf_tensor(uname(f"xt{i}"), [128, 2048], F32)) for i in range(2)]
            st = [es.enter_context(nc.sbuf_tensor(uname(f"st{i}"), [128, 24], F32)) for i in range(2)]
            mvv = [es.enter_context(nc.sbuf_tensor(uname(f"mv{i}"), [128, 2], F32)) for i in range(2)]
            rs = [es.enter_context(nc.sbuf_tensor(uname(f"rs{i}"), [128, 4], F32)) for i in range(2)]
            return xt, st, mvv, rs

        def out_blocks(i):
            res = []
            o0 = 512 * i - LAG
            for blk in range(4):
                t0 = o0 + blk * 128
                lo = max(t0, 0); hi = min(t0 + 128, TE)
                if hi > lo:
                    res.append((blk, t0, lo - t0, hi - t0))
            return res

        grow_tmp = [None]

        def rep_row(es, colap, ckey, dst, dkey):
            dgl = [es.enter_context(nc.sbuf_tensor(uname(f"dgl{i}"), [128, 128], F32)) for i in range(2)]
            grow_tmp[0] = [es.enter_context(nc.sbuf_tensor(uname(f"gtmp{i}"), [128, 512], F32)) for i in range(2)]
            for g4 in range(4):
                for i in range(4):
                    ct = g4 * 4 + i
                    b = i % 2
                    S.op('dve', lambda e, ct=ct, b=b: e.tensor_scalar(out=dgl[b][:], in0=ident[:], scalar1=colap[:, ct:ct + 1], scalar2=None, op0=ALU.mult), reads=['ident', ckey], writes=[f"dgl{b}"])
                    S.op('pe', lambda e, b=b, i=i: e.matmul(PS[6][:, i * 128:(i + 1) * 128], lhsT=onesf[:], rhs=dgl[b][:], start=True, stop=True, skip_group_check=True), reads=['onesf', f"dgl{b}"], writes=[PK[6]])
                S.op('act', lambda e, g4=g4: e.copy(out=dst[:, g4 * 512:(g4 + 1) * 512], in_=PS[6][:]), reads=[PK[6]], writes=[dkey])

        wo_ctr = [0]

        def outproj_resid(es, mlhs, mkey, nkt, Wb, wtag, hin, hout, hokey, i, wo, hres, lagged=True, between=None, grow=None):
            blocks = out_blocks(i) if lagged else [(b, 512 * i + b * 128, 0, 128) for b in range(4)]
            for (blk, t0, lo, hi) in blocks:
                S.op('sp', lambda e, blk=blk, t0=t0, lo=lo, hi=hi: e.dma_start(out=hres[blk][lo:hi, :], in_=hin[t0 + lo:t0 + hi, :]), writes=[f"hres{blk}"], dma=f"hres{blk}")
            nch = nkt // 8
            for cg in range(4):
                bufs = []
                for ch in range(nch):
                    b = wo_ctr[0] % 4; wo_ctr[0] += 1
                    bufs.append(b)
                    S.op('sp', lambda e, b=b, cg=cg, ch=ch: e.dma_start(out=wo[b][:], in_=Wb[ch * 8:ch * 8 + 8, :, cg * 512:(cg + 1) * 512].rearrange("k p c -> p k c")), reads=[wtag], writes=[f"wo{b}"], dma=f"wo{b}")
                for (blk, t0, lo, hi) in blocks:
                    pb = blk
                    for kt in range(nkt):
                        b = bufs[kt // 8]
                        S.op('pe', lambda e, kt=kt, blk=blk, b=b, pb=pb: e.matmul(PS[pb][:], lhsT=mlhs(kt, blk), rhs=wo[b][:, kt % 8, :], start=(kt == 0), stop=(kt == nkt - 1)), reads=[mkey, f"wo{b}"], writes=[PK[pb]])
                    tb = blk % 2
                    S.op('dve', lambda e, cg=cg, pb=pb, tb=tb: e.tensor_tensor(out=grow_tmp[0][tb][:], in0=PS[pb][:], in1=grow[:, cg * 512:(cg + 1) * 512], op=ALU.mult), reads=[PK[pb], 'grow'], writes=[f"gtmp{tb}"])
                    S.op('pool', lambda e, blk=blk, cg=cg, tb=tb: e.tensor_tensor(out=hres[blk][:, cg * 512:(cg + 1) * 512], in0=grow_tmp[0][tb][:], in1=hres[blk][:, cg * 512:(cg + 1) * 512], op=ALU.add), reads=[f"gtmp{tb}", f"hres{blk}"], writes=[f"hres{blk}"])
                if between is not None:
                    between(cg)
            for (blk, t0, lo, hi) in blocks:
                S.op('act', lambda e, blk=blk, t0=t0, lo=lo, hi=hi: e.dma_start(out=hout[t0 + lo:t0 + hi, :], in_=hres[blk][lo:hi, :]), reads=[f"hres{blk}"], writes=[hokey], dma=f"st_{hokey}")

        def ffn_stage(l, hin, hinkey, hout, hokey):
            with ExitStack() as es:
                def sb(name, shape, dt=F32):
                    return es.enter_context(nc.sbuf_tensor(uname(name), list(shape), dt))
                nb = norm_bufs(es)
                hn1 = sb("hn0", [128, 16, 512], BF16)
                hn = [hn1, hn1]
                win = [sb(f"win{i}", [128, 16, 128], BF16) for i in range(5)]
                zt = [sb(f"zt{i}", [128, 10, 66], BF16) for i in range(4)]
                zh = sb("zh", [128, 64, 2, 66], BF16)
                dgf = [sb(f"dgf{i}", [128, 9, 128], BF16) for i in range(4)]
                sg = [sb(f"sg{i}", [128, 512]) for i in range(2)]
                m = sb("m", [128, 32, 512], BF16)
                wo = [sb(f"wo{b}", [128, 8, 512], BF16) for b in range(4)]
                hres = [sb(f"hres{b}", [128, 2048]) for b in range(4)]
                grow = sb("grow", [128, 2048])
                rep_row(es, modc[:, l, 80:96], 'modc', grow, 'grow')
                for i in range(4):
                    S.op('pool', lambda e, i=i: e.memset(zt[i][:], 0.0), writes=[f"zt{i}"])
                S.op('pool', lambda e: e.memset(zh[:], 0.0), writes=['zh'])
                nwin = [0]
                NA, NBk = A2[:, l, :], modc[:, l, 48:64]
                norm_tile(nb, hin, 0, NA, NBk, 'A2', 'modc', hn[0], "hn0")

                def inproj(i, cp):
                    has_in = i < NT
                    for half in range(2):
                        c = cp + 32 * half
                        b = (2 * cp + half) % 4
                        zk = f"zt{b}"
                        S.op('pool', lambda e, b=b, c=c: e.tensor_copy(out=zt[b][:, 0:2, :], in_=zh[:, c, :, :]), reads=['zh'], writes=[zk])
                        if has_in:
                            wb_ = nwin[0] % 5; nwin[0] += 1
                            wk = f"win{wb_}"
                            S.op('sp', lambda e, c=c, wb_=wb_: e.dma_start(out=win[wb_][:], in_=Wb_fin[l][c]), reads=[f'Wb_fin{l}'], writes=[wk], dma=wk)
                            pb = 2 * (cp % 2) + half
                            for kt in range(16):
                                S.op('pe', lambda e, kt=kt, wb_=wb_, pb=pb: e.matmul(PS[pb][:], lhsT=win[wb_][:, kt, :], rhs=hn[0][:, kt, :], start=(kt == 0), stop=(kt == 15)), reads=[wk, "hn0"], writes=[PK[pb]])
                            S.op('act', lambda e, b=b, pb=pb: e.copy(out=zt[b][:, 2:10, 1:65], in_=PS[pb][:].rearrange("p (r c) -> p r c", c=64)), reads=[PK[pb]], writes=[zk])
                        else:
                            S.op('pool', lambda e, b=b: e.memset(zt[b][:, 2:10, :], 0.0), writes=[zk])
                        S.op('pool', lambda e, b=b, c=c: e.tensor_copy(out=zh[:, c, :, :], in_=zt[b][:, 8:10, :]), reads=[zk], writes=['zh'])
                        for t in range(9):
                            S.op('dve', lambda e, b=b, c=c, t=t: e.tensor_scalar(out=dgf[b][:, t, :], in0=identb[:], scalar1=fcw[:, l, t, c:c + 1], scalar2=None, op0=ALU.mult), reads=['identb', 'fcw'], writes=[f"dgf{b}"])

                def conv(i, cp):
                    for half in range(2):
                        b = (2 * cp + half) % 4
                        zk = f"zt{b}"
                        pc = 4 + half
                        for t in range(9):
                            dr, dc = t // 3, t % 3
                            S.op('pe', lambda e, b=b, t=t, dr=dr, dc=dc, pc=pc: e.matmul(PS[pc][:], lhsT=dgf[b][:, t, :], rhs=zt[b][:, dr:dr + 8, dc:dc + 64], start=(t == 0), stop=(t == 8)), reads=[f"dgf{b}", zk], writes=[PK[pc]])
                    sb_ = cp % 2
                    S.op('act', lambda e, cp=cp, sb_=sb_: e.activation(out=sg[sb_][:], in_=PS[5][:], func=AF.Silu, bias=fcb[:, l, 32 + cp:33 + cp]), reads=[PK[5], 'fcb'], writes=[f"sg{sb_}"])
                    S.op('dve', lambda e, cp=cp, sb_=sb_: e.scalar_tensor_tensor(out=m[:, cp, :], in0=PS[4][:], scalar=fcb[:, l, cp:cp + 1], in1=sg[sb_][:], op0=ALU.add, op1=ALU.mult), reads=[PK[4], 'fcb', f"sg{sb_}"], writes=['m'])

                for i in range(NT + 1):
                    inproj(i, 0)
                    for cp in range(32):
                        if cp + 1 < 32:
                            inproj(i, cp + 1)
                        conv(i, cp)
                        if l == 0 and cp % 3 == 2:
                            emit_casts(castC, 1)
                    btw = None
                    if i + 1 < NT:
                        norm_pre(nb, hin, 512 * (i + 1), 0)
                        btw = lambda cg, i=i: norm_block(nb, hin, 512 * (i + 1), cg, NA, NBk, 'A2', 'modc', hn[0], "hn0")
                    outproj_resid(es, lambda kt, blk: m[:, kt, blk * 128:(blk + 1) * 128], 'm', 32, Wb_fout[l], f'Wb_fout{l}', hin, hout, hokey, i, wo, hres, between=btw, grow=grow)
                if l == 0:
                    emit_casts(castC, 10 ** 6)
                emit_stage()

        def l1mix_stage(hin, hout, hokey):
            with ExitStack() as es:
                def sb(name, shape, dt=F32):
                    return es.enter_context(nc.sbuf_tensor(uname(name), list(shape), dt))
                nb = norm_bufs(es)
                hn1 = sb("hn0", [128, 16, 512], BF16)
                hn = [hn1, hn1]
                wu = [sb(f"win{i}", [128, 16, 128], BF16) for i in range(3)]
                wo = [sb(f"wo{b}", [128, 8, 512], BF16) for b in range(4)]
                uv = sb("uv", [128, 16, 512])
                vn = sb("vn", [128, 4, 2048], BF16)
                m = sb("m", [128, 16, 512], BF16)
                tmp = [sb(f"tmp{i}", [128, 128]) for i in range(2)]
                st = sb("lst", [128, 4, 24]); mv = sb("lmv", [128, 4, 2]); rsd = sb("lrs", [128, 4, 2])
                hres = [sb(f"hres{b}", [128, 2048]) for b in range(4)]
                uvv = uv[:].rearrange("p (cb four) c -> p cb (four c)", four=4)
                grow = sb("grow", [128, 2048])
                rep_row(es, modc[:, 1, 32:48], 'modc', grow, 'grow')
                sguT = sb("sguT", [128, 8, 128], BF16)
                Cst = sb("Cst", [128, 16, 128])
                stg = sb("stg", [128, 128])
                pcol = PS[7]
                for h in range(8):
                    S.op('sp', lambda e, h=h: e.dma_start(out=stg[:], in_=P['o_sgu_w'][0, h]), writes=['stg'], dma='stg')
                    S.op('pe', lambda e: e.transpose(pcol[:, 0:128], stg[:], ident[:]), reads=['stg', 'ident'], writes=[PK[7]])
                    S.op('dve', lambda e, h=h: e.tensor_copy(out=sguT[:, h, :], in_=pcol[:, 0:128]), reads=[PK[7]], writes=['sguT'])
                rw = uv[:, 0:2, :].rearrange("p a c -> p (a c)").rearrange("p (h q) -> p h q", q=128)
                for h in range(8):
                    S.op('pe', lambda e, h=h: e.matmul(PS[6][:, 0:128], lhsT=onesb[:], rhs=sguT[:, h, :], start=True, stop=True), reads=['onesb', 'sguT'], writes=[PK[6]])
                    S.op('dve', lambda e, h=h: e.tensor_copy(out=rw[:, h, :], in_=PS[6][:, 0:128]), reads=[PK[6]], writes=['uv0'])
                sgb = uv[:, 2:4, :].rearrange("p a c -> p (a c)").rearrange("p (h q) -> p h q", q=128)
                S.op('sp', lambda e: e.dma_start(out=uv[:, 2:4, :].rearrange("p a c -> p (a c)"), in_=P['o_sgu_b'][0].rearrange("h q -> (h q)").partition_broadcast(128)), writes=['uv0'], dma='sgb')
                for ct in range(16):
                    h = ct // 2
                    S.op('dve', lambda e, ct=ct, h=h: e.scalar_tensor_tensor(out=Cst[:, ct, :], in0=rw[:, h, :], scalar=olb[:, ct:ct + 1], in1=sgb[:, h, :], op0=ALU.mult, op1=ALU.add), reads=['uv0', 'olb'], writes=['Cst'])

                nwu = [0]
                norm_tile(nb, hin, 0, A1[:, 1, :], modc[:, 1, 0:16], 'A1', 'modc', hn[0], "hn0")
                for i in range(NT):
                    hb = 0
                    for cg in range(4):
                        bufs = []
                        for ch in range(2):
                            b = wo_ctr[0] % 4; wo_ctr[0] += 1
                            bufs.append(b)
                            S.op('sp', lambda e, b=b, cg=cg, ch=ch: e.dma_start(out=wo[b][:], in_=Wb_oinv[cg][:, ch * 8:ch * 8 + 8, :]), reads=['Wb_oinv'], writes=[f"wo{b}"], dma=f"wo{b}")
                        for cb in range(4):
                            pb = cb
                            for kt in range(16):
                                b = bufs[kt // 8]
                                S.op('pe', lambda e, kt=kt, cb=cb, b=b, pb=pb: e.matmul(PS[pb][:], lhsT=hn[hb][:, kt, cb * 128:(cb + 1) * 128], rhs=wo[b][:, kt % 8, :], start=(kt == 0), stop=(kt == 15)), reads=[f"hn{hb}", f"wo{b}"], writes=[PK[pb]])
                            S.op('act', lambda e, cb=cb, cg=cg, pb=pb: e.activation(out=uvv[:, cb, cg * 512:(cg + 1) * 512], in_=PS[pb][:], func=AF.Gelu_apprx_tanh), reads=[PK[pb]], writes=[f"uv{cb}"])
                    for cb in range(4):
                        for q4 in range(4):
                            S.op('dve', lambda e, cb=cb, q4=q4: e.bn_stats(out=st[:, cb, q4 * 6:(q4 + 1) * 6], in_=uvv[:, cb, q4 * 512:(q4 + 1) * 512]), reads=[f"uv{cb}"], writes=['lst'])
                        S.op('dve', lambda e, cb=cb: e.bn_aggr(out=mv[:, cb, :], in_=st[:, cb, :]), reads=['lst'], writes=['lmv'])
                        S.op('dve', lambda e, cb=cb: e.tensor_scalar(out=rsd[:, cb, 0:1], in0=mv[:, cb, 1:2], scalar1=1e-5, scalar2=None, op0=ALU.add), reads=['lmv'], writes=['lrs'])
                        S.op('act', lambda e, cb=cb: e.activation(out=rsd[:, cb, 1:2], in_=rsd[:, cb, 0:1], func=AF.Sqrt), reads=['lrs'], writes=['lrs'])
                        S.op('dve', lambda e, cb=cb: e.reciprocal(out=rsd[:, cb, 0:1], in_=rsd[:, cb, 1:2]), reads=['lrs'], writes=['lrs'])
                        S.op('dve', lambda e, cb=cb: e.tensor_scalar(out=vn[:, cb, :], in0=uvv[:, cb, :], scalar1=mv[:, cb, 0:1], scalar2=rsd[:, cb, 0:1], op0=ALU.subtract, op1=ALU.mult), reads=[f"uv{cb}", 'lmv', 'lrs'], writes=['vn'])
                    def sgu(ct):
                        h = ct // 2
                        for cb in range(4):
                            pb = 4 + (cb % 2)
                            tb = cb % 2
                            S.op('pe', lambda e, cb=cb, ct=ct, h=h, pb=pb: e.matmul(PS[pb][:, 0:128], lhsT=vn[:, cb, ct * 128:(ct + 1) * 128], rhs=sguT[:, h, :], start=True, stop=True), reads=['vn', 'sguT'], writes=[PK[pb]])
                            S.op('dve', lambda e, ct=ct, pb=pb, tb=tb: e.scalar_tensor_tensor(out=tmp[tb][:], in0=PS[pb][:, 0:128], scalar=olg[:, ct:ct + 1], in1=Cst[:, ct, :], op0=ALU.mult, op1=ALU.add), reads=[PK[pb], 'olg', 'Cst'], writes=[f"tmp{tb}"])
                            S.op('dve', lambda e, cb=cb, ct=ct, tb=tb: e.tensor_tensor(out=m[:, ct, cb * 128:(cb + 1) * 128], in0=tmp[tb][:], in1=uv[:, ct, cb * 128:(cb + 1) * 128], op=ALU.mult), reads=[f"tmp{tb}", f"uv{ct // 4}"], writes=['m'])
                    for ot in range(16):
                        wb_ = nwu[0] % 3; nwu[0] += 1
                        wk = f"win{wb_}"
                        S.op('sp', lambda e, ot=ot, wb_=wb_: e.dma_start(out=wu[wb_][:], in_=Wb_oinu[ot]), reads=['Wb_oinu'], writes=[wk], dma=wk)
                        pb = ot % 4
                        for kt in range(16):
                            S.op('pe', lambda e, kt=kt, wb_=wb_, pb=pb: e.matmul(PS[pb][:], lhsT=wu[wb_][:, kt, :], rhs=hn[hb][:, kt, :], start=(kt == 0), stop=(kt == 15)), reads=[wk, f"hn{hb}"], writes=[PK[pb]])
                        S.op('act', lambda e, ot=ot, pb=pb: e.activation(out=uv[:, ot, :], in_=PS[pb][:], func=AF.Gelu_apprx_tanh), reads=[PK[pb]], writes=[f"uv{ot // 4}"])
                        if ot >= 1:
                            sgu(ot - 1)
                    sgu(15)
                    btw = None
                    if i + 1 < NT:
                        norm_pre(nb, hin, 512 * (i + 1), 0)
                        btw = lambda cg, i=i: norm_block(nb, hin, 512 * (i + 1), cg, A1[:, 1, :], modc[:, 1, 0:16], 'A1', 'modc', hn[0], "hn0")
                    outproj_resid(es, lambda kt, blk: m[:, kt, blk * 128:(blk + 1) * 128], 'm', 16, Wb_oout, 'Wb_oout', hin, hout, hokey, i, wo, hres, lagged=False, between=btw, grow=grow)
                emit_stage()

        def final_stage(hin, houtap):
            with ExitStack() as es:
                xt, st, mvv, rs = norm_bufs(es)
                fgrep = es.enter_context(nc.sbuf_tensor(uname("fgrep"), [128, 2048], F32))
                rep_row(es, fgc[:], 'fgc', fgrep, 'fgrep')
                for blk in range(TE // 128):
                    b = blk % 2
                    xk = f"xt{b}"
                    r0 = blk * 128
                    S.op('sp', lambda e, b=b, r0=r0: e.dma_start(out=xt[b][:], in_=hin[r0:r0 + 128, :]), writes=[xk], dma=xk)
                    for q4 in range(4):
                        S.op('dve', lambda e, b=b, q4=q4: e.bn_stats(out=st[b][:, q4 * 6:(q4 + 1) * 6], in_=xt[b][:, q4 * 512:(q4 + 1) * 512]), reads=[xk], writes=[f"st{b}"])
                    S.op('dve', lambda e, b=b: e.bn_aggr(out=mvv[b][:], in_=st[b][:]), reads=[f"st{b}"], writes=[f"mv{b}"])
                    S.op('dve', lambda e, b=b: e.scalar_tensor_tensor(out=rs[b][:, 0:1], in0=mvv[b][:, 0:1], scalar=mvv[b][:, 0:1], in1=mvv[b][:, 1:2], op0=ALU.mult, op1=ALU.add), reads=[f"mv{b}"], writes=[f"rs{b}"])
                    S.op('dve', lambda e, b=b: e.tensor_scalar(out=rs[b][:, 0:1], in0=rs[b][:, 0:1], scalar1=1e-6, scalar2=None, op0=ALU.add), reads=[f"rs{b}"], writes=[f"rs{b}"])
                    S.op('act', lambda e, b=b: e.activation(out=rs[b][:, 1:2], in_=rs[b][:, 0:1], func=AF.Sqrt), reads=[f"rs{b}"], writes=[f"rs{b}"])
                    S.op('dve', lambda e, b=b: e.reciprocal(out=rs[b][:, 2:3], in_=rs[b][:, 1:2]), reads=[f"rs{b}"], writes=[f"rs{b}"])
                    S.op('dve', lambda e, b=b: e.scalar_tensor_tensor(out=xt[b][:], in0=xt[b][:], scalar=rs[b][:, 2:3], in1=fgrep[:], op0=ALU.mult, op1=ALU.mult), reads=[xk, f"rs{b}", 'fgrep'], writes=[xk])
                    S.op('act', lambda e, b=b, r0=r0: e.dma_start(out=houtap[r0:r0 + 128, :], in_=xt[b][:]), reads=[xk], writes=['out'], dma='st_out')
                emit_stage()


        def l0mix_stage(hin, hout, hokey, use_s5=True):
            with ExitStack() as es:
                def sb(name, shape, dt=F32):
                    return es.enter_context(nc.sbuf_tensor(uname(name), list(shape), dt))
                nb = norm_bufs(es)
                hn1 = sb("hn0", [128, 16, 512], BF16)
                win = [sb(f"win{i}", [128, 16, 128], BF16) for i in range(5)]
                zc = [sb(f"zc{i}", [128, 608], BF16) for i in range(12)]
                dgc = [sb(f"dgc{i}", [128, 31, 128], BF16) for i in range(2)]
                sgm = [sb(f"sgm{i}", [128, 512]) for i in range(2)]
                ysq = [sb(f"ysq{i}", [128, 512]) for i in range(2)]
                mean_t = sb("mean_t", [128, 512]); rstd_t = sb("rstd_t", [128, 512]); var_t = sb("var_t", [128, 512])
                sg4 = [sgm[0], sgm[1], ysq[0], ysq[1]]
                sg4k = ["sgm0", "sgm1", "ysq0", "ysq1"]
                s5_late = []
                ymix = sb("ymix", [128, 16, 512], BF16)
                ygb = sb("ygb", [128, 4, 512], BF16)
                wo = [sb(f"wo{b}", [128, 8, 512], BF16) for b in range(4)]
                hres = [sb(f"hres{b}", [128, 2048]) for b in range(4)]
                ycv = lambda c: hres[c // 4][:, (c % 4) * 512:(c % 4 + 1) * 512]
                ycvk = lambda c: f"hres{c // 4}"
                yraw = hres[3][:].rearrange("p (g c) -> p g c", c=512)
                grow = sb("grow", [128, 2048])
                rep_row(es, modc[:, 0, 32:48], 'modc', grow, 'grow')
                glw = sb("glw", [128, 4, 512], BF16)
                S.op('pool', lambda e: e.dma_start(out=glw[:], in_=P['s5_glu_w'][0].rearrange("(kt p) c -> p kt c", p=128)), writes=['glw'], dma='glw')
                for c in range(12):
                    S.op('pool', lambda e, c=c: e.memset(zc[c][:], 0.0), writes=[f"zc{c}"])
                if not use_s5:
                    S.op('pool', lambda e: e.memset(ymix[:, 12:16, :], 0.0), writes=['ymix'])
                nwin = [0]
                norm_tile(nb, hin, 0, A1[:, 0, :], modc[:, 0, 0:16], 'A1', 'modc', hn1, "hn0")
                def l0_inproj(i, cp):
                    has_in = i < NT
                    zk = f"zc{cp}"
                    if has_in:
                        for half in range(2):
                            c = cp + 12 * half
                            wb_ = nwin[0] % 5; nwin[0] += 1
                            wk = f"win{wb_}"
                            pb = 2 * (cp % 2) + half
                            S.op('sp', lambda e, c=c, wb_=wb_: e.dma_start(out=win[wb_][:], in_=Wb_ein[c]), reads=['Wb_ein'], writes=[wk], dma=wk)
                            for kt in range(16):
                                S.op('pe', lambda e, kt=kt, wb_=wb_, pb=pb: e.matmul(PS[pb][:], lhsT=win[wb_][:, kt, :], rhs=hn1[:, kt, :], start=(kt == 0), stop=(kt == 15)), reads=[wk, "hn0"], writes=[PK[pb]])
                        sb_ = cp % 2
                        pa, pg = 2 * (cp % 2), 2 * (cp % 2) + 1
                        S.op('act', lambda e, sb_=sb_, pg=pg: e.activation(out=sgm[sb_][:], in_=PS[pg][:], func=AF.Sigmoid), reads=[PK[pg]], writes=[f"sgm{sb_}"])
                        S.op('dve', lambda e, cp=cp, sb_=sb_, pa=pa: e.tensor_tensor(out=zc[cp][:, 96:608], in0=PS[pa][:], in1=sgm[sb_][:], op=ALU.mult), reads=[PK[pa], f"sgm{sb_}"], writes=[zk])
                    else:
                        S.op('pool', lambda e, cp=cp: e.memset(zc[cp][:, 96:608], 0.0), writes=[zk])
                    db = cp % 2
                    for k in range(31):
                        eng = 'dve' if k % 2 == 0 else 'pool'
                        S.op(eng, lambda e, db=db, k=k, cp=cp: e.tensor_scalar(out=dgc[db][:, k, :], in0=identb[:], scalar1=cwc[:, k, cp:cp + 1], scalar2=0.0, op0=ALU.mult, op1=ALU.add), reads=['identb', 'cwc'], writes=[f"dgc{db}_{k % 2}"])

                def l0_conv(i, cp):
                    zk = f"zc{cp}"
                    db = cp % 2
                    pc = 4 + (cp % 2)
                    for k in range(31):
                        S.op('pe', lambda e, db=db, k=k, cp=cp, pc=pc: e.matmul(PS[pc][:], lhsT=dgc[db][:, k, :], rhs=zc[cp][:, 17 + k:17 + k + 512], start=(k == 0), stop=(k == 30)), reads=[f"dgc{db}_0", f"dgc{db}_1", zk], writes=[PK[pc]])
                    S.op('pool', lambda e, cp=cp: e.tensor_copy(out=zc[cp][:, 0:96], in_=zc[cp][:, 512:608]), reads=[zk], writes=[zk])
                    qb = cp % 2
                    S.op('act', lambda e, cp=cp, pc=pc: e.activation(out=ycv(cp), in_=PS[pc][:], func=AF.Identity, bias=cbc[:, cp:cp + 1]), reads=[PK[pc], 'cbc'], writes=[ycvk(cp)])
                    S.op('act', lambda e, cp=cp, pc=pc, qb=qb: e.activation(out=ysq[qb][:], in_=PS[pc][:], func=AF.Square, bias=cbc[:, cp:cp + 1]), reads=[PK[pc], 'cbc'], writes=[f"ysq{qb}"])
                    S.op('pe', lambda e, cp=cp: e.matmul(PS[6][:], lhsT=onesf[:], rhs=ycv(cp), start=(cp == 0), stop=(cp == 11)), reads=['onesf', ycvk(cp)], writes=[PK[6]])
                    S.op('pe', lambda e, cp=cp, qb=qb: e.matmul(PS[7][:], lhsT=onesf[:], rhs=ysq[qb][:], start=(cp == 0), stop=(cp == 11)), reads=['onesf', f"ysq{qb}"], writes=[PK[7]])

                for i in range(NT + 1):
                    has_in = i < NT
                    l0_inproj(i, 0)
                    for cp in range(12):
                        if cp + 1 < 12:
                            l0_inproj(i, cp + 1)
                        l0_conv(i, cp)
                        if cp % 3 == 2:
                            emit_casts(castB, 1)
                    if use_s5:
                        for gt in range(4):
                            S.op('sp', lambda e, gt=gt, i=i: e.dma_start(out=yraw[:, gt, :].rearrange("p (t j) -> p t j", j=64), in_=Ys[gt * 128:(gt + 1) * 128, :, 64 * i:64 * i + 64]), reads=['Ys'], writes=['hres3'], dma='hres3')
                        S.op('act', lambda e: e.activation(out=hres[3][:], in_=hres[3][:], func=AF.Gelu_apprx_tanh), reads=['hres3'], writes=['hres3'])
                        yrv = lambda gt: yraw[:, gt, :].rearrange("p (t j) -> p j t", j=64)
                        for gt in range(4):
                            S.op('pool', lambda e, gt=gt: e.tensor_copy(out=ygb[:, gt, :].rearrange("p (j t) -> p j t", t=8), in_=yrv(gt)), reads=['hres3'], writes=['ygb'])
                        for ot in range(4):
                            pc = 2 + (ot % 2)
                            for kt in range(4):
                                S.op('pe', lambda e, ot=ot, kt=kt, pc=pc: e.matmul(PS[pc][:], lhsT=glw[:, kt, ot * 128:(ot + 1) * 128], rhs=ygb[:, kt, :], start=(kt == 0), stop=(kt == 3)), reads=['glw', 'ygb'], writes=[PK[pc]])
                            sb_ = ot % 2
                            S.op('act', lambda e, ot=ot, pc=pc, sb_=sb_: e.activation(out=sg4[ot][:], in_=PS[pc][:], func=AF.Sigmoid, bias=glub[:, ot:ot + 1]), reads=[PK[pc], 'glub'], writes=[sg4k[ot]])
                            s5_late.append(((lambda e, ot=ot, sb_=sb_: e.tensor_tensor(out=ymix[:, 12 + ot, :].rearrange("p (j t) -> p j t", t=8), in0=yrv(ot), in1=sg4[ot][:].rearrange("p (j t) -> p j t", t=8), op=ALU.mult)), ['hres3', sg4k[ot]], ['ymix']))
                    S.op('dve', lambda e: e.tensor_scalar(out=mean_t[:], in0=PS[6][:], scalar1=1.0 / 1536, scalar2=None, op0=ALU.mult), reads=[PK[6]], writes=['mean_t'])
                    S.op('dve', lambda e: e.tensor_tensor(out=var_t[:], in0=mean_t[:], in1=mean_t[:], op=ALU.mult), reads=['mean_t'], writes=['var_t'])
                    S.op('dve', lambda e: e.scalar_tensor_tensor(out=var_t[:], in0=PS[7][:], scalar=1.0 / 1536, in1=var_t[:], op0=ALU.mult, op1=ALU.subtract), reads=[PK[7], 'var_t'], writes=['var_t'])
                    S.op('dve', lambda e: e.tensor_scalar(out=var_t[:], in0=var_t[:], scalar1=1e-5, scalar2=None, op0=ALU.add), reads=['var_t'], writes=['var_t'])
                    S.op('act', lambda e: e.activation(out=var_t[:], in_=var_t[:], func=AF.Sqrt), reads=['var_t'], writes=['var_t'])
                    S.op('dve', lambda e: e.reciprocal(out=rstd_t[:], in_=var_t[:]), reads=['var_t'], writes=['rstd_t'])
                    for cp in range(12):
                        S.op('pool' if cp % 2 == 1 else 'dve', lambda e, cp=cp: e.tensor_tensor(out=ycv(cp), in0=ycv(cp), in1=mean_t[:], op=ALU.subtract), reads=[ycvk(cp), 'mean_t'], writes=[ycvk(cp)])
                        S.op('dve', lambda e, cp=cp: e.tensor_tensor(out=ycv(cp), in0=ycv(cp), in1=rstd_t[:], op=ALU.mult), reads=[ycvk(cp), 'rstd_t'], writes=[ycvk(cp)])
                        S.op('act', lambda e, cp=cp: e.activation(out=ymix[:, cp, :], in_=ycv(cp), func=AF.Silu, scale=lgc[:, cp:cp + 1], bias=lbc[:, cp:cp + 1]), reads=[ycvk(cp), 'lgc', 'lbc'], writes=['ymix'])
                    while s5_late:
                        fn_, rk_, wk_ = s5_late.pop(0)
                        S.op('pool', fn_, reads=rk_, writes=wk_)
                    btw = None
                    if i + 1 < NT:
                        norm_pre(nb, hin, 512 * (i + 1), 0)
                        btw = lambda cg, i=i: norm_block(nb, hin, 512 * (i + 1), cg, A1[:, 0, :], modc[:, 0, 0:16], 'A1', 'modc', hn1, "hn0")
                    outproj_resid(es, lambda kt, blk: ymix[:, kt, blk * 128:(blk + 1) * 128], 'ymix', 16, Wb_eout, 'Wb_eout', hin, hout, hokey, i, wo, hres, between=btw, grow=grow)
                emit_casts(castB, 10 ** 6)
                emit_stage()


        def s5b_stage():
            with ExitStack() as es:
                def sb(name, shape, dt=F32):
                    return es.enter_context(nc.sbuf_tensor(uname(name), list(shape), dt))
                TWO_PI = 6.283185307179586
                MAGIC = 12582912.0
                es_t = ExitStack()

                def sbt(name, shape, dt=F32):
                    return es_t.enter_context(nc.sbuf_tensor(uname(name), list(shape), dt))
                PFa = sb("PFa", [128, 64, 33]); PFb = sb("PFb", [128, 64, 33])
                PRa = sb("PRa", [128, 64, 33]); PRb = sb("PRb", [128, 64, 33])
                PNa = sb("PNa", [128, 64, 8]); PNb = sb("PNb", [128, 64, 8])
                PNRa = sb("PNRa", [128, 64, 8]); PNRb = sb("PNRb", [128, 64, 8])
                Qa = sb("Qa", [128, 64, 9]); Qb = sb("Qb", [128, 64, 9])
                X1b = sb("X1b", [128, 64, 16]); X2b = sb("X2b", [128, 64, 16])
                X1c = sb("X1c", [128, 64, 16]); X2c = sb("X2c", [128, 64, 16])
                Dcol = sb("Dcol", [128, 32])
                Jm = sb("Jm", [128, 128]); mkf = sb("mkf", [128, 128]); mkb = sb("mkb", [128, 128])
                ohr = [sb(f"ohr{d}", [128, 384]) for d in range(2)]
                stg = sbt("stg", [128, 128])
                are2 = sbt("are2", [128, 64]); aim2 = sbt("aim2", [128, 64]); dtr = sbt("dtr", [128, 64])
                la = sbt("la", [128, 64]); lb = sbt("lb", [128, 64])
                tA = sbt("tA", [128, 64]); tB = sbt("tB", [128, 64]); tC = sbt("tC", [128, 64]); tD = sbt("tD", [128, 64])
                fre = sbt("fre", [128, 64]); fim = sbt("fim", [128, 64])
                pt = [sbt(f"pt{i}", [128, 64, 16]) for i in range(4)]
                Bre = sbt("Bre", [128, 64, 16]); Bim = sbt("Bim", [128, 64, 16])
                Cre = sbt("Cre", [128, 64, 16]); Cim = sbt("Cim", [128, 64, 16])
                K1 = ['s5k']

                def T(eng, fn, reads=('s5k',), writes=('s5k',)):
                    S.op(eng, fn, reads=list(reads), writes=list(writes))

                S.defer_begin()
                T('pool', lambda e: e.memset(Jm[:], 1.0))
                T('pool', lambda e: e.affine_select(out=Jm[:], in_=Jm[:], pattern=[[1, 128]], compare_op=ALU.is_equal, fill=0.0, base=-64, channel_multiplier=-1))
                T('pool', lambda e: e.memset(mkf[:], -1.0))
                T('pool', lambda e: e.affine_select(out=mkf[:], in_=mkf[:], pattern=[[-1, 128]], compare_op=ALU.is_equal, fill=0.0, base=-64, channel_multiplier=1))
                T('pool', lambda e: e.tensor_tensor(out=Jm[:], in0=Jm[:], in1=mkf[:], op=ALU.add))
                T('pool', lambda e: e.memset(mkf[:], 1.0))
                T('pool', lambda e: e.affine_select(out=mkf[:].rearrange("p (t q) -> p t q", q=16), in_=mkf[:].rearrange("p (t q) -> p t q", q=16), pattern=[[16, 8], [0, 16]], compare_op=ALU.is_ge, fill=0.0, base=15, channel_multiplier=-1))
                T('pool', lambda e: e.memset(mkb[:], 1.0))
                T('pool', lambda e: e.affine_select(out=mkb[:].rearrange("p (t q) -> p t q", q=16), in_=mkb[:].rearrange("p (t q) -> p t q", q=16), pattern=[[-16, 8], [0, 16]], compare_op=ALU.is_ge, fill=0.0, base=0, channel_multiplier=1))
                S.op('sp', lambda e: e.dma_start(out=ohr[0][:], in_=ohf[0].partition_broadcast(128)), writes=['ohr'], dma='ohr')
                S.op('sp', lambda e: e.dma_start(out=ohr[1][:], in_=ohb[0].partition_broadcast(128)), writes=['ohr'], dma='ohr')
                for s_ in range(8):
                    S.op('sp', lambda e, s_=s_: e.dma_start(out=Dcol[s_ * 16:(s_ + 1) * 16, :], in_=P['s5_d'][0].rearrange("(g q) -> q g", q=16), allow_slow_non_contiguous=True), writes=['s5k'], dma='s5p')

                def load_dup(src2d, dst):
                    S.op('sp', lambda e: e.dma_start(out=stg[0:64, 0:64], in_=src2d), writes=['stg'], dma='stg')
                    S.op('sp', lambda e: e.dma_start(out=stg[0:64, 64:128], in_=src2d), writes=['stg'], dma='stg')
                    S.op('pe', lambda e: e.transpose(PS[0][:, 0:64], stg[0:64, :], ident[0:64, 0:64]), reads=['stg', 'ident'], writes=[PK[0]])
                    S.op('dve', lambda e: e.tensor_copy(out=dst, in_=PS[0][:, 0:64]), reads=[PK[0]], writes=['s5k'])
                load_dup(P['s5_a_re'][0].rearrange("d g n -> (d g) n"), are2[:])
                load_dup(P['s5_a_im'][0].rearrange("d g n -> (d g) n"), aim2[:])
                S.op('sp', lambda e: e.dma_start(out=dtr[:], in_=P['s5_log_step'][0].rearrange("d g -> (d g)").partition_broadcast(128)), writes=['s5k'], dma='s5p')
                for (src, dst) in ((P['s5_c_re'][0], Cre), (P['s5_c_im'][0], Cim)):
                    sv = src.rearrange("d g p n -> (d g p) n")
                    dv = dst[:].rearrange("p a q -> p (a q)")
                    for r in range(8):
                        S.op('sp', lambda e, r=r, sv=sv: e.dma_start(out=stg[:, 0:64], in_=sv[r * 128:(r + 1) * 128, :]), writes=['stg'], dma='stg')
                        S.op('sp', lambda e, r=r, sv=sv: e.dma_start(out=stg[:, 64:128], in_=sv[r * 128:(r + 1) * 128, :]), writes=['stg'], dma='stg')
                        S.op('pe', lambda e: e.transpose(PS[0][:, 0:128], stg[:], ident[:]), reads=['stg', 'ident'], writes=[PK[0]])
                        S.op('dve', lambda e, r=r, dv=dv: e.tensor_copy(out=dv[:, r * 128:(r + 1) * 128], in_=PS[0][:, 0:128]), reads=[PK[0]], writes=['s5k'])
                for (src, dst) in ((P['s5_b_re'][0], Bre), (P['s5_b_im'][0], Bim)):
                    sv = src.rearrange("d g n q -> n (d g) q")
                    for hf in range(2):
                        S.op('sp', lambda e, sv=sv, dst=dst, hf=hf: e.dma_start(out=dst[hf * 64:(hf + 1) * 64, :, :], in_=sv), writes=['s5k'], dma='s5p')

                T('act', lambda e: e.activation(out=dtr[:], in_=dtr[:], func=AF.Exp))
                T('dve', lambda e: e.tensor_tensor(out=tA[:], in0=are2[:], in1=dtr[:], op=ALU.mult))
                T('act', lambda e: e.activation(out=tA[:], in_=tA[:], func=AF.Exp))
                T('dve', lambda e: e.tensor_tensor(out=tB[:], in0=aim2[:], in1=dtr[:], op=ALU.mult))

                def sin_reduced(dst, src_ang, shift):
                    T('dve', lambda e: e.tensor_scalar(out=tC[:], in0=src_ang, scalar1=shift, scalar2=None, op0=ALU.add))
                    T('dve', lambda e: e.tensor_scalar(out=tD[:], in0=tC[:], scalar1=1.0 / TWO_PI, scalar2=None, op0=ALU.mult))
                    T('dve', lambda e: e.tensor_scalar(out=tD[:], in0=tD[:], scalar1=MAGIC, scalar2=None, op0=ALU.add))
                    T('dve', lambda e: e.tensor_scalar(out=tD[:], in0=tD[:], scalar1=-MAGIC, scalar2=None, op0=ALU.add))
                    T('dve', lambda e: e.scalar_tensor_tensor(out=tC[:], in0=tD[:], scalar=-TWO_PI, in1=tC[:], op0=ALU.mult, op1=ALU.add))
                    T('dve', lambda e: e.tensor_scalar(out=tC[:], in0=tC[:], scalar1=3.1415925, scalar2=-3.1415925, op0=ALU.min, op1=ALU.max))
                    T('act', lambda e: e.activation(out=dst, in_=tC[:], func=AF.Sin))
                sin_reduced(lb[:], tB[:], 0.0)
                sin_reduced(la[:], tB[:], 1.5707963267948966)
                T('dve', lambda e: e.tensor_tensor(out=la[:], in0=la[:], in1=tA[:], op=ALU.mult))
                T('dve', lambda e: e.tensor_tensor(out=lb[:], in0=lb[:], in1=tA[:], op=ALU.mult))
                T('dve', lambda e: e.tensor_tensor(out=tA[:], in0=are2[:], in1=are2[:], op=ALU.mult))
                T('dve', lambda e: e.tensor_tensor(out=tB[:], in0=aim2[:], in1=aim2[:], op=ALU.mult))
                T('dve', lambda e: e.tensor_tensor(out=tA[:], in0=tA[:], in1=tB[:], op=ALU.add))
                T('dve', lambda e: e.reciprocal(out=tA[:], in_=tA[:]))
                T('dve', lambda e: e.tensor_scalar(out=tB[:], in0=la[:], scalar1=-1.0, scalar2=None, op0=ALU.add))
                T('dve', lambda e: e.tensor_tensor(out=tC[:], in0=tB[:], in1=are2[:], op=ALU.mult))
                T('dve', lambda e: e.tensor_tensor(out=tD[:], in0=lb[:], in1=aim2[:], op=ALU.mult))
                T('dve', lambda e: e.tensor_tensor(out=tC[:], in0=tC[:], in1=tD[:], op=ALU.add))
                T('dve', lambda e: e.tensor_tensor(out=fre[:], in0=tC[:], in1=tA[:], op=ALU.mult))
                T('dve', lambda e: e.tensor_tensor(out=tC[:], in0=lb[:], in1=are2[:], op=ALU.mult))
                T('dve', lambda e: e.tensor_tensor(out=tD[:], in0=tB[:], in1=aim2[:], op=ALU.mult))
                T('dve', lambda e: e.tensor_tensor(out=tC[:], in0=tC[:], in1=tD[:], op=ALU.subtract))
                T('dve', lambda e: e.tensor_tensor(out=fim[:], in0=tC[:], in1=tA[:], op=ALU.mult))
                bc16 = lambda ap: ap.unsqueeze(2).broadcast_to([128, 64, 16])
                T('dve', lambda e: e.tensor_tensor(out=pt[0][:], in0=Bre[:], in1=bc16(fre[:]), op=ALU.mult))
                T('dve', lambda e: e.tensor_tensor(out=pt[1][:], in0=Bim[:], in1=bc16(fim[:]), op=ALU.mult))
                T('dve', lambda e: e.tensor_tensor(out=pt[0][:], in0=pt[0][:], in1=pt[1][:], op=ALU.subtract))
                T('dve', lambda e: e.tensor_tensor(out=pt[2][:], in0=Bim[:], in1=bc16(fre[:]), op=ALU.mult))
                T('dve', lambda e: e.tensor_tensor(out=pt[3][:], in0=Bre[:], in1=bc16(fim[:]), op=ALU.mult))
                T('dve', lambda e: e.tensor_tensor(out=pt[2][:], in0=pt[2][:], in1=pt[3][:], op=ALU.add))
                T('dve', lambda e: e.tensor_copy(out=X1b[0:64], in_=pt[0][0:64]))
                T('dve', lambda e: e.tensor_copy(out=X1b[64:128], in_=pt[2][64:128]))
                T('dve', lambda e: e.tensor_copy(out=X2b[0:64], in_=pt[2][0:64]))
                T('dve', lambda e: e.tensor_scalar(out=X2b[64:128], in0=pt[0][64:128], scalar1=-1.0, scalar2=None, op0=ALU.mult))
                T('dve', lambda e: e.tensor_copy(out=X1c[0:64], in_=Cre[0:64]))
                T('dve', lambda e: e.tensor_scalar(out=X1c[64:128], in0=Cim[64:128], scalar1=-1.0, scalar2=None, op0=ALU.mult))
                T('dve', lambda e: e.tensor_copy(out=X2c[0:64], in_=Cim[0:64]))
                T('dve', lambda e: e.tensor_copy(out=X2c[64:128], in_=Cre[64:128]))

                def cmul(oa, ob, xa, xb, ya, yb, shape):
                    n = 1
                    for v in shape[1:]:
                        n *= v
                    tv = [pt[i][:].rearrange("p a q -> p (a q)")[:, 0:n] for i in range(4)]
                    if len(shape) == 3:
                        tv = [t.rearrange("p (a k) -> p a k", k=shape[2]) for t in tv]
                    T('dve', lambda e: e.tensor_tensor(out=tv[0], in0=xa, in1=ya, op=ALU.mult))
                    T('dve', lambda e: e.tensor_tensor(out=tv[1], in0=xb, in1=yb, op=ALU.mult))
                    T('dve', lambda e: e.tensor_tensor(out=tv[2], in0=xa, in1=yb, op=ALU.mult))
                    T('dve', lambda e: e.tensor_tensor(out=tv[3], in0=xb, in1=ya, op=ALU.mult))
                    T('dve', lambda e: e.tensor_tensor(out=oa, in0=tv[0], in1=tv[1], op=ALU.subtract))
                    T('dve', lambda e: e.tensor_tensor(out=ob, in0=tv[2], in1=tv[3], op=ALU.add))
                T('dve', lambda e: e.memset(PFa[:, :, 0:1], 1.0))
                T('dve', lambda e: e.memset(PFb[:, :, 0:1], 0.0))
                T('dve', lambda e: e.tensor_copy(out=PFa[:, :, 1:2], in_=la[:].unsqueeze(2)))
                T('dve', lambda e: e.tensor_copy(out=PFb[:, :, 1:2], in_=lb[:].unsqueeze(2)))
                k0 = 2
                while k0 <= 16:
                    h_ = k0 // 2
                    cmul(PFa[:, :, k0:k0 + 1], PFb[:, :, k0:k0 + 1], PFa[:, :, h_:h_ + 1], PFb[:, :, h_:h_ + 1], PFa[:, :, h_:h_ + 1], PFb[:, :, h_:h_ + 1], [128, 64, 1])
                    if k0 > 2 or True:
                        bk = lambda ap, k0=k0: ap.broadcast_to([128, 64, k0 - 1])
                        cmul(PFa[:, :, k0 + 1:2 * k0], PFb[:, :, k0 + 1:2 * k0], PFa[:, :, 1:k0], PFb[:, :, 1:k0], bk(PFa[:, :, k0:k0 + 1]), bk(PFb[:, :, k0:k0 + 1]), [128, 64, k0 - 1])
                    k0 *= 2
                cmul(PFa[:, :, 32:33], PFb[:, :, 32:33], PFa[:, :, 16:17], PFb[:, :, 16:17], PFa[:, :, 16:17], PFb[:, :, 16:17], [128, 64, 1])
                for k in range(33):
                    T('pool', lambda e, k=k: e.tensor_copy(out=PRa[:, :, k:k + 1], in_=PFa[:, :, 32 - k:33 - k]))
                    T('pool', lambda e, k=k: e.tensor_copy(out=PRb[:, :, k:k + 1], in_=PFb[:, :, 32 - k:33 - k]))
                T('dve', lambda e: e.tensor_tensor(out=tA[:], in0=la[:], in1=la[:], op=ALU.mult))
                T('dve', lambda e: e.tensor_tensor(out=tB[:], in0=lb[:], in1=lb[:], op=ALU.mult))
                T('dve', lambda e: e.tensor_tensor(out=tA[:], in0=tA[:], in1=tB[:], op=ALU.add))
                T('dve', lambda e: e.reciprocal(out=tA[:], in_=tA[:]))
                T('dve', lambda e: e.memset(PNa[:, :, 0:1], 1.0))
                T('dve', lambda e: e.memset(PNb[:, :, 0:1], 0.0))
                T('dve', lambda e: e.tensor_tensor(out=PNa[:, :, 1:2], in0=la[:].unsqueeze(2), in1=tA[:].unsqueeze(2), op=ALU.mult))
                T('dve', lambda e: e.scalar_tensor_tensor(out=PNb[:, :, 1:2], in0=lb[:].unsqueeze(2), scalar=-1.0, in1=tA[:].unsqueeze(2), op0=ALU.mult, op1=ALU.mult))
                for k in range(2, 8):
                    cmul(PNa[:, :, k:k + 1], PNb[:, :, k:k + 1], PNa[:, :, k - 1:k], PNb[:, :, k - 1:k], PNa[:, :, 1:2], PNb[:, :, 1:2], [128, 64, 1])
                for k in range(8):
                    T('pool', lambda e, k=k: e.tensor_copy(out=PNRa[:, :, k:k + 1], in_=PNa[:, :, 7 - k:8 - k]))
                    T('pool', lambda e, k=k: e.tensor_copy(out=PNRb[:, :, k:k + 1], in_=PNb[:, :, 7 - k:8 - k]))
                T('dve', lambda e: e.tensor_copy(out=Qa[:, :, 0:1], in_=PFa[:, :, 32:33]))
                T('dve', lambda e: e.tensor_copy(out=Qb[:, :, 0:1], in_=PFb[:, :, 32:33]))
                for j in range(1, 9):
                    cmul(Qa[:, :, j:j + 1], Qb[:, :, j:j + 1], Qa[:, :, j - 1:j], Qb[:, :, j - 1:j], Qa[:, :, j - 1:j], Qb[:, :, j - 1:j], [128, 64, 1])

                prep_list = S.defer_end()
                nb = norm_bufs(es_t)
                hn = [sbt(f"hn{i}", [128, 16, 512], BF16) for i in range(2)]
                wub = sbt("wub", [128, 16, 512], BF16)
                usb = [sbt(f"usb{i}", [128, 4, 8, 64]) for i in range(2)]
                S.op('pool', lambda e: e.dma_start(out=wub[:], in_=P['e_w_in'][0][:, 3072:3584].rearrange("(kt p) c -> p kt c", p=128)), writes=['wub'], dma='wub')
                jobs = []
                for i in range(NT):
                    jobs.append((xe, 512 * i, 4, A1[:, 0, :], modc[:, 0, 0:16], 'A1', 'modc', [(Us_own, 'Us_own', 64 * i)]))
                for i in range(FAR // 512):
                    jobs.append((xfar, 512 * i, 4, A1[:, 0, :], modc[:, 0, 0:16], 'A1', 'modc', [(Us_far, 'Us_far', 32 + 64 * i)]))
                jobs.append((ctxb, 0, 2, Ac[:], modx[:, 0:16], 'Ac', 'modx', [(Us_far, 'Us_far', 0), (Us_far, 'Us_far', 1504)]))
                norm_pre(nb, jobs[0][0], jobs[0][1], 0, use_pow=True)
                pend_st = []

                def pop_store():
                    if pend_st:
                        fn, rk, dk = pend_st.pop(0)
                        S.op('pool', fn, reads=[rk], writes=[dk], dma='st_us')
                for n, (src, tok0, nblk, Acol, Bcol, akey, bkey, dsts) in enumerate(jobs):
                    hb = n % 2
                    ntok = nblk * 128
                    nj = ntok // 8
                    for blk in range(nblk):
                        norm_post(nb, blk, Acol, Bcol, akey, bkey, hn[hb], f"hn{hb}", evac_eng='act')
                        if blk + 1 < nblk:
                            norm_pre(nb, src, tok0, blk + 1, use_pow=True)
                        elif n + 1 < len(jobs):
                            norm_pre(nb, jobs[n + 1][0], jobs[n + 1][1], 0, use_pow=True)
                        pop_store()
                        S.replay(prep_list, 3)
                    for ot in range(4):
                        pb = ot % 2
                        for kt in range(16):
                            S.op('pe', lambda e, kt=kt, ot=ot, pb=pb, hb=hb, ntok=ntok: e.matmul(PS[pb][:, 0:ntok], lhsT=wub[:, kt, ot * 128:(ot + 1) * 128], rhs=hn[hb][:, kt, 0:ntok], start=(kt == 0), stop=(kt == 15)), reads=['wub', f"hn{hb}"], writes=[PK[pb]])
                        S.op('act', lambda e, ot=ot, pb=pb, hb=hb, ntok=ntok, nj=nj: e.copy(out=usb[hb][:, ot, :, 0:nj], in_=PS[pb][:, 0:ntok].rearrange("p (j s) -> p s j", s=8)), reads=[PK[pb]], writes=[f"usb{hb}"])
                    for (dst, dkey, j0) in dsts:
                        for ot in range(4):
                            pend_st.append(((lambda e, dst=dst, j0=j0, ot=ot, hb=hb, nj=nj: e.dma_start(out=dst[:, ot * 128:(ot + 1) * 128, j0:j0 + nj].rearrange("s c j -> c s j"), in_=usb[hb][:, ot, :, 0:nj])), f"usb{hb}", dkey))
                while pend_st:
                    pop_store()
                S.replay(prep_list, 10 ** 6)
                emit_stage()
                es_t.close()
                WT = [[sb(f"WT{d}_{b}", [128, 32, 16]) for b in range(2)] for d in range(2)]
                VV = [[sb(f"VV{d}_{b}", [128, 32, 16]) for b in range(2)] for d in range(2)]
                VN = [[sb(f"VN{d}_{b}", [128, 8, 16]) for b in range(2)] for d in range(2)]
                WS = [[sb(f"WS{d}_{b}", [128, 4, 128]) for b in range(2)] for d in range(2)]
                t1 = [sb(f"t1_{i}", [128, 32, 16]) for i in range(2)]
                t2 = [sb(f"t2_{i}", [128, 32, 16]) for i in range(2)]
                Mg = [sb(f"Mg{b}", [128, 7, 128]) for b in range(2)]
                Lm = [[sb(f"Lm{d}_{b}", [128, 9, 128]) for b in range(2)] for d in range(2)]
                Xfar = [sb(f"Xfar{d}", [128, 376]) for d in range(2)]
                XO = [sb(f"XO{d}", [128, 145]) for d in range(2)]
                cvec_ = [sb(f"cv{d}", [128, 1]) for d in range(2)]
                junk = sb("junk", [128, 376])
                uo = [sb(f"uo{b}", [128, 4, 576]) for b in range(2)]
                uf = sb("uf", [128, 2, 1536])
                ysb = sb("ysb", [128, 4, 576])
                nt = [0]
                def gen_part(g):
                    emit_casts(castA, 3)
                    gl = g % 4
                    gb = g % 2
                    ub = (g // 4) % 2
                    if gl == 0:
                        for s_ in range(8):
                            S.op('sp', lambda e, s_=s_, g=g, ub=ub: e.dma_start(out=uo[ub][s_ * 16:(s_ + 1) * 16, :, :], in_=Us_own[s_, g * 16:(g + 4) * 16, :].rearrange("(g q) j -> q g j", q=16)), reads=['Us_own'], writes=[f"uo{ub}"], dma=f"uo{ub}")
                    if g % 2 == 0:
                        for s_ in range(8):
                            S.op('sp', lambda e, s_=s_, g=g: e.dma_start(out=uf[s_ * 16:(s_ + 1) * 16, :, :], in_=Us_far[s_, g * 16:(g + 2) * 16, :].rearrange("(g q) j -> q g j", q=16)), reads=['Us_far'], writes=["uf"], dma="uf")
                    for d in range(2):
                        dg = d * 32 + g
                        tabs_w = (PRa[:, dg, 1:33], PRb[:, dg, 1:33]) if d == 0 else (PFa[:, dg, 0:32], PFb[:, dg, 0:32])
                        tabs_v = (PFa[:, dg, 1:33], PFb[:, dg, 1:33]) if d == 0 else (PRa[:, dg, 0:32], PRb[:, dg, 0:32])
                        tabs_n = (PNRa[:, dg, :], PNRb[:, dg, :]) if d == 0 else (PNa[:, dg, :], PNb[:, dg, :])
                        for (dst, dkey, x1, x2, tabs, ns) in ((WT[d][gb], f"WT{d}_{gb}", X1b, X2b, tabs_w, 32), (VV[d][gb], f"VV{d}_{gb}", X1c, X2c, tabs_v, 32), (VN[d][gb], f"VN{d}_{gb}", X1c, X2c, tabs_n, 8)):
                            tb = nt[0] % 2; nt[0] += 1
                            xa = x1[:, dg, :].unsqueeze(1).broadcast_to([128, ns, 16])
                            xb = x2[:, dg, :].unsqueeze(1).broadcast_to([128, ns, 16])
                            pa = tabs[0].unsqueeze(2).broadcast_to([128, ns, 16])
                            pb_ = tabs[1].unsqueeze(2).broadcast_to([128, ns, 16])
                            S.op('dve', lambda e, tb=tb, xa=xa, pa=pa, ns=ns: e.tensor_tensor(out=t1[tb][:, 0:ns, :], in0=xa, in1=pa, op=ALU.mult), reads=['s5k'], writes=[f"t1_{tb}"])
                            S.op('pool', lambda e, tb=tb, xb=xb, pb_=pb_, ns=ns: e.tensor_tensor(out=t2[tb][:, 0:ns, :], in0=xb, in1=pb_, op=ALU.mult), reads=['s5k'], writes=[f"t2_{tb}"])
                            S.op('dve', lambda e, tb=tb, dst=dst, ns=ns: e.tensor_tensor(out=dst[:, 0:ns, :], in0=t1[tb][:, 0:ns, :], in1=t2[tb][:, 0:ns, :], op=ALU.subtract), reads=[f"t1_{tb}", f"t2_{tb}"], writes=[dkey])
                        wtv = WT[d][gb][:].rearrange("p (h s) q -> p h (s q)", h=4)
                        for sh_ in range(4):
                            S.op('pe', lambda e, sh_=sh_, wtv=wtv: e.transpose(PS[0][:, sh_ * 128:(sh_ + 1) * 128], wtv[:, sh_, :], ident[:]), reads=[f"WT{d}_{gb}", 'ident'], writes=[PK[0]])
                        S.op('act', lambda e, d=d, gb=gb: e.copy(out=WS[d][gb][:].rearrange("p h c -> p (h c)"), in_=PS[0][:]), reads=[PK[0]], writes=[f"WS{d}_{gb}"])
                        for j in range(9):
                            S.op('pool', lambda e, d=d, gb=gb, j=j, dg=dg: e.tensor_scalar(out=Lm[d][gb][:, j, :], in0=ident[:], scalar1=Qa[:, dg, j:j + 1], scalar2=0.0, op0=ALU.mult, op1=ALU.add), reads=['ident', 's5k'], writes=[f"Lm{d}_{gb}"])
                            S.op('dve', lambda e, d=d, gb=gb, j=j, dg=dg: e.scalar_tensor_tensor(out=Lm[d][gb][:, j, :], in0=Jm[:], scalar=Qb[:, dg, j:j + 1], in1=Lm[d][gb][:, j, :], op0=ALU.mult, op1=ALU.add), reads=['s5k', f"Lm{d}_{gb}"], writes=[f"Lm{d}_{gb}"])
                    vv = [VV[d][gb][:].rearrange("p (h t) q -> p h (t q)", h=4) for d in range(2)]
                    wv = [WT[d][gb][:].rearrange("p (h s) q -> p h (s q)", h=4) for d in range(2)]
                    vn = [VN[d][gb][:].rearrange("p t q -> p (t q)") for d in range(2)]
                    rk = [f"WT0_{gb}", f"WT1_{gb}", f"VV0_{gb}", f"VV1_{gb}", f"VN0_{gb}", f"VN1_{gb}"]
                    for dl in range(1, 4):
                        S.op('pe', lambda e, dl=dl: e.matmul(PS[1][:, (dl - 1) * 128:dl * 128], lhsT=wv[0][:, 3, :], rhs=vv[0][:, dl - 1, :], start=True, stop=True, skip_group_check=True), reads=rk, writes=[PK[1]])
                        S.op('pe', lambda e, dl=dl: e.matmul(PS[2][:, (dl - 1) * 128:dl * 128], lhsT=wv[1][:, 0, :], rhs=vv[1][:, 4 - dl, :], start=True, stop=True, skip_group_check=True), reads=rk, writes=[PK[2]])
                    S.op('pe', lambda e: e.matmul(PS[1][:, 384:512], lhsT=wv[0][:, 3, :], rhs=vn[0], start=True, stop=True, skip_group_check=True), reads=rk, writes=[PK[1]])
                    S.op('pe', lambda e: e.matmul(PS[2][:, 384:512], lhsT=wv[1][:, 0, :], rhs=vn[1], start=True, stop=True, skip_group_check=True), reads=rk, writes=[PK[2]])
                    mk = f"Mg{gb}"
                    S.op('act', lambda e, gb=gb: e.copy(out=Mg[gb][:, 4:7, :].rearrange("p a c -> p (a c)"), in_=PS[1][:, 0:384]), reads=[PK[1]], writes=[mk])
                    for dl in range(1, 4):
                        S.op('act', lambda e, gb=gb, dl=dl: e.copy(out=Mg[gb][:, 3 - dl, :], in_=PS[2][:, (dl - 1) * 128:dl * 128]), reads=[PK[2]], writes=[mk])
                    S.op('dve', lambda e, gb=gb: e.tensor_tensor(out=Mg[gb][:, 3, :], in0=PS[1][:, 384:512], in1=mkf[:], op=ALU.mult), reads=[PK[1], 's5k'], writes=[mk])
                    S.op('dve', lambda e, gb=gb: e.tensor_tensor(out=junk[:, 0:128], in0=PS[2][:, 384:512], in1=mkb[:], op=ALU.mult), reads=[PK[2], 's5k'], writes=['junk'])
                    S.op('dve', lambda e, gb=gb: e.tensor_tensor(out=Mg[gb][:, 3, :], in0=Mg[gb][:, 3, :], in1=junk[:, 0:128], op=ALU.add), reads=[mk, 'junk'], writes=[mk])
                    S.op('dve', lambda e, gb=gb, g=g: e.scalar_tensor_tensor(out=Mg[gb][:, 3, :], in0=ident[:], scalar=Dcol[:, g:g + 1], in1=Mg[gb][:, 3, :], op0=ALU.mult, op1=ALU.add), reads=[mk, 'ident', 's5k'], writes=[mk])
                def main_part(g, fill):
                    gl = g % 4
                    gb = g % 2
                    ub = (g // 4) % 2
                    mk = f"Mg{gb}"
                    vv = [VV[d][gb][:].rearrange("p (h t) q -> p h (t q)", h=4) for d in range(2)]
                    for d in range(2):
                        b0 = 0 if d == 0 else 32
                        for sh_ in range(4):
                            S.op('pe', lambda e, d=d, sh_=sh_, b0=b0, gl=gl: e.matmul(PS[3][:, 0:376], lhsT=WS[d][gb][:, sh_, :], rhs=uf[:, g % 2, b0 + sh_:b0 + sh_ + 1501:4], start=(sh_ == 0), stop=(sh_ == 3)), reads=[f"WS{d}_{gb}", 'uf'], writes=[PK[3]])
                        S.op('act', lambda e, d=d: e.copy(out=Xfar[d][:], in_=PS[3][:, 0:376]), reads=[PK[3]], writes=[f"Xfar{d}"])
                    for j in range(9):
                        sh = 1 << j
                        n_ = 376 - sh
                        for d in range(2):
                            src = Xfar[d][:, 0:n_] if d == 0 else Xfar[d][:, sh:376]
                            dst = Xfar[d][:, sh:376] if d == 0 else Xfar[d][:, 0:n_]
                            S.op('pe', lambda e, d=d, j=j, src=src, n_=n_: e.matmul(PS[4 + d][:, 0:n_], lhsT=Lm[d][gb][:, j, :], rhs=src, start=True, stop=True), reads=[f"Lm{d}_{gb}", f"Xfar{d}"], writes=[PK[4 + d]])
                            S.op('dve', lambda e, d=d, dst=dst, n_=n_: e.tensor_tensor(out=dst, in0=PS[4 + d][:, 0:n_], in1=dst, op=ALU.add), reads=[PK[4 + d], f"Xfar{d}"], writes=[f"Xfar{d}"])
                        fill(6)
                    for d in range(2):
                        S.op('dve', lambda e, d=d: e.scalar_tensor_tensor(out=junk[:], in0=Xfar[d][:], scalar=1.0, in1=ohr[d][:, 0:376], op0=ALU.mult, op1=ALU.mult, accum_out=cvec_[d][:]), reads=[f"Xfar{d}", 'ohr'], writes=['junk', f"cv{d}"])
                    for d in range(2):
                        for sh_ in range(4):
                            S.op('pe', lambda e, d=d, sh_=sh_, gl=gl, ub=ub: e.matmul(PS[3][:, 0:144], lhsT=WS[d][gb][:, sh_, :], rhs=uo[ub][:, gl, sh_:sh_ + 573:4], start=(sh_ == 0), stop=(sh_ == 3)), reads=[f"WS{d}_{gb}", f"uo{ub}"], writes=[PK[3]])
                        if d == 0:
                            S.op('act', lambda e: e.copy(out=XO[0][:, 1:145], in_=PS[3][:, 0:144]), reads=[PK[3]], writes=["XO0"])
                            S.op('dve', lambda e: e.tensor_copy(out=XO[0][:, 0:1], in_=cvec_[0][:]), reads=["cv0"], writes=["XO0"])
                        else:
                            S.op('act', lambda e: e.copy(out=XO[1][:, 0:144], in_=PS[3][:, 0:144]), reads=[PK[3]], writes=["XO1"])
                            S.op('dve', lambda e: e.tensor_copy(out=XO[1][:, 144:145], in_=cvec_[1][:]), reads=["cv1"], writes=["XO1"])
                    for j in range(8):
                        sh = 1 << j
                        n_ = 145 - sh
                        for d in range(2):
                            src = XO[d][:, 0:n_] if d == 0 else XO[d][:, sh:145]
                            dst = XO[d][:, sh:145] if d == 0 else XO[d][:, 0:n_]
                            S.op('pe', lambda e, d=d, j=j, src=src, n_=n_: e.matmul(PS[4 + d][:, 0:n_], lhsT=Lm[d][gb][:, j, :], rhs=src, start=True, stop=True), reads=[f"Lm{d}_{gb}", f"XO{d}"], writes=[PK[4 + d]])
                            S.op('dve', lambda e, d=d, dst=dst, n_=n_: e.tensor_tensor(out=dst, in0=PS[4 + d][:, 0:n_], in1=dst, op=ALU.add), reads=[PK[4 + d], f"XO{d}"], writes=[f"XO{d}"])
                        fill(6)
                    fill(10000)
                    for th in range(4):
                        yb = 6 + th // 2
                        c0 = (th % 2) * 144
                        for sh_ in range(4):
                            S.op('pe', lambda e, th=th, sh_=sh_, yb=yb, c0=c0: e.matmul(PS[yb][:, c0:c0 + 144], lhsT=Mg[gb][:, th - sh_ + 3, :], rhs=uo[ub][:, gl, sh_:sh_ + 573:4], start=(sh_ == 0), stop=False, skip_group_check=True), reads=[mk, f"uo{ub}"], writes=[PK[yb]])
                        for d in range(2):
                            hsrc = XO[0][:, 0:144] if d == 0 else XO[1][:, 1:145]
                            S.op('pe', lambda e, th=th, d=d, yb=yb, c0=c0, hsrc=hsrc: e.matmul(PS[yb][:, c0:c0 + 144], lhsT=vv[d][:, th, :], rhs=hsrc, start=False, stop=(d == 1), skip_group_check=True), reads=[f"VV{d}_{gb}", f"XO{d}"], writes=[PK[yb]])
                    for hh in range(2):
                        yb = 6 + hh
                        S.op('act', lambda e, hh=hh, yb=yb: e.copy(out=ysb[:, gl, :].rearrange("p (j t) -> p t j", t=4)[:, 2 * hh:2 * hh + 2, :], in_=PS[yb][:, 0:288].rearrange("p (t j) -> p t j", t=2)), reads=[PK[yb]], writes=['ysb'])
                    if gl == 3:
                        g0 = g - 3
                        for tl in range(8):
                            S.op('sp', lambda e, tl=tl, g0=g0: e.dma_start(out=Ys[g0 * 16:(g0 + 4) * 16, tl, 8:584].rearrange("(g p) j -> p g j", p=16), in_=ysb[tl * 16:(tl + 1) * 16, :, :]), reads=['ysb'], writes=['Ys'], dma='st_ys')
                S.defer_begin(); gen_part(0); pend = S.defer_end()
                S.replay(pend, 10000)
                for g in range(32):
                    if g + 1 < 32:
                        S.defer_begin(); gen_part(g + 1); pend = S.defer_end()
                    else:
                        pend = []
                    main_part(g, lambda n, pend=pend: S.replay(pend, n))
                emit_casts(castA, 10 ** 6)
                emit_stage()

        if 's5' in stages or 'l0mix' in stages:
            s5b_stage()
        if 'l0mix' in stages or 'l0mix_nos5' in stages:
            l0mix_stage(xe, h1, 'h1', use_s5=('l0mix' in stages))
        if 'l0ffn' in stages:
            ffn_stage(0, h1 if ('l0mix' in stages or 'l0mix_nos5' in stages) else xe, 'h1', h2, 'h2')
        if 'l1mix' in stages:
            l1mix_stage(h2, h3, 'h3')
        if 'l1ffn' in stages:
            ffn_stage(1, h3, 'h3', h1, 'h1')
            final_stage(h1, out)
    return nc


_NC_CACHE = {}


def make_in_maps(inputs):
    x = np.ascontiguousarray(inputs['x'], dtype=np.float32)
    c = np.asarray(inputs['c'], dtype=np.float32)
    ctx = np.asarray(inputs['ctx'], dtype=np.float32)
    c_ctx = np.asarray(inputs['c_ctx'], dtype=np.float32)
    pr = {n: np.ascontiguousarray(np.asarray(inputs[n], dtype=np.float32)) for n, _ in PARAMS}
    maps, offs = [], []
    for k in range(8):
        b, q = k // 4, k % 4
        w0 = min(max(q * 4096 - 256, 0), 16384 - TE)
        offs.append((b, q, q * 4096 - w0))
        nbc = w0 // 32
        ohf = np.zeros((1, 384), np.float32); ohf[0, 7 + nbc] = 1.0
        ohb = np.zeros((1, 384), np.float32); ohb[0, nbc] = 1.0
        m = dict(pr)
        m['xe'] = np.ascontiguousarray(x[b, w0:w0 + TE])
        m['xfar'] = np.ascontiguousarray(np.concatenate([x[b, :w0], x[b, w0 + TE:]], axis=0))
        m['ctxb'] = np.ascontiguousarray(ctx[b])
        m['cvec'] = np.ascontiguousarray(np.stack([c[b], c_ctx], axis=0))
        m['ohf'] = ohf
        m['ohb'] = ohb
        maps.append(m)
    return maps, offs


def kernel(**inputs):
    maps, offs = make_in_maps(inputs)
    if 'nc' not in _NC_CACHE:
        _NC_CACHE['nc'] = build()
    res = run_bass_kernel_spmd(_NC_CACHE['nc'], maps, core_ids=list(range(8)))
    out = np.empty((2, 16384, D), np.float32)
    for k, (b, q, off) in enumerate(offs):
        out[b, q * 4096:(q + 1) * 4096] = res.results[k]['out'][off:off + 4096]
    return out
```

```python
import numpy as np
import concourse.bass as bass
import concourse.mybir as mybir
from concourse.bass_utils import run_bass_kernel_spmd
from contextlib import ExitStack

F32 = mybir.dt.float32
BF16 = mybir.dt.bfloat16
AF = mybir.ActivationFunctionType
ALU = mybir.AluOpType

D = 2048
TE = 4608
NT = 9
FAR = 11776
LAG = 64
ENGS = ['pe', 'act', 'dve', 'pool', 'sp']
SAME_ENG_WAIT = True
POOL_SCALE = True


class Sched:
    def __init__(self, nc, es):
        self.nc = nc
        self.es = es
        self.q = {e: [] for e in ENGS}
        self.sem = {}
        self.cnt = {}
        self.known = {e: {} for e in ENGS}
        self.lastw = {}
        self.readers = {}
        self.nops = 0
        self.deferred = None
        self.nodrain = set()

    def defer_begin(self):
        self.deferred = []

    def defer_end(self):
        d = self.deferred
        self.deferred = None
        return d

    def replay(self, lst, n):
        for _ in range(min(n, len(lst))):
            e, fn, reads, writes, dma = lst.pop(0)
            self.op(e, fn, reads, writes, dma)

    def _sem(self, name):
        if name not in self.sem:
            self.sem[name] = self.es.enter_context(self.nc.semaphore(name))
            self.cnt[name] = 0
        return self.sem[name]

    def op(self, e, fn, reads=(), writes=(), dma=None):
        if self.deferred is not None:
            self.deferred.append((e, fn, list(reads), list(writes), dma))
            return None
        deps = []
        raw = set()
        for r in reads:
            if r in self.lastw:
                deps.append(self.lastw[r])
                raw.add(self.lastw[r])
        for w in writes:
            if w in self.lastw:
                deps.append(self.lastw[w])
            deps.extend(self.readers.get(w, {}).items())
        if dma is None:
            sname, inc = 'S_' + e, 1
        else:
            sname, inc = 'D_' + dma, 16
        self._sem(sname)
        waits = {}
        for (sn, v) in deps:
            if sn == 'S_' + e and (e == 'pe' or not SAME_ENG_WAIT or (sn, v) not in raw):
                continue
            if self.known[e].get(sn, 0) >= v:
                continue
            waits[sn] = max(waits.get(sn, 0), v)
        for sn, v in waits.items():
            self.known[e][sn] = v
        self.cnt[sname] += inc
        tok = (sname, self.cnt[sname])
        self.q[e].append((list(waits.items()), fn, sname, inc))
        for w in writes:
            self.lastw[w] = tok
            self.readers[w] = {}
        for r in reads:
            d = self.readers.setdefault(r, {})
            d[tok[0]] = max(d.get(tok[0], 0), tok[1])
        self.nops += 1
        return tok

    def drain(self, e='sp'):
        waits = []
        for sn, v in self.cnt.items():
            if v > 0 and self.known[e].get(sn, 0) < v and sn != 'S_' + e and sn not in self.nodrain:
                waits.append((sn, v))
                self.known[e][sn] = v
        self.q[e].append((waits, None, None, 0))

    def emit(self):
        nc = self.nc
        engobj = {'pe': 'tensor', 'act': 'scalar', 'dve': 'vector', 'pool': 'gpsimd', 'sp': 'sync'}
        with nc.Block() as block:
            for e in ENGS:
                def body(eng, e=e):
                    for waits, fn, sname, inc in self.q[e]:
                        for sn, v in waits:
                            eng.wait_ge(self.sem[sn], v)
                        if fn is not None:
                            ins = fn(eng)
                            ins.then_inc(self.sem[sname], inc)
                getattr(block, engobj[e])(body)
        self.q = {e: [] for e in ENGS}


PARAMS = [
    ('ada_w', [2, 2048, 12288]), ('ada_b', [2, 12288]), ('norm_g', [2, 2, 2048]), ('final_g', [2048]),
    ('e_w_in', [1, 2048, 3584]), ('e_conv_w', [1, 31, 1536]), ('e_conv_b', [1, 1536]),
    ('e_ln_g', [1, 1536]), ('e_ln_b', [1, 1536]),
    ('s5_a_re', [1, 2, 32, 64]), ('s5_a_im', [1, 2, 32, 64]), ('s5_log_step', [1, 2, 32]),
    ('s5_b_re', [1, 2, 32, 64, 16]), ('s5_b_im', [1, 2, 32, 64, 16]),
    ('s5_c_re', [1, 2, 32, 16, 64]), ('s5_c_im', [1, 2, 32, 16, 64]),
    ('s5_d', [1, 512]), ('s5_glu_w', [1, 512, 512]), ('s5_glu_b', [1, 512]), ('e_w_out', [1, 2048, 2048]),
    ('o_w_in', [1, 2048, 4096]), ('o_ln_g', [1, 2048]), ('o_ln_b', [1, 2048]),
    ('o_sgu_w', [1, 8, 128, 128]), ('o_sgu_b', [1, 8, 128]), ('o_w_out', [1, 2048, 2048]),
    ('f_w_in', [2, 2048, 8192]), ('f_conv_w', [2, 3, 3, 8192]), ('f_conv_b', [2, 8192]),
    ('f_w_out', [2, 4096, 2048]),
]


def build(stages=('setup', 'l0mix', 'l0ffn', 'l1mix', 'l1ffn'), debug=False):
    nc = bass.Bass("TRN2", target_bir_lowering=False)

    def dram(name, shape, dtype=F32, kind="ExternalInput"):
        return nc.dram_tensor(name, list(shape), dtype, kind=kind).ap()

    xe = dram("xe", [TE, D])
    xfar = dram("xfar", [FAR, D])
    ctxb = dram("ctxb", [256, D])
    cvec = dram("cvec", [2, D])
    ohf = dram("ohf", [1, 384])
    ohb = dram("ohb", [1, 384])
    P = {n: dram(n, s) for n, s in PARAMS}
    out = dram("out", [TE, D], kind="ExternalOutput")
    hk = "ExternalOutput" if debug else "Internal"
    h1 = dram("h1", [TE, D], kind=hk)
    h2 = dram("h2", [TE, D], kind=hk)
    h3 = dram("h3", [TE, D], kind=hk)
    Wb_ein = dram("Wb_ein", [24, 128, 16, 128], BF16, kind="Internal")
    Wb_fin = [dram(f"Wb_fin{l}", [64, 128, 16, 128], BF16, kind="Internal") for l in range(2)]
    Wb_oinu = dram("Wb_oinu", [16, 128, 16, 128], BF16, kind="Internal")
    Wb_oinv = dram("Wb_oinv", [4, 128, 16, 512], BF16, kind="Internal")
    Wb_eout = dram("Wb_eout", [16, 128, 2048], BF16, kind="Internal")
    Wb_oout = dram("Wb_oout", [16, 128, 2048], BF16, kind="Internal")
    Wb_fout = [dram(f"Wb_fout{l}", [32, 128, 2048], BF16, kind="Internal") for l in range(2)]
    moddram = dram("moddram", [2, 2, 12288], kind="Internal")
    Ys = dram("Ys", [512, 8, 656], kind=hk)
    Us_own = dram("Us_own", [8, 512, 576], kind=hk)
    Us_far = dram("Us_far", [8, 512, 1536], kind=hk)

    es_g = ExitStack()
    with es_g:
        S = Sched(nc, es_g)

        def sbg(name, shape, dt=F32):
            return es_g.enter_context(nc.sbuf_tensor(name, list(shape), dt))

        PS = [es_g.enter_context(nc.psum_tensor(f"ps{i}", [128, 512], F32)) for i in range(8)]
        PK = [f"ps{i}" for i in range(8)]

        ident = sbg("ident", [128, 128])
        identb = sbg("identb", [128, 128], BF16)
        onesb = sbg("onesb", [128, 128], BF16)
        onesf = sbg("onesf", [128, 128])
        mhalf = sbg("mhalf", [128, 1])
        modc = sbg("modc", [128, 2, 96])
        modx = sbg("modx", [128, 32])
        A1 = sbg("A1", [128, 2, 16]); A2 = sbg("A2", [128, 2, 16]); Ac = sbg("Ac", [128, 16])
        fgc = sbg("fgc", [128, 16])
        cwc = sbg("cwc", [128, 31, 12]); cbc = sbg("cbc", [128, 12]); lgc = sbg("lgc", [128, 12]); lbc = sbg("lbc", [128, 12])
        fcw = sbg("fcw", [128, 2, 9, 64]); fcb = sbg("fcb", [128, 2, 64])
        glub = sbg("glub", [128, 4])
        olg = sbg("olg", [128, 16]); olb = sbg("olb", [128, 16])

        uid = [0]

        def uname(name):
            uid[0] += 1
            return f"{name}_u{uid[0]}"

        def emit_stage():
            S.drain('sp')
            S.emit()

        def cast_blocks(dst, src, nblk, tag):
            sv = src.rearrange("(kt p) (ot c) -> ot p kt c", p=128, c=128)
            return [((lambda e, ot=ot: e.dma_start(out=dst[ot], in_=sv[ot])), tag) for ot in range(nblk)]

        def cast_rows(dst, src, nkt, tag):
            return [((lambda e, kt=kt: e.dma_start(out=dst[kt], in_=src[kt * 128:(kt + 1) * 128, :])), tag) for kt in range(nkt)]

        def emit_casts(lst, n):
            for _ in range(min(n, len(lst))):
                fn, tag = lst.pop(0)
                S.op('pool', fn, writes=[tag], dma=tag)

        svv_ = P['o_w_in'][0][:, 2048:4096].rearrange("(kt p) (cg c) -> cg p kt c", p=128, c=512)
        cast0 = cast_blocks(Wb_ein, P['e_w_in'][0][:, 0:3072], 24, 'Wb_ein') + cast_rows(Wb_eout, P['e_w_out'][0], 16, 'Wb_eout')
        castA = cast_blocks(Wb_fin[0], P['f_w_in'][0], 64, 'Wb_fin0') + cast_rows(Wb_fout[0], P['f_w_out'][0], 32, 'Wb_fout0')
        castB = (cast_blocks(Wb_oinu, P['o_w_in'][0][:, 0:2048], 16, 'Wb_oinu')
                 + [((lambda e, cg=cg: e.dma_start(out=Wb_oinv[cg], in_=svv_[cg])), 'Wb_oinv') for cg in range(4)]
                 + cast_rows(Wb_oout, P['o_w_out'][0], 16, 'Wb_oout'))
        castC = cast_blocks(Wb_fin[1], P['f_w_in'][1], 64, 'Wb_fin1') + cast_rows(Wb_fout[1], P['f_w_out'][1], 32, 'Wb_fout1')
        for t_ in ('Wb_ein', 'Wb_oinu', 'Wb_oinv', 'Wb_fin0', 'Wb_fin1', 'Wb_eout', 'Wb_oout', 'Wb_fout0', 'Wb_fout1'):
            S.nodrain.add('D_' + t_)

        if 'setup' in stages:
            with ExitStack() as es:
                def sb(name, shape, dt=F32):
                    return es.enter_context(nc.sbuf_tensor(uname(name), list(shape), dt))

                S.op('pool', lambda e: e.memset(ident[:], 1.0), writes=['ident'])
                S.op('pool', lambda e: e.affine_select(out=ident[:], in_=ident[:], pattern=[[-1, 128]], compare_op=ALU.is_equal, fill=0.0, base=0, channel_multiplier=1), reads=['ident'], writes=['ident'])
                S.op('pool', lambda e: e.tensor_copy(out=identb[:], in_=ident[:]), reads=['ident'], writes=['identb'])
                S.op('pool', lambda e: e.memset(onesb[:], 1.0), writes=['onesb'])
                S.op('pool', lambda e: e.memset(onesf[:], 1.0), writes=['onesf'])
                S.op('pool', lambda e: e.memset(mhalf[:], -0.5), writes=['mhalf'])

                for t_ in ('Wb_ein', 'Wb_oinu', 'Wb_oinv', 'Wb_fin0', 'Wb_fin1', 'Wb_eout', 'Wb_oout', 'Wb_fout0', 'Wb_fout1'):
                    S.nodrain.add('D_' + t_)
                emit_casts(cast0, 10 ** 6)

                stg = sb("stg", [128, 128])
                pcol = PS[7]
                ncl = [0]

                def col_load(src2d, T, dst, dkey, post=None, rkey=None):
                    i = ncl[0]; ncl[0] += 1
                    S.op('sp', lambda e: e.dma_start(out=stg[0:T, :], in_=src2d), reads=([rkey] if rkey else []), writes=['stg'], dma='stg')
                    S.op('pe', lambda e: e.transpose(pcol[:, 0:T], stg[0:T, :], ident[0:T, 0:T]), reads=['stg', 'ident'], writes=[PK[7]])
                    S.op('dve', lambda e: e.tensor_copy(out=dst, in_=pcol[:, 0:T]), reads=[PK[7]], writes=[dkey])

                r128 = lambda ap1d: ap1d.rearrange("(t p) -> t p", p=128)
                ngc = sb("ngc", [128, 4, 16])
                for l in range(2):
                    for w in range(2):
                        col_load(r128(P['norm_g'][l, w]), 16, ngc[:, l * 2 + w, :], 'ngc')
                col_load(r128(P['final_g']), 16, fgc[:], 'fgc')
                adb = sb("adb", [128, 2, 96])
                for l in range(2):
                    col_load(r128(P['ada_b'][l]), 96, adb[:, l, :], 'adb')
                for k in range(31):
                    col_load(r128(P['e_conv_w'][0, k]), 12, cwc[:, k, :], 'cwc')
                col_load(r128(P['e_conv_b'][0]), 12, cbc[:], 'cbc')
                col_load(r128(P['e_ln_g'][0]), 12, lgc[:], 'lgc')
                col_load(r128(P['e_ln_b'][0]), 12, lbc[:], 'lbc')
                for l in range(2):
                    for t in range(9):
                        col_load(r128(P['f_conv_w'][l, t // 3, t % 3]), 64, fcw[:, l, t, :], 'fcw')
                    col_load(r128(P['f_conv_b'][l]), 64, fcb[:, l, :], 'fcb')
                col_load(r128(P['s5_glu_b'][0]), 4, glub[:], 'glub')
                col_load(r128(P['o_ln_g'][0]), 16, olg[:], 'olg')
                col_load(r128(P['o_ln_b'][0]), 16, olb[:], 'olb')
                craw = sb("craw", [128, 2, 16])
                for j in range(2):
                    col_load(r128(cvec[j]), 16, craw[:, j, :], 'craw')
                sc = sb("sc", [128, 16, 2])
                S.op('act', lambda e: e.activation(out=sc[:].rearrange("p k j -> p j k"), in_=craw[:], func=AF.Silu), reads=['craw'], writes=['sc'])

                wst = [sb(f"wst{i}", [128, 16, 512]) for i in range(2)]
                mrow = [sb(f"mrow{i}", [2, 512]) for i in range(2)]
                pm = PS[5]
                nw = 0
                for l in range(2):
                    wv_ = P['ada_w'][l].rearrange("(kt p) c -> p kt c", p=128)
                    for j in range(24):
                        b = nw % 2; nw += 1
                        wk = f"wst{b}"
                        for hf in range(2):
                            S.op('sp', lambda e, b=b, j=j, hf=hf, wv_=wv_: e.dma_start(out=wst[b][:, hf * 8:(hf + 1) * 8, :], in_=wv_[:, hf * 8:(hf + 1) * 8, j * 512:(j + 1) * 512]), writes=[wk], dma=wk)
                        for kt in range(16):
                            S.op('pe', lambda e, b=b, kt=kt: e.matmul(pm[0:2, :], lhsT=sc[:, kt, :], rhs=wst[b][:, kt, :], start=(kt == 0), stop=(kt == 15)), reads=[wk, 'sc'], writes=[PK[5]])
                        S.op('act', lambda e, b=b: e.copy(out=mrow[b][:], in_=pm[0:2, :]), reads=[PK[5]], writes=[f"mrow{b}"])
                        S.op('act', lambda e, b=b, l=l, j=j: e.dma_start(out=moddram[l, :, j * 512:(j + 1) * 512], in_=mrow[b][:]), reads=[f"mrow{b}"], writes=['moddram'], dma='st_mod')
                macc = sb("macc", [128, 2, 96]); maccx = sb("maccx", [128, 32])
                for l in range(2):
                    col_load(r128(moddram[l, 0]), 96, macc[:, l, :], 'macc', rkey='moddram')
                col_load(r128(moddram[0, 1][0:4096]), 32, maccx[:], 'maccx', rkey='moddram')
                S.op('dve', lambda e: e.tensor_tensor(out=modc[:], in0=macc[:], in1=adb[:], op=ALU.add), reads=['macc', 'adb'], writes=['modc'])
                S.op('dve', lambda e: e.tensor_tensor(out=modx[:], in0=maccx[:], in1=adb[:, 0, 0:32], op=ALU.add), reads=['maccx', 'adb'], writes=['modx'])
                for l in range(2):
                    S.op('dve', lambda e, l=l: e.scalar_tensor_tensor(out=A1[:, l, :], in0=modc[:, l, 16:32], scalar=1.0, in1=ngc[:, l * 2, :], op0=ALU.add, op1=ALU.mult), reads=['modc', 'ngc'], writes=['A1'])
                    S.op('dve', lambda e, l=l: e.scalar_tensor_tensor(out=A2[:, l, :], in0=modc[:, l, 64:80], scalar=1.0, in1=ngc[:, l * 2 + 1, :], op0=ALU.add, op1=ALU.mult), reads=['modc', 'ngc'], writes=['A2'])
                S.op('dve', lambda e: e.scalar_tensor_tensor(out=Ac[:], in0=modx[:, 16:32], scalar=1.0, in1=ngc[:, 0, :], op0=ALU.add, op1=ALU.mult), reads=['modx', 'ngc'], writes=['Ac'])

                emit_stage()

        def norm_pre(es_bufs, src, tok0, blk, use_pow=False):
            xt, st, mvv, rs = es_bufs
            b = blk % 2
            xk = f"xt{b}"
            r0 = tok0 + blk * 128
            S.op('sp', lambda e, b=b, r0=r0: e.dma_start(out=xt[b][:], in_=src[r0:r0 + 128, :]), writes=[xk], dma=xk)
            for q4 in range(4):
                S.op('dve', lambda e, b=b, q4=q4: e.bn_stats(out=st[b][:, q4 * 6:(q4 + 1) * 6], in_=xt[b][:, q4 * 512:(q4 + 1) * 512]), reads=[xk], writes=[f"st{b}"])
            S.op('dve', lambda e, b=b: e.bn_aggr(out=mvv[b][:], in_=st[b][:]), reads=[f"st{b}"], writes=[f"mv{b}"])
            S.op('dve', lambda e, b=b: e.scalar_tensor_tensor(out=rs[b][:, 0:1], in0=mvv[b][:, 0:1], scalar=mvv[b][:, 0:1], in1=mvv[b][:, 1:2], op0=ALU.mult, op1=ALU.add), reads=[f"mv{b}"], writes=[f"rs{b}"])
            S.op('dve', lambda e, b=b: e.tensor_scalar(out=rs[b][:, 0:1], in0=rs[b][:, 0:1], scalar1=1e-6, scalar2=None, op0=ALU.add), reads=[f"rs{b}"], writes=[f"rs{b}"])
            if use_pow:
                S.op('pool', lambda e, b=b: e.tensor_tensor(out=rs[b][:, 2:3], in0=rs[b][:, 0:1], in1=mhalf[:], op=ALU.pow), reads=[f"rs{b}", 'mhalf'], writes=[f"rs{b}"])
            else:
                S.op('act', lambda e, b=b: e.activation(out=rs[b][:, 1:2], in_=rs[b][:, 0:1], func=AF.Sqrt), reads=[f"rs{b}"], writes=[f"rs{b}"])
                S.op('dve', lambda e, b=b: e.reciprocal(out=rs[b][:, 2:3], in_=rs[b][:, 1:2]), reads=[f"rs{b}"], writes=[f"rs{b}"])
            if POOL_SCALE:
                S.op('pool', lambda e, b=b: e.tensor_scalar(out=xt[b][:], in0=xt[b][:], scalar1=rs[b][:, 2:3], scalar2=0.0, op0=ALU.mult, op1=ALU.add), reads=[xk, f"rs{b}"], writes=[xk])
            else:
                S.op('act', lambda e, b=b: e.activation(out=xt[b][:], in_=xt[b][:], func=AF.Copy, scale=rs[b][:, 2:3]), reads=[xk, f"rs{b}"], writes=[xk])

        def norm_post(es_bufs, blk, Acol, Bcol, akey, bkey, hn, hnkey, evac_eng=None):
            xt, st, mvv, rs = es_bufs
            b = blk % 2
            xk = f"xt{b}"
            for g4 in range(4):
                pb = 6 + (g4 % 2)
                for i in range(4):
                    ct = g4 * 4 + i
                    S.op('pe', lambda e, b=b, ct=ct, i=i, pb=pb: e.transpose(PS[pb][:, i * 128:(i + 1) * 128], xt[b][:, ct * 128:(ct + 1) * 128], ident[:]), reads=[xk, 'ident'], writes=[PK[pb]])
                for i in range(4):
                    ct = g4 * 4 + i
                    eng = evac_eng if evac_eng else ('dve' if blk % 2 == 0 else 'act')
                    if eng == 'dve':
                        S.op('dve', lambda e, ct=ct, i=i, pb=pb, blk=blk: e.tensor_scalar(out=hn[:, ct, blk * 128:(blk + 1) * 128], in0=PS[pb][:, i * 128:(i + 1) * 128], scalar1=Acol[:, ct:ct + 1], scalar2=Bcol[:, ct:ct + 1], op0=ALU.mult, op1=ALU.add), reads=[PK[pb], akey, bkey], writes=[hnkey])
                    else:
                        S.op('act', lambda e, ct=ct, i=i, pb=pb, blk=blk: e.activation(out=hn[:, ct, blk * 128:(blk + 1) * 128], in_=PS[pb][:, i * 128:(i + 1) * 128], func=AF.Identity, scale=Acol[:, ct:ct + 1], bias=Bcol[:, ct:ct + 1]), reads=[PK[pb], akey, bkey], writes=[hnkey])

        def norm_block(es_bufs, src, tok0, blk, Acol, Bcol, akey, bkey, hn, hnkey):
            norm_post(es_bufs, blk, Acol, Bcol, akey, bkey, hn, hnkey)
            if blk + 1 < 4:
                norm_pre(es_bufs, src, tok0, blk + 1)

        def norm_tile(es_bufs, src, tok0, Acol, Bcol, akey, bkey, hn, hnkey, nblk=4):
            norm_pre(es_bufs, src, tok0, 0)
            for blk in range(nblk):
                norm_post(es_bufs, blk, Acol, Bcol, akey, bkey, hn, hnkey)
                if blk + 1 < nblk:
                    norm_pre(es_bufs, src, tok0, blk + 1)

        def norm_bufs(es):
            xt = [es.enter_context(nc.sbuf_tensor(uname(f"xt{i}"), [128, 2048], F32)) for i in range(2)]
            st = [es.enter_context(nc.sbuf_tensor(uname(f"st{i}"), [128, 24], F32)) for i in range(2)]
            mvv = [es.enter_context(nc.sbuf_tensor(uname(f"mv{i}"), [128, 2], F32)) for i in range(2)]
            rs = [es.enter_context(nc.sbuf_tensor(uname(f"rs{i}"), [128, 4], F32)) for i in range(2)]
            return xt, st, mvv, rs

        def out_blocks(i):
            res = []
            o0 = 512 * i - LAG
            for blk in range(4):
                t0 = o0 + blk * 128
                lo = max(t0, 0); hi = min(t0 + 128, TE)
                if hi > lo:
                    res.append((blk, t0, lo - t0, hi - t0))
            return res

        grow_tmp = [None]

        def rep_row(es, colap, ckey, dst, dkey):
            dgl = [es.enter_context(nc.sbuf_tensor(uname(f"dgl{i}"), [128, 128], F32)) for i in range(2)]
            grow_tmp[0] = [es.enter_context(nc.sbuf_tensor(uname(f"gtmp{i}"), [128, 512], F32)) for i in range(2)]
            for g4 in range(4):
                for i in range(4):
                    ct = g4 * 4 + i
                    b = i % 2
                    S.op('dve', lambda e, ct=ct, b=b: e.tensor_scalar(out=dgl[b][:], in0=ident[:], scalar1=colap[:, ct:ct + 1], scalar2=None, op0=ALU.mult), reads=['ident', ckey], writes=[f"dgl{b}"])
                    S.op('pe', lambda e, b=b, i=i: e.matmul(PS[6][:, i * 128:(i + 1) * 128], lhsT=onesf[:], rhs=dgl[b][:], start=True, stop=True, skip_group_check=True), reads=['onesf', f"dgl{b}"], writes=[PK[6]])
                S.op('act', lambda e, g4=g4: e.copy(out=dst[:, g4 * 512:(g4 + 1) * 512], in_=PS[6][:]), reads=[PK[6]], writes=[dkey])

        wo_ctr = [0]

        def outproj_resid(es, mlhs, mkey, nkt, Wb, wtag, hin, hout, hokey, i, wo, hres, lagged=True, between=None, grow=None):
            blocks = out_blocks(i) if lagged else [(b, 512 * i + b * 128, 0, 128) for b in range(4)]
            for (blk, t0, lo, hi) in blocks:
                S.op('sp', lambda e, blk=blk, t0=t0, lo=lo, hi=hi: e.dma_start(out=hres[blk][lo:hi, :], in_=hin[t0 + lo:t0 + hi, :]), writes=[f"hres{blk}"], dma=f"hres{blk}")
            nch = nkt // 8
            for cg in range(4):
                bufs = []
                for ch in range(nch):
                    b = wo_ctr[0] % 4; wo_ctr[0] += 1
                    bufs.append(b)
                    S.op('sp', lambda e, b=b, cg=cg, ch=ch: e.dma_start(out=wo[b][:], in_=Wb[ch * 8:ch * 8 + 8, :, cg * 512:(cg + 1) * 512].rearrange("k p c -> p k c")), reads=[wtag], writes=[f"wo{b}"], dma=f"wo{b}")
                for (blk, t0, lo, hi) in blocks:
                    pb = blk
                    for kt in range(nkt):
                        b = bufs[kt // 8]
                        S.op('pe', lambda e, kt=kt, blk=blk, b=b, pb=pb: e.matmul(PS[pb][:], lhsT=mlhs(kt, blk), rhs=wo[b][:, kt % 8, :], start=(kt == 0), stop=(kt == nkt - 1)), reads=[mkey, f"wo{b}"], writes=[PK[pb]])
                    tb = blk % 2
                    S.op('dve', lambda e, cg=cg, pb=pb, tb=tb: e.tensor_tensor(out=grow_tmp[0][tb][:], in0=PS[pb][:], in1=grow[:, cg * 512:(cg + 1) * 512], op=ALU.mult), reads=[PK[pb], 'grow'], writes=[f"gtmp{tb}"])
                    S.op('pool', lambda e, blk=blk, cg=cg, tb=tb: e.tensor_tensor(out=hres[blk][:, cg * 512:(cg + 1) * 512], in0=grow_tmp[0][tb][:], in1=hres[blk][:, cg * 512:(cg + 1) * 512], op=ALU.add), reads=[f"gtmp{tb}", f"hres{blk}"], writes=[f"hres{blk}"])
                if between is not None:
                    between(cg)
            for (blk, t0, lo, hi) in blocks:
                S.op('act', lambda e, blk=blk, t0=t0, lo=lo, hi=hi: e.dma_start(out=hout[t0 + lo:t0 + hi, :], in_=hres[blk][lo:hi, :]), reads=[f"hres{blk}"], writes=[hokey], dma=f"st_{hokey}")

        def ffn_stage(l, hin, hinkey, hout, hokey):
            with ExitStack() as es:
                def sb(name, shape, dt=F32):
                    return es.enter_context(nc.sbuf_tensor(uname(name), list(shape), dt))
                nb = norm_bufs(es)
                hn1 = sb("hn0", [128, 16, 512], BF16)
                hn = [hn1, hn1]
                win = [sb(f"win{i}", [128, 16, 128], BF16) for i in range(5)]
                zt = [sb(f"zt{i}", [128, 10, 66], BF16) for i in range(4)]
                zh = sb("zh", [128, 64, 2, 66], BF16)
                dgf = [sb(f"dgf{i}", [128, 9, 128], BF16) for i in range(4)]
                sg = [sb(f"sg{i}", [128, 512]) for i in range(2)]
                m = sb("m", [128, 32, 512], BF16)
                wo = [sb(f"wo{b}", [128, 8, 512], BF16) for b in range(4)]
                hres = [sb(f"hres{b}", [128, 2048]) for b in range(4)]
                grow = sb("grow", [128, 2048])
                rep_row(es, modc[:, l, 80:96], 'modc', grow, 'grow')
                for i in range(4):
                    S.op('pool', lambda e, i=i: e.memset(zt[i][:], 0.0), writes=[f"zt{i}"])
                S.op('pool', lambda e: e.memset(zh[:], 0.0), writes=['zh'])
                nwin = [0]
                NA, NBk = A2[:, l, :], modc[:, l, 48:64]
                norm_tile(nb, hin, 0, NA, NBk, 'A2', 'modc', hn[0], "hn0")

                def inproj(i, cp):
                    has_in = i < NT
                    for half in range(2):
                        c = cp + 32 * half
                        b = (2 * cp + half) % 4
                        zk = f"zt{b}"
                        S.op('pool', lambda e, b=b, c=c: e.tensor_copy(out=zt[b][:, 0:2, :], in_=zh[:, c, :, :]), reads=['zh'], writes=[zk])
                        if has_in:
                            wb_ = nwin[0] % 5; nwin[0] += 1
                            wk = f"win{wb_}"
                            S.op('sp', lambda e, c=c, wb_=wb_: e.dma_start(out=win[wb_][:], in_=Wb_fin[l][c]), reads=[f'Wb_fin{l}'], writes=[wk], dma=wk)
                            pb = 2 * (cp % 2) + half
                            for kt in range(16):
                                S.op('pe', lambda e, kt=kt, wb_=wb_, pb=pb: e.matmul(PS[pb][:], lhsT=win[wb_][:, kt, :], rhs=hn[0][:, kt, :], start=(kt == 0), stop=(kt == 15)), reads=[wk, "hn0"], writes=[PK[pb]])
                            S.op('act', lambda e, b=b, pb=pb: e.copy(out=zt[b][:, 2:10, 1:65], in_=PS[pb][:].rearrange("p (r c) -> p r c", c=64)), reads=[PK[pb]], writes=[zk])
                        else:
                            S.op('pool', lambda e, b=b: e.memset(zt[b][:, 2:10, :], 0.0), writes=[zk])
                        S.op('pool', lambda e, b=b, c=c: e.tensor_copy(out=zh[:, c, :, :], in_=zt[b][:, 8:10, :]), reads=[zk], writes=['zh'])
                        for t in range(9):
                            S.op('dve', lambda e, b=b, c=c, t=t: e.tensor_scalar(out=dgf[b][:, t, :], in0=identb[:], scalar1=fcw[:, l, t, c:c + 1], scalar2=None, op0=ALU.mult), reads=['identb', 'fcw'], writes=[f"dgf{b}"])

                def conv(i, cp):
                    for half in range(2):
                        b = (2 * cp + half) % 4
                        zk = f"zt{b}"
                        pc = 4 + half
                        ntap = 9 if i < NT else 6
                        for t in range(ntap):
                            dr, dc = t // 3, t % 3
                            S.op('pe', lambda e, b=b, t=t, dr=dr, dc=dc, pc=pc, ntap=ntap: e.matmul(PS[pc][:], lhsT=dgf[b][:, t, :], rhs=zt[b][:, dr:dr + 8, dc:dc + 64], start=(t == 0), stop=(t == ntap - 1)), reads=[f"dgf{b}", zk], writes=[PK[pc]])
                    sb_ = cp % 2
                    S.op('act', lambda e, cp=cp, sb_=sb_: e.activation(out=sg[sb_][:], in_=PS[5][:], func=AF.Silu, bias=fcb[:, l, 32 + cp:33 + cp]), reads=[PK[5], 'fcb'], writes=[f"sg{sb_}"])
                    S.op('dve', lambda e, cp=cp, sb_=sb_: e.scalar_tensor_tensor(out=m[:, cp, :], in0=PS[4][:], scalar=fcb[:, l, cp:cp + 1], in1=sg[sb_][:], op0=ALU.add, op1=ALU.mult), reads=[PK[4], 'fcb', f"sg{sb_}"], writes=['m'])

                for i in range(NT + 1):
                    inproj(i, 0)
                    for cp in range(32):
                        if cp + 1 < 32:
                            inproj(i, cp + 1)
                        conv(i, cp)
                        if l == 0 and cp % 3 == 2:
                            emit_casts(castC, 1)
                    btw = None
                    if i + 1 < NT:
                        norm_pre(nb, hin, 512 * (i + 1), 0)
                        btw = lambda cg, i=i: norm_block(nb, hin, 512 * (i + 1), cg, NA, NBk, 'A2', 'modc', hn[0], "hn0")
                    outproj_resid(es, lambda kt, blk: m[:, kt, blk * 128:(blk + 1) * 128], 'm', 32, Wb_fout[l], f'Wb_fout{l}', hin, hout, hokey, i, wo, hres, between=btw, grow=grow)
                if l == 0:
                    emit_casts(castC, 10 ** 6)
                emit_stage()

        def l1mix_stage(hin, hout, hokey):
            with ExitStack() as es:
                def sb(name, shape, dt=F32):
                    return es.enter_context(nc.sbuf_tensor(uname(name), list(shape), dt))
                nb = norm_bufs(es)
                hn1 = sb("hn0", [128, 16, 512], BF16)
                hn = [hn1, hn1]
                wu = [sb(f"win{i}", [128, 16, 128], BF16) for i in range(3)]
                wo = [sb(f"wo{b}", [128, 8, 512], BF16) for b in range(4)]
                uv = sb("uv", [128, 16, 512])
                vn = sb("vn", [128, 4, 2048], BF16)
                m = sb("m", [128, 16, 512], BF16)
                tmp = [sb(f"tmp{i}", [128, 128]) for i in range(2)]
                st = sb("lst", [128, 4, 24]); mv = sb("lmv", [128, 4, 2]); rsd = sb("lrs", [128, 4, 2])
                hres = [sb(f"hres{b}", [128, 2048]) for b in range(4)]
                uvv = uv[:].rearrange("p (cb four) c -> p cb (four c)", four=4)
                grow = sb("grow", [128, 2048])
                rep_row(es, modc[:, 1, 32:48], 'modc', grow, 'grow')
                sguT = sb("sguT", [128, 8, 128], BF16)
                Cst = sb("Cst", [128, 16, 128])
                stg = sb("stg", [128, 128])
                pcol = PS[7]
                for h in range(8):
                    S.op('sp', lambda e, h=h: e.dma_start(out=stg[:], in_=P['o_sgu_w'][0, h]), writes=['stg'], dma='stg')
                    S.op('pe', lambda e: e.transpose(pcol[:, 0:128], stg[:], ident[:]), reads=['stg', 'ident'], writes=[PK[7]])
                    S.op('dve', lambda e, h=h: e.tensor_copy(out=sguT[:, h, :], in_=pcol[:, 0:128]), reads=[PK[7]], writes=['sguT'])
                rw = uv[:, 0:2, :].rearrange("p a c -> p (a c)").rearrange("p (h q) -> p h q", q=128)
                for h in range(8):
                    S.op('pe', lambda e, h=h: e.matmul(PS[6][:, 0:128], lhsT=onesb[:], rhs=sguT[:, h, :], start=True, stop=True), reads=['onesb', 'sguT'], writes=[PK[6]])
                    S.op('dve', lambda e, h=h: e.tensor_copy(out=rw[:, h, :], in_=PS[6][:, 0:128]), reads=[PK[6]], writes=['uv0'])
                sgb = uv[:, 2:4, :].rearrange("p a c -> p (a c)").rearrange("p (h q) -> p h q", q=128)
                S.op('sp', lambda e: e.dma_start(out=uv[:, 2:4, :].rearrange("p a c -> p (a c)"), in_=P['o_sgu_b'][0].rearrange("h q -> (h q)").partition_broadcast(128)), writes=['uv0'], dma='sgb')
                for ct in range(16):
                    h = ct // 2
                    S.op('dve', lambda e, ct=ct, h=h: e.scalar_tensor_tensor(out=Cst[:, ct, :], in0=rw[:, h, :], scalar=olb[:, ct:ct + 1], in1=sgb[:, h, :], op0=ALU.mult, op1=ALU.add), reads=['uv0', 'olb'], writes=['Cst'])

                nwu = [0]
                norm_tile(nb, hin, 0, A1[:, 1, :], modc[:, 1, 0:16], 'A1', 'modc', hn[0], "hn0")
                for i in range(NT):
                    hb = 0
                    for cg in range(4):
                        bufs = []
                        for ch in range(2):
                            b = wo_ctr[0] % 4; wo_ctr[0] += 1
                            bufs.append(b)
                            S.op('sp', lambda e, b=b, cg=cg, ch=ch: e.dma_start(out=wo[b][:], in_=Wb_oinv[cg][:, ch * 8:ch * 8 + 8, :]), reads=['Wb_oinv'], writes=[f"wo{b}"], dma=f"wo{b}")
                        for cb in range(4):
                            pb = cb
                            for kt in range(16):
                                b = bufs[kt // 8]
                                S.op('pe', lambda e, kt=kt, cb=cb, b=b, pb=pb: e.matmul(PS[pb][:], lhsT=hn[hb][:, kt, cb * 128:(cb + 1) * 128], rhs=wo[b][:, kt % 8, :], start=(kt == 0), stop=(kt == 15)), reads=[f"hn{hb}", f"wo{b}"], writes=[PK[pb]])
                            S.op('act', lambda e, cb=cb, cg=cg, pb=pb: e.activation(out=uvv[:, cb, cg * 512:(cg + 1) * 512], in_=PS[pb][:], func=AF.Gelu_apprx_tanh), reads=[PK[pb]], writes=[f"uv{cb}"])
                    for cb in range(4):
                        for q4 in range(4):
                            S.op('dve', lambda e, cb=cb, q4=q4: e.bn_stats(out=st[:, cb, q4 * 6:(q4 + 1) * 6], in_=uvv[:, cb, q4 * 512:(q4 + 1) * 512]), reads=[f"uv{cb}"], writes=['lst'])
                        S.op('dve', lambda e, cb=cb: e.bn_aggr(out=mv[:, cb, :], in_=st[:, cb, :]), reads=['lst'], writes=['lmv'])
                        S.op('dve', lambda e, cb=cb: e.tensor_scalar(out=rsd[:, cb, 0:1], in0=mv[:, cb, 1:2], scalar1=1e-5, scalar2=None, op0=ALU.add), reads=['lmv'], writes=['lrs'])
                        S.op('act', lambda e, cb=cb: e.activation(out=rsd[:, cb, 1:2], in_=rsd[:, cb, 0:1], func=AF.Sqrt), reads=['lrs'], writes=['lrs'])
                        S.op('dve', lambda e, cb=cb: e.reciprocal(out=rsd[:, cb, 0:1], in_=rsd[:, cb, 1:2]), reads=['lrs'], writes=['lrs'])
                        S.op('dve', lambda e, cb=cb: e.tensor_scalar(out=vn[:, cb, :], in0=uvv[:, cb, :], scalar1=mv[:, cb, 0:1], scalar2=rsd[:, cb, 0:1], op0=ALU.subtract, op1=ALU.mult), reads=[f"uv{cb}", 'lmv', 'lrs'], writes=['vn'])
                    def sgu(ct):
                        h = ct // 2
                        for cb in range(4):
                            pb = 4 + (cb % 2)
                            tb = cb % 2
                            S.op('pe', lambda e, cb=cb, ct=ct, h=h, pb=pb: e.matmul(PS[pb][:, 0:128], lhsT=vn[:, cb, ct * 128:(ct + 1) * 128], rhs=sguT[:, h, :], start=True, stop=True), reads=['vn', 'sguT'], writes=[PK[pb]])
                            S.op('dve', lambda e, ct=ct, pb=pb, tb=tb: e.scalar_tensor_tensor(out=tmp[tb][:], in0=PS[pb][:, 0:128], scalar=olg[:, ct:ct + 1], in1=Cst[:, ct, :], op0=ALU.mult, op1=ALU.add), reads=[PK[pb], 'olg', 'Cst'], writes=[f"tmp{tb}"])
                            S.op('dve', lambda e, cb=cb, ct=ct, tb=tb: e.tensor_tensor(out=m[:, ct, cb * 128:(cb + 1) * 128], in0=tmp[tb][:], in1=uv[:, ct, cb * 128:(cb + 1) * 128], op=ALU.mult), reads=[f"tmp{tb}", f"uv{ct // 4}"], writes=['m'])
                    for ot in range(16):
                        wb_ = nwu[0] % 3; nwu[0] += 1
                        wk = f"win{wb_}"
                        S.op('sp', lambda e, ot=ot, wb_=wb_: e.dma_start(out=wu[wb_][:], in_=Wb_oinu[ot]), reads=['Wb_oinu'], writes=[wk], dma=wk)
                        pb = ot % 4
                        for kt in range(16):
                            S.op('pe', lambda e, kt=kt, wb_=wb_, pb=pb: e.matmul(PS[pb][:], lhsT=wu[wb_][:, kt, :], rhs=hn[hb][:, kt, :], start=(kt == 0), stop=(kt == 15)), reads=[wk, f"hn{hb}"], writes=[PK[pb]])
                        S.op('act', lambda e, ot=ot, pb=pb: e.activation(out=uv[:, ot, :], in_=PS[pb][:], func=AF.Gelu_apprx_tanh), reads=[PK[pb]], writes=[f"uv{ot // 4}"])
                        if ot >= 1:
                            sgu(ot - 1)
                    sgu(15)
                    btw = None
                    if i + 1 < NT:
                        norm_pre(nb, hin, 512 * (i + 1), 0)
                        btw = lambda cg, i=i: norm_block(nb, hin, 512 * (i + 1), cg, A1[:, 1, :], modc[:, 1, 0:16], 'A1', 'modc', hn[0], "hn0")
                    outproj_resid(es, lambda kt, blk: m[:, kt, blk * 128:(blk + 1) * 128], 'm', 16, Wb_oout, 'Wb_oout', hin, hout, hokey, i, wo, hres, lagged=False, between=btw, grow=grow)
                emit_stage()

        def final_stage(hin, houtap):
            with ExitStack() as es:
                xt, st, mvv, rs = norm_bufs(es)
                fgrep = es.enter_context(nc.sbuf_tensor(uname("fgrep"), [128, 2048], F32))
                rep_row(es, fgc[:], 'fgc', fgrep, 'fgrep')
                for blk in range(TE // 128):
                    b = blk % 2
                    xk = f"xt{b}"
                    r0 = blk * 128
                    S.op('sp', lambda e, b=b, r0=r0: e.dma_start(out=xt[b][:], in_=hin[r0:r0 + 128, :]), writes=[xk], dma=xk)
                    for q4 in range(4):
                        S.op('dve', lambda e, b=b, q4=q4: e.bn_stats(out=st[b][:, q4 * 6:(q4 + 1) * 6], in_=xt[b][:, q4 * 512:(q4 + 1) * 512]), reads=[xk], writes=[f"st{b}"])
                    S.op('dve', lambda e, b=b: e.bn_aggr(out=mvv[b][:], in_=st[b][:]), reads=[f"st{b}"], writes=[f"mv{b}"])
                    S.op('dve', lambda e, b=b: e.scalar_tensor_tensor(out=rs[b][:, 0:1], in0=mvv[b][:, 0:1], scalar=mvv[b][:, 0:1], in1=mvv[b][:, 1:2], op0=ALU.mult, op1=ALU.add), reads=[f"mv{b}"], writes=[f"rs{b}"])
                    S.op('dve', lambda e, b=b: e.tensor_scalar(out=rs[b][:, 0:1], in0=rs[b][:, 0:1], scalar1=1e-6, scalar2=None, op0=ALU.add), reads=[f"rs{b}"], writes=[f"rs{b}"])
                    S.op('act', lambda e, b=b: e.activation(out=rs[b][:, 1:2], in_=rs[b][:, 0:1], func=AF.Sqrt), reads=[f"rs{b}"], writes=[f"rs{b}"])
                    S.op('dve', lambda e, b=b: e.reciprocal(out=rs[b][:, 2:3], in_=rs[b][:, 1:2]), reads=[f"rs{b}"], writes=[f"rs{b}"])
                    S.op('dve', lambda e, b=b: e.scalar_tensor_tensor(out=xt[b][:], in0=xt[b][:], scalar=rs[b][:, 2:3], in1=fgrep[:], op0=ALU.mult, op1=ALU.mult), reads=[xk, f"rs{b}", 'fgrep'], writes=[xk])
                    S.op('act', lambda e, b=b, r0=r0: e.dma_start(out=houtap[r0:r0 + 128, :], in_=xt[b][:]), reads=[xk], writes=['out'], dma='st_out')
                emit_stage()


        def l0mix_stage(hin, hout, hokey, use_s5=True):
            with ExitStack() as es:
                def sb(name, shape, dt=F32):
                    return es.enter_context(nc.sbuf_tensor(uname(name), list(shape), dt))
                nb = norm_bufs(es)
                hn1 = sb("hn0", [128, 16, 512], BF16)
                win = [sb(f"win{i}", [128, 16, 128], BF16) for i in range(5)]
                zc = [sb(f"zc{i}", [128, 608], BF16) for i in range(12)]
                dgc = [sb(f"dgc{i}", [128, 31, 128], BF16) for i in range(2)]
                sgm = [sb(f"sgm{i}", [128, 512]) for i in range(2)]
                ysq = [sb(f"ysq{i}", [128, 512]) for i in range(2)]
                mean_t = sb("mean_t", [128, 512]); rstd_t = sb("rstd_t", [128, 512]); var_t = sb("var_t", [128, 512])
                sg4 = [sgm[0], sgm[1], ysq[0], ysq[1]]
                sg4k = ["sgm0", "sgm1", "ysq0", "ysq1"]
                s5_late = []
                ymix = sb("ymix", [128, 16, 512], BF16)
                ygb = sb("ygb", [128, 4, 512], BF16)
                wo = [sb(f"wo{b}", [128, 8, 512], BF16) for b in range(4)]
                hres = [sb(f"hres{b}", [128, 2048]) for b in range(4)]
                ycv = lambda c: hres[c // 4][:, (c % 4) * 512:(c % 4 + 1) * 512]
                ycvk = lambda c: f"hres{c // 4}"
                yraw = hres[3][:].rearrange("p (g c) -> p g c", c=512)
                grow = sb("grow", [128, 2048])
                rep_row(es, modc[:, 0, 32:48], 'modc', grow, 'grow')
                glw = sb("glw", [128, 4, 512], BF16)
                S.op('pool', lambda e: e.dma_start(out=glw[:], in_=P['s5_glu_w'][0].rearrange("(kt p) c -> p kt c", p=128)), writes=['glw'], dma='glw')
                for c in range(12):
                    S.op('pool', lambda e, c=c: e.memset(zc[c][:], 0.0), writes=[f"zc{c}"])
                if not use_s5:
                    S.op('pool', lambda e: e.memset(ymix[:, 12:16, :], 0.0), writes=['ymix'])
                nwin = [0]
                norm_tile(nb, hin, 0, A1[:, 0, :], modc[:, 0, 0:16], 'A1', 'modc', hn1, "hn0")
                def l0_inproj(i, cp):
                    has_in = i < NT
                    zk = f"zc{cp}"
                    if has_in:
                        for half in range(2):
                            c = cp + 12 * half
                            wb_ = nwin[0] % 5; nwin[0] += 1
                            wk = f"win{wb_}"
                            pb = 2 * (cp % 2) + half
                            S.op('sp', lambda e, c=c, wb_=wb_: e.dma_start(out=win[wb_][:], in_=Wb_ein[c]), reads=['Wb_ein'], writes=[wk], dma=wk)
                            for kt in range(16):
                                S.op('pe', lambda e, kt=kt, wb_=wb_, pb=pb: e.matmul(PS[pb][:], lhsT=win[wb_][:, kt, :], rhs=hn1[:, kt, :], start=(kt == 0), stop=(kt == 15)), reads=[wk, "hn0"], writes=[PK[pb]])
                        sb_ = cp % 2
                        pa, pg = 2 * (cp % 2), 2 * (cp % 2) + 1
                        S.op('act', lambda e, sb_=sb_, pg=pg: e.activation(out=sgm[sb_][:], in_=PS[pg][:], func=AF.Sigmoid), reads=[PK[pg]], writes=[f"sgm{sb_}"])
                        S.op('dve', lambda e, cp=cp, sb_=sb_, pa=pa: e.tensor_tensor(out=zc[cp][:, 96:608], in0=PS[pa][:], in1=sgm[sb_][:], op=ALU.mult), reads=[PK[pa], f"sgm{sb_}"], writes=[zk])
                    else:
                        S.op('pool', lambda e, cp=cp: e.memset(zc[cp][:, 96:608], 0.0), writes=[zk])
                    db = cp % 2
                    for k in range(31):
                        eng = 'dve' if k % 2 == 0 else 'pool'
                        S.op(eng, lambda e, db=db, k=k, cp=cp: e.tensor_scalar(out=dgc[db][:, k, :], in0=identb[:], scalar1=cwc[:, k, cp:cp + 1], scalar2=0.0, op0=ALU.mult, op1=ALU.add), reads=['identb', 'cwc'], writes=[f"dgc{db}_{k % 2}"])

                def l0_conv(i, cp):
                    zk = f"zc{cp}"
                    db = cp % 2
                    pc = 4 + (cp % 2)
                    for k in range(31):
                        S.op('pe', lambda e, db=db, k=k, cp=cp, pc=pc: e.matmul(PS[pc][:], lhsT=dgc[db][:, k, :], rhs=zc[cp][:, 17 + k:17 + k + 512], start=(k == 0), stop=(k == 30)), reads=[f"dgc{db}_0", f"dgc{db}_1", zk], writes=[PK[pc]])
                    S.op('pool', lambda e, cp=cp: e.tensor_copy(out=zc[cp][:, 0:96], in_=zc[cp][:, 512:608]), reads=[zk], writes=[zk])
                    qb = cp % 2
                    S.op('act', lambda e, cp=cp, pc=pc: e.activation(out=ycv(cp), in_=PS[pc][:], func=AF.Identity, bias=cbc[:, cp:cp + 1]), reads=[PK[pc], 'cbc'], writes=[ycvk(cp)])
                    S.op('act', lambda e, cp=cp, pc=pc, qb=qb: e.activation(out=ysq[qb][:], in_=PS[pc][:], func=AF.Square, bias=cbc[:, cp:cp + 1]), reads=[PK[pc], 'cbc'], writes=[f"ysq{qb}"])
                    S.op('pe', lambda e, cp=cp: e.matmul(PS[6][:], lhsT=onesf[:], rhs=ycv(cp), start=(cp == 0), stop=(cp == 11)), reads=['onesf', ycvk(cp)], writes=[PK[6]])
                    S.op('pe', lambda e, cp=cp, qb=qb: e.matmul(PS[7][:], lhsT=onesf[:], rhs=ysq[qb][:], start=(cp == 0), stop=(cp == 11)), reads=['onesf', f"ysq{qb}"], writes=[PK[7]])

                for i in range(NT + 1):
                    has_in = i < NT
                    l0_inproj(i, 0)
                    for cp in range(12):
                        if cp + 1 < 12:
                            l0_inproj(i, cp + 1)
                        l0_conv(i, cp)
                        if cp % 3 == 2:
                            emit_casts(castB, 1)
                    if use_s5:
                        for gt in range(4):
                            S.op('sp', lambda e, gt=gt, i=i: e.dma_start(out=yraw[:, gt, :].rearrange("p (t j) -> p t j", j=64), in_=Ys[gt * 128:(gt + 1) * 128, :, 64 * i:64 * i + 64]), reads=['Ys'], writes=['hres3'], dma='hres3')
                        S.op('act', lambda e: e.activation(out=hres[3][:], in_=hres[3][:], func=AF.Gelu_apprx_tanh), reads=['hres3'], writes=['hres3'])
                        yrv = lambda gt: yraw[:, gt, :].rearrange("p (t j) -> p j t", j=64)
                        for gt in range(4):
                            S.op('pool', lambda e, gt=gt: e.tensor_copy(out=ygb[:, gt, :].rearrange("p (j t) -> p j t", t=8), in_=yrv(gt)), reads=['hres3'], writes=['ygb'])
                        for ot in range(4):
                            pc = 2 + (ot % 2)
                            for kt in range(4):
                                S.op('pe', lambda e, ot=ot, kt=kt, pc=pc: e.matmul(PS[pc][:], lhsT=glw[:, kt, ot * 128:(ot + 1) * 128], rhs=ygb[:, kt, :], start=(kt == 0), stop=(kt == 3)), reads=['glw', 'ygb'], writes=[PK[pc]])
                            sb_ = ot % 2
                            S.op('act', lambda e, ot=ot, pc=pc, sb_=sb_: e.activation(out=sg4[ot][:], in_=PS[pc][:], func=AF.Sigmoid, bias=glub[:, ot:ot + 1]), reads=[PK[pc], 'glub'], writes=[sg4k[ot]])
                            s5_late.append(((lambda e, ot=ot, sb_=sb_: e.tensor_tensor(out=ymix[:, 12 + ot, :].rearrange("p (j t) -> p j t", t=8), in0=yrv(ot), in1=sg4[ot][:].rearrange("p (j t) -> p j t", t=8), op=ALU.mult)), ['hres3', sg4k[ot]], ['ymix']))
                    S.op('dve', lambda e: e.tensor_scalar(out=mean_t[:], in0=PS[6][:], scalar1=1.0 / 1536, scalar2=None, op0=ALU.mult), reads=[PK[6]], writes=['mean_t'])
                    S.op('dve', lambda e: e.tensor_tensor(out=var_t[:], in0=mean_t[:], in1=mean_t[:], op=ALU.mult), reads=['mean_t'], writes=['var_t'])
                    S.op('dve', lambda e: e.scalar_tensor_tensor(out=var_t[:], in0=PS[7][:], scalar=1.0 / 1536, in1=var_t[:], op0=ALU.mult, op1=ALU.subtract), reads=[PK[7], 'var_t'], writes=['var_t'])
                    S.op('dve', lambda e: e.tensor_scalar(out=var_t[:], in0=var_t[:], scalar1=1e-5, scalar2=None, op0=ALU.add), reads=['var_t'], writes=['var_t'])
                    S.op('act', lambda e: e.activation(out=var_t[:], in_=var_t[:], func=AF.Sqrt), reads=['var_t'], writes=['var_t'])
                    S.op('dve', lambda e: e.reciprocal(out=rstd_t[:], in_=var_t[:]), reads=['var_t'], writes=['rstd_t'])
                    for cp in range(12):
                        S.op('pool' if cp % 2 == 1 else 'dve', lambda e, cp=cp: e.tensor_tensor(out=ycv(cp), in0=ycv(cp), in1=mean_t[:], op=ALU.subtract), reads=[ycvk(cp), 'mean_t'], writes=[ycvk(cp)])
                        S.op('dve', lambda e, cp=cp: e.tensor_tensor(out=ycv(cp), in0=ycv(cp), in1=rstd_t[:], op=ALU.mult), reads=[ycvk(cp), 'rstd_t'], writes=[ycvk(cp)])
                        S.op('act', lambda e, cp=cp: e.activation(out=ymix[:, cp, :], in_=ycv(cp), func=AF.Silu, scale=lgc[:, cp:cp + 1], bias=lbc[:, cp:cp + 1]), reads=[ycvk(cp), 'lgc', 'lbc'], writes=['ymix'])
                    while s5_late:
                        fn_, rk_, wk_ = s5_late.pop(0)
                        S.op('pool', fn_, reads=rk_, writes=wk_)
                    btw = None
                    if i + 1 < NT:
                        norm_pre(nb, hin, 512 * (i + 1), 0)
                        btw = lambda cg, i=i: norm_block(nb, hin, 512 * (i + 1), cg, A1[:, 0, :], modc[:, 0, 0:16], 'A1', 'modc', hn1, "hn0")
                    outproj_resid(es, lambda kt, blk: ymix[:, kt, blk * 128:(blk + 1) * 128], 'ymix', 16, Wb_eout, 'Wb_eout', hin, hout, hokey, i, wo, hres, between=btw, grow=grow)
                emit_casts(castB, 10 ** 6)
                emit_stage()


        def s5b_stage():
            with ExitStack() as es:
                def sb(name, shape, dt=F32):
                    return es.enter_context(nc.sbuf_tensor(uname(name), list(shape), dt))
                TWO_PI = 6.283185307179586
                MAGIC = 12582912.0
                es_t = ExitStack()

                def sbt(name, shape, dt=F32):
                    return es_t.enter_context(nc.sbuf_tensor(uname(name), list(shape), dt))
                PFa = sb("PFa", [128, 64, 33]); PFb = sb("PFb", [128, 64, 33])
                PRa = sb("PRa", [128, 64, 33]); PRb = sb("PRb", [128, 64, 33])
                PNa = sb("PNa", [128, 64, 8]); PNb = sb("PNb", [128, 64, 8])
                PNRa = sb("PNRa", [128, 64, 8]); PNRb = sb("PNRb", [128, 64, 8])
                Qa = sb("Qa", [128, 64, 9]); Qb = sb("Qb", [128, 64, 9])
                X1b = sb("X1b", [128, 64, 16]); X2b = sb("X2b", [128, 64, 16])
                X1c = sb("X1c", [128, 64, 16]); X2c = sb("X2c", [128, 64, 16])
                Dcol = sb("Dcol", [128, 32])
                Jm = sb("Jm", [128, 128]); mkf = sb("mkf", [128, 128]); mkb = sb("mkb", [128, 128])
                ohr = [sb(f"ohr{d}", [128, 384]) for d in range(2)]
                stg = sbt("stg", [128, 128])
                are2 = sbt("are2", [128, 64]); aim2 = sbt("aim2", [128, 64]); dtr = sbt("dtr", [128, 64])
                la = sbt("la", [128, 64]); lb = sbt("lb", [128, 64])
                tA = sbt("tA", [128, 64]); tB = sbt("tB", [128, 64]); tC = sbt("tC", [128, 64]); tD = sbt("tD", [128, 64])
                fre = sbt("fre", [128, 64]); fim = sbt("fim", [128, 64])
                pt = [sbt(f"pt{i}", [128, 64, 16]) for i in range(4)]
                Bre = sbt("Bre", [128, 64, 16]); Bim = sbt("Bim", [128, 64, 16])
                Cre = sbt("Cre", [128, 64, 16]); Cim = sbt("Cim", [128, 64, 16])
                K1 = ['s5k']

                def T(eng, fn, reads=('s5k',), writes=('s5k',)):
                    S.op(eng, fn, reads=list(reads), writes=list(writes))

                S.defer_begin()
                T('pool', lambda e: e.memset(Jm[:], 1.0))
                T('pool', lambda e: e.affine_select(out=Jm[:], in_=Jm[:], pattern=[[1, 128]], compare_op=ALU.is_equal, fill=0.0, base=-64, channel_multiplier=-1))
                T('pool', lambda e: e.memset(mkf[:], -1.0))
                T('pool', lambda e: e.affine_select(out=mkf[:], in_=mkf[:], pattern=[[-1, 128]], compare_op=ALU.is_equal, fill=0.0, base=-64, channel_multiplier=1))
                T('pool', lambda e: e.tensor_tensor(out=Jm[:], in0=Jm[:], in1=mkf[:], op=ALU.add))
                T('pool', lambda e: e.memset(mkf[:], 1.0))
                T('pool', lambda e: e.affine_select(out=mkf[:].rearrange("p (t q) -> p t q", q=16), in_=mkf[:].rearrange("p (t q) -> p t q", q=16), pattern=[[16, 8], [0, 16]], compare_op=ALU.is_ge, fill=0.0, base=15, channel_multiplier=-1))
                T('pool', lambda e: e.memset(mkb[:], 1.0))
                T('pool', lambda e: e.affine_select(out=mkb[:].rearrange("p (t q) -> p t q", q=16), in_=mkb[:].rearrange("p (t q) -> p t q", q=16), pattern=[[-16, 8], [0, 16]], compare_op=ALU.is_ge, fill=0.0, base=0, channel_multiplier=1))
                S.op('sp', lambda e: e.dma_start(out=ohr[0][:], in_=ohf[0].partition_broadcast(128)), writes=['ohr'], dma='ohr')
                S.op('sp', lambda e: e.dma_start(out=ohr[1][:], in_=ohb[0].partition_broadcast(128)), writes=['ohr'], dma='ohr')
                for s_ in range(8):
                    S.op('sp', lambda e, s_=s_: e.dma_start(out=Dcol[s_ * 16:(s_ + 1) * 16, :], in_=P['s5_d'][0].rearrange("(g q) -> q g", q=16), allow_slow_non_contiguous=True), writes=['s5k'], dma='s5p')

                def load_dup(src2d, dst):
                    S.op('sp', lambda e: e.dma_start(out=stg[0:64, 0:64], in_=src2d), writes=['stg'], dma='stg')
                    S.op('sp', lambda e: e.dma_start(out=stg[0:64, 64:128], in_=src2d), writes=['stg'], dma='stg')
                    S.op('pe', lambda e: e.transpose(PS[0][:, 0:64], stg[0:64, :], ident[0:64, 0:64]), reads=['stg', 'ident'], writes=[PK[0]])
                    S.op('dve', lambda e: e.tensor_copy(out=dst, in_=PS[0][:, 0:64]), reads=[PK[0]], writes=['s5k'])
                load_dup(P['s5_a_re'][0].rearrange("d g n -> (d g) n"), are2[:])
                load_dup(P['s5_a_im'][0].rearrange("d g n -> (d g) n"), aim2[:])
                S.op('sp', lambda e: e.dma_start(out=dtr[:], in_=P['s5_log_step'][0].rearrange("d g -> (d g)").partition_broadcast(128)), writes=['s5k'], dma='s5p')
                for (src, dst) in ((P['s5_c_re'][0], Cre), (P['s5_c_im'][0], Cim)):
                    sv = src.rearrange("d g p n -> (d g p) n")
                    dv = dst[:].rearrange("p a q -> p (a q)")
                    for r in range(8):
                        S.op('sp', lambda e, r=r, sv=sv: e.dma_start(out=stg[:, 0:64], in_=sv[r * 128:(r + 1) * 128, :]), writes=['stg'], dma='stg')
                        S.op('sp', lambda e, r=r, sv=sv: e.dma_start(out=stg[:, 64:128], in_=sv[r * 128:(r + 1) * 128, :]), writes=['stg'], dma='stg')
                        S.op('pe', lambda e: e.transpose(PS[0][:, 0:128], stg[:], ident[:]), reads=['stg', 'ident'], writes=[PK[0]])
                        S.op('dve', lambda e, r=r, dv=dv: e.tensor_copy(out=dv[:, r * 128:(r + 1) * 128], in_=PS[0][:, 0:128]), reads=[PK[0]], writes=['s5k'])
                for (src, dst) in ((P['s5_b_re'][0], Bre), (P['s5_b_im'][0], Bim)):
                    sv = src.rearrange("d g n q -> n (d g) q")
                    for hf in range(2):
                        S.op('sp', lambda e, sv=sv, dst=dst, hf=hf: e.dma_start(out=dst[hf * 64:(hf + 1) * 64, :, :], in_=sv), writes=['s5k'], dma='s5p')

                T('act', lambda e: e.activation(out=dtr[:], in_=dtr[:], func=AF.Exp))
                T('dve', lambda e: e.tensor_tensor(out=tA[:], in0=are2[:], in1=dtr[:], op=ALU.mult))
                T('act', lambda e: e.activation(out=tA[:], in_=tA[:], func=AF.Exp))
                T('dve', lambda e: e.tensor_tensor(out=tB[:], in0=aim2[:], in1=dtr[:], op=ALU.mult))

                def sin_reduced(dst, src_ang, shift):
                    T('dve', lambda e: e.tensor_scalar(out=tC[:], in0=src_ang, scalar1=shift, scalar2=None, op0=ALU.add))
                    T('dve', lambda e: e.tensor_scalar(out=tD[:], in0=tC[:], scalar1=1.0 / TWO_PI, scalar2=None, op0=ALU.mult))
                    T('dve', lambda e: e.tensor_scalar(out=tD[:], in0=tD[:], scalar1=MAGIC, scalar2=None, op0=ALU.add))
                    T('dve', lambda e: e.tensor_scalar(out=tD[:], in0=tD[:], scalar1=-MAGIC, scalar2=None, op0=ALU.add))
                    T('dve', lambda e: e.scalar_tensor_tensor(out=tC[:], in0=tD[:], scalar=-TWO_PI, in1=tC[:], op0=ALU.mult, op1=ALU.add))
                    T('dve', lambda e: e.tensor_scalar(out=tC[:], in0=tC[:], scalar1=3.1415925, scalar2=-3.1415925, op0=ALU.min, op1=ALU.max))
                    T('act', lambda e: e.activation(out=dst, in_=tC[:], func=AF.Sin))
                sin_reduced(lb[:], tB[:], 0.0)
                sin_reduced(la[:], tB[:], 1.5707963267948966)
                T('dve', lambda e: e.tensor_tensor(out=la[:], in0=la[:], in1=tA[:], op=ALU.mult))
                T('dve', lambda e: e.tensor_tensor(out=lb[:], in0=lb[:], in1=tA[:], op=ALU.mult))
                T('dve', lambda e: e.tensor_tensor(out=tA[:], in0=are2[:], in1=are2[:], op=ALU.mult))
                T('dve', lambda e: e.tensor_tensor(out=tB[:], in0=aim2[:], in1=aim2[:], op=ALU.mult))
                T('dve', lambda e: e.tensor_tensor(out=tA[:], in0=tA[:], in1=tB[:], op=ALU.add))
                T('dve', lambda e: e.reciprocal(out=tA[:], in_=tA[:]))
                T('dve', lambda e: e.tensor_scalar(out=tB[:], in0=la[:], scalar1=-1.0, scalar2=None, op0=ALU.add))
                T('dve', lambda e: e.tensor_tensor(out=tC[:], in0=tB[:], in1=are2[:], op=ALU.mult))
                T('dve', lambda e: e.tensor_tensor(out=tD[:], in0=lb[:], in1=aim2[:], op=ALU.mult))
                T('dve', lambda e: e.tensor_tensor(out=tC[:], in0=tC[:], in1=tD[:], op=ALU.add))
                T('dve', lambda e: e.tensor_tensor(out=fre[:], in0=tC[:], in1=tA[:], op=ALU.mult))
                T('dve', lambda e: e.tensor_tensor(out=tC[:], in0=lb[:], in1=are2[:], op=ALU.mult))
                T('dve', lambda e: e.tensor_tensor(out=tD[:], in0=tB[:], in1=aim2[:], op=ALU.mult))
                T('dve', lambda e: e.tensor_tensor(out=tC[:], in0=tC[:], in1=tD[:], op=ALU.subtract))
                T('dve', lambda e: e.tensor_tensor(out=fim[:], in0=tC[:], in1=tA[:], op=ALU.mult))
                bc16 = lambda ap: ap.unsqueeze(2).broadcast_to([128, 64, 16])
                T('dve', lambda e: e.tensor_tensor(out=pt[0][:], in0=Bre[:], in1=bc16(fre[:]), op=ALU.mult))
                T('dve', lambda e: e.tensor_tensor(out=pt[1][:], in0=Bim[:], in1=bc16(fim[:]), op=ALU.mult))
                T('dve', lambda e: e.tensor_tensor(out=pt[0][:], in0=pt[0][:], in1=pt[1][:], op=ALU.subtract))
                T('dve', lambda e: e.tensor_tensor(out=pt[2][:], in0=Bim[:], in1=bc16(fre[:]), op=ALU.mult))
                T('dve', lambda e: e.tensor_tensor(out=pt[3][:], in0=Bre[:], in1=bc16(fim[:]), op=ALU.mult))
                T('dve', lambda e: e.tensor_tensor(out=pt[2][:], in0=pt[2][:], in1=pt[3][:], op=ALU.add))
                T('dve', lambda e: e.tensor_copy(out=X1b[0:64], in_=pt[0][0:64]))
                T('dve', lambda e: e.tensor_copy(out=X1b[64:128], in_=pt[2][64:128]))
                T('dve', lambda e: e.tensor_copy(out=X2b[0:64], in_=pt[2][0:64]))
                T('dve', lambda e: e.tensor_scalar(out=X2b[64:128], in0=pt[0][64:128], scalar1=-1.0, scalar2=None, op0=ALU.mult))
                T('dve', lambda e: e.tensor_copy(out=X1c[0:64], in_=Cre[0:64]))
                T('dve', lambda e: e.tensor_scalar(out=X1c[64:128], in0=Cim[64:128], scalar1=-1.0, scalar2=None, op0=ALU.mult))
                T('dve', lambda e: e.tensor_copy(out=X2c[0:64], in_=Cim[0:64]))
                T('dve', lambda e: e.tensor_copy(out=X2c[64:128], in_=Cre[64:128]))

                def cmul(oa, ob, xa, xb, ya, yb, shape):
                    n = 1
                    for v in shape[1:]:
                        n *= v
                    tv = [pt[i][:].rearrange("p a q -> p (a q)")[:, 0:n] for i in range(4)]
                    if len(shape) == 3:
                        tv = [t.rearrange("p (a k) -> p a k", k=shape[2]) for t in tv]
                    T('dve', lambda e: e.tensor_tensor(out=tv[0], in0=xa, in1=ya, op=ALU.mult))
                    T('dve', lambda e: e.tensor_tensor(out=tv[1], in0=xb, in1=yb, op=ALU.mult))
                    T('dve', lambda e: e.tensor_tensor(out=tv[2], in0=xa, in1=yb, op=ALU.mult))
                    T('dve', lambda e: e.tensor_tensor(out=tv[3], in0=xb, in1=ya, op=ALU.mult))
                    T('dve', lambda e: e.tensor_tensor(out=oa, in0=tv[0], in1=tv[1], op=ALU.subtract))
                    T('dve', lambda e: e.tensor_tensor(out=ob, in0=tv[2], in1=tv[3], op=ALU.add))
                T('dve', lambda e: e.memset(PFa[:, :, 0:1], 1.0))
                T('dve', lambda e: e.memset(PFb[:, :, 0:1], 0.0))
                T('dve', lambda e: e.tensor_copy(out=PFa[:, :, 1:2], in_=la[:].unsqueeze(2)))
                T('dve', lambda e: e.tensor_copy(out=PFb[:, :, 1:2], in_=lb[:].unsqueeze(2)))
                k0 = 2
                while k0 <= 16:
                    h_ = k0 // 2
                    cmul(PFa[:, :, k0:k0 + 1], PFb[:, :, k0:k0 + 1], PFa[:, :, h_:h_ + 1], PFb[:, :, h_:h_ + 1], PFa[:, :, h_:h_ + 1], PFb[:, :, h_:h_ + 1], [128, 64, 1])
                    if k0 > 2 or True:
                        bk = lambda ap, k0=k0: ap.broadcast_to([128, 64, k0 - 1])
                        cmul(PFa[:, :, k0 + 1:2 * k0], PFb[:, :, k0 + 1:2 * k0], PFa[:, :, 1:k0], PFb[:, :, 1:k0], bk(PFa[:, :, k0:k0 + 1]), bk(PFb[:, :, k0:k0 + 1]), [128, 64, k0 - 1])
                    k0 *= 2
                cmul(PFa[:, :, 32:33], PFb[:, :, 32:33], PFa[:, :, 16:17], PFb[:, :, 16:17], PFa[:, :, 16:17], PFb[:, :, 16:17], [128, 64, 1])
                for k in range(33):
                    T('pool', lambda e, k=k: e.tensor_copy(out=PRa[:, :, k:k + 1], in_=PFa[:, :, 32 - k:33 - k]))
                    T('pool', lambda e, k=k: e.tensor_copy(out=PRb[:, :, k:k + 1], in_=PFb[:, :, 32 - k:33 - k]))
                T('dve', lambda e: e.tensor_tensor(out=tA[:], in0=la[:], in1=la[:], op=ALU.mult))
                T('dve', lambda e: e.tensor_tensor(out=tB[:], in0=lb[:], in1=lb[:], op=ALU.mult))
                T('dve', lambda e: e.tensor_tensor(out=tA[:], in0=tA[:], in1=tB[:], op=ALU.add))
                T('dve', lambda e: e.reciprocal(out=tA[:], in_=tA[:]))
                T('dve', lambda e: e.memset(PNa[:, :, 0:1], 1.0))
                T('dve', lambda e: e.memset(PNb[:, :, 0:1], 0.0))
                T('dve', lambda e: e.tensor_tensor(out=PNa[:, :, 1:2], in0=la[:].unsqueeze(2), in1=tA[:].unsqueeze(2), op=ALU.mult))
                T('dve', lambda e: e.scalar_tensor_tensor(out=PNb[:, :, 1:2], in0=lb[:].unsqueeze(2), scalar=-1.0, in1=tA[:].unsqueeze(2), op0=ALU.mult, op1=ALU.mult))
                for k in range(2, 8):
                    cmul(PNa[:, :, k:k + 1], PNb[:, :, k:k + 1], PNa[:, :, k - 1:k], PNb[:, :, k - 1:k], PNa[:, :, 1:2], PNb[:, :, 1:2], [128, 64, 1])
                for k in range(8):
                    T('pool', lambda e, k=k: e.tensor_copy(out=PNRa[:, :, k:k + 1], in_=PNa[:, :, 7 - k:8 - k]))
                    T('pool', lambda e, k=k: e.tensor_copy(out=PNRb[:, :, k:k + 1], in_=PNb[:, :, 7 - k:8 - k]))
                T('dve', lambda e: e.tensor_copy(out=Qa[:, :, 0:1], in_=PFa[:, :, 32:33]))
                T('dve', lambda e: e.tensor_copy(out=Qb[:, :, 0:1], in_=PFb[:, :, 32:33]))
                for j in range(1, 9):
                    cmul(Qa[:, :, j:j + 1], Qb[:, :, j:j + 1], Qa[:, :, j - 1:j], Qb[:, :, j - 1:j], Qa[:, :, j - 1:j], Qb[:, :, j - 1:j], [128, 64, 1])

                prep_list = S.defer_end()
                nb = norm_bufs(es_t)
                hn = [sbt(f"hn{i}", [128, 16, 512], BF16) for i in range(2)]
                wub = sbt("wub", [128, 16, 512], BF16)
                usb = [sbt(f"usb{i}", [128, 4, 8, 64]) for i in range(2)]
                S.op('pool', lambda e: e.dma_start(out=wub[:], in_=P['e_w_in'][0][:, 3072:3584].rearrange("(kt p) c -> p kt c", p=128)), writes=['wub'], dma='wub')
                jobs = []
                for i in range(NT):
                    jobs.append((xe, 512 * i, 4, A1[:, 0, :], modc[:, 0, 0:16], 'A1', 'modc', [(Us_own, 'Us_own', 64 * i)]))
                for i in range(FAR // 512):
                    jobs.append((xfar, 512 * i, 4, A1[:, 0, :], modc[:, 0, 0:16], 'A1', 'modc', [(Us_far, 'Us_far', 32 + 64 * i)]))
                jobs.append((ctxb, 0, 2, Ac[:], modx[:, 0:16], 'Ac', 'modx', [(Us_far, 'Us_far', 0), (Us_far, 'Us_far', 1504)]))
                norm_pre(nb, jobs[0][0], jobs[0][1], 0, use_pow=True)
                pend_st = []

                def pop_store():
                    if pend_st:
                        fn, rk, dk = pend_st.pop(0)
                        S.op('pool', fn, reads=[rk], writes=[dk], dma='st_us')
                for n, (src, tok0, nblk, Acol, Bcol, akey, bkey, dsts) in enumerate(jobs):
                    hb = n % 2
                    ntok = nblk * 128
                    nj = ntok // 8
                    for blk in range(nblk):
                        norm_post(nb, blk, Acol, Bcol, akey, bkey, hn[hb], f"hn{hb}", evac_eng='act')
                        if blk + 1 < nblk:
                            norm_pre(nb, src, tok0, blk + 1, use_pow=True)
                        elif n + 1 < len(jobs):
                            norm_pre(nb, jobs[n + 1][0], jobs[n + 1][1], 0, use_pow=True)
                        pop_store()
                        S.replay(prep_list, 3)
                    for ot in range(4):
                        pb = ot % 2
                        for kt in range(16):
                            S.op('pe', lambda e, kt=kt, ot=ot, pb=pb, hb=hb, ntok=ntok: e.matmul(PS[pb][:, 0:ntok], lhsT=wub[:, kt, ot * 128:(ot + 1) * 128], rhs=hn[hb][:, kt, 0:ntok], start=(kt == 0), stop=(kt == 15)), reads=['wub', f"hn{hb}"], writes=[PK[pb]])
                        S.op('act', lambda e, ot=ot, pb=pb, hb=hb, ntok=ntok, nj=nj: e.copy(out=usb[hb][:, ot, :, 0:nj], in_=PS[pb][:, 0:ntok].rearrange("p (j s) -> p s j", s=8)), reads=[PK[pb]], writes=[f"usb{hb}"])
                    for (dst, dkey, j0) in dsts:
                        for ot in range(4):
                            pend_st.append(((lambda e, dst=dst, j0=j0, ot=ot, hb=hb, nj=nj: e.dma_start(out=dst[:, ot * 128:(ot + 1) * 128, j0:j0 + nj].rearrange("s c j -> c s j"), in_=usb[hb][:, ot, :, 0:nj])), f"usb{hb}", dkey))
                while pend_st:
                    pop_store()
                S.replay(prep_list, 10 ** 6)
                emit_stage()
                es_t.close()
                WT = [[sb(f"WT{d}_{b}", [128, 32, 16]) for b in range(2)] for d in range(2)]
                VV = [[sb(f"VV{d}_{b}", [128, 32, 16]) for b in range(2)] for d in range(2)]
                VN = [[sb(f"VN{d}_{b}", [128, 8, 16]) for b in range(2)] for d in range(2)]
                WS = [[sb(f"WS{d}_{b}", [128, 4, 128]) for b in range(2)] for d in range(2)]
                t1 = [sb(f"t1_{i}", [128, 32, 16]) for i in range(2)]
                t2 = [sb(f"t2_{i}", [128, 32, 16]) for i in range(2)]
                Mg = [sb(f"Mg{b}", [128, 7, 128]) for b in range(2)]
                Lm = [[sb(f"Lm{d}_{b}", [128, 9, 128]) for b in range(2)] for d in range(2)]
                Xfar = [sb(f"Xfar{d}", [128, 376]) for d in range(2)]
                XO = [sb(f"XO{d}", [128, 145]) for d in range(2)]
                cvec_ = [sb(f"cv{d}", [128, 1]) for d in range(2)]
                junk = sb("junk", [128, 376])
                uo = [sb(f"uo{b}", [128, 4, 576]) for b in range(2)]
                uf = sb("uf", [128, 2, 1536])
                ysb = sb("ysb", [128, 4, 576])
                nt = [0]
                def gen_part(g):
                    emit_casts(castA, 3)
                    gl = g % 4
                    gb = g % 2
                    ub = (g // 4) % 2
                    if gl == 0:
                        for s_ in range(8):
                            S.op('sp', lambda e, s_=s_, g=g, ub=ub: e.dma_start(out=uo[ub][s_ * 16:(s_ + 1) * 16, :, :], in_=Us_own[s_, g * 16:(g + 4) * 16, :].rearrange("(g q) j -> q g j", q=16)), reads=['Us_own'], writes=[f"uo{ub}"], dma=f"uo{ub}")
                    if g % 2 == 0:
                        for s_ in range(8):
                            S.op('sp', lambda e, s_=s_, g=g: e.dma_start(out=uf[s_ * 16:(s_ + 1) * 16, :, :], in_=Us_far[s_, g * 16:(g + 2) * 16, :].rearrange("(g q) j -> q g j", q=16)), reads=['Us_far'], writes=["uf"], dma="uf")
                    for d in range(2):
                        dg = d * 32 + g
                        tabs_w = (PRa[:, dg, 1:33], PRb[:, dg, 1:33]) if d == 0 else (PFa[:, dg, 0:32], PFb[:, dg, 0:32])
                        tabs_v = (PFa[:, dg, 1:33], PFb[:, dg, 1:33]) if d == 0 else (PRa[:, dg, 0:32], PRb[:, dg, 0:32])
                        tabs_n = (PNRa[:, dg, :], PNRb[:, dg, :]) if d == 0 else (PNa[:, dg, :], PNb[:, dg, :])
                        for (dst, dkey, x1, x2, tabs, ns) in ((WT[d][gb], f"WT{d}_{gb}", X1b, X2b, tabs_w, 32), (VV[d][gb], f"VV{d}_{gb}", X1c, X2c, tabs_v, 32), (VN[d][gb], f"VN{d}_{gb}", X1c, X2c, tabs_n, 8)):
                            tb = nt[0] % 2; nt[0] += 1
                            xa = x1[:, dg, :].unsqueeze(1).broadcast_to([128, ns, 16])
                            xb = x2[:, dg, :].unsqueeze(1).broadcast_to([128, ns, 16])
                            pa = tabs[0].unsqueeze(2).broadcast_to([128, ns, 16])
                            pb_ = tabs[1].unsqueeze(2).broadcast_to([128, ns, 16])
                            S.op('dve', lambda e, tb=tb, xa=xa, pa=pa, ns=ns: e.tensor_tensor(out=t1[tb][:, 0:ns, :], in0=xa, in1=pa, op=ALU.mult), reads=['s5k'], writes=[f"t1_{tb}"])
                            S.op('pool', lambda e, tb=tb, xb=xb, pb_=pb_, ns=ns: e.tensor_tensor(out=t2[tb][:, 0:ns, :], in0=xb, in1=pb_, op=ALU.mult), reads=['s5k'], writes=[f"t2_{tb}"])
                            S.op('dve', lambda e, tb=tb, dst=dst, ns=ns: e.tensor_tensor(out=dst[:, 0:ns, :], in0=t1[tb][:, 0:ns, :], in1=t2[tb][:, 0:ns, :], op=ALU.subtract), reads=[f"t1_{tb}", f"t2_{tb}"], writes=[dkey])
                        wtv = WT[d][gb][:].rearrange("p (h s) q -> p h (s q)", h=4)
                        for sh_ in range(4):
                            S.op('pe', lambda e, sh_=sh_, wtv=wtv: e.transpose(PS[0][:, sh_ * 128:(sh_ + 1) * 128], wtv[:, sh_, :], ident[:]), reads=[f"WT{d}_{gb}", 'ident'], writes=[PK[0]])
                        S.op('act', lambda e, d=d, gb=gb: e.copy(out=WS[d][gb][:].rearrange("p h c -> p (h c)"), in_=PS[0][:]), reads=[PK[0]], writes=[f"WS{d}_{gb}"])
                        for j in range(9):
                            S.op('pool', lambda e, d=d, gb=gb, j=j, dg=dg: e.tensor_scalar(out=Lm[d][gb][:, j, :], in0=ident[:], scalar1=Qa[:, dg, j:j + 1], scalar2=0.0, op0=ALU.mult, op1=ALU.add), reads=['ident', 's5k'], writes=[f"Lm{d}_{gb}"])
                            S.op('dve', lambda e, d=d, gb=gb, j=j, dg=dg: e.scalar_tensor_tensor(out=Lm[d][gb][:, j, :], in0=Jm[:], scalar=Qb[:, dg, j:j + 1], in1=Lm[d][gb][:, j, :], op0=ALU.mult, op1=ALU.add), reads=['s5k', f"Lm{d}_{gb}"], writes=[f"Lm{d}_{gb}"])
                    vv = [VV[d][gb][:].rearrange("p (h t) q -> p h (t q)", h=4) for d in range(2)]
                    wv = [WT[d][gb][:].rearrange("p (h s) q -> p h (s q)", h=4) for d in range(2)]
                    vn = [VN[d][gb][:].rearrange("p t q -> p (t q)") for d in range(2)]
                    rk = [f"WT0_{gb}", f"WT1_{gb}", f"VV0_{gb}", f"VV1_{gb}", f"VN0_{gb}", f"VN1_{gb}"]
                    for dl in range(1, 4):
                        S.op('pe', lambda e, dl=dl: e.matmul(PS[1][:, (dl - 1) * 128:dl * 128], lhsT=wv[0][:, 3, :], rhs=vv[0][:, dl - 1, :], start=True, stop=True, skip_group_check=True), reads=rk, writes=[PK[1]])
                        S.op('pe', lambda e, dl=dl: e.matmul(PS[2][:, (dl - 1) * 128:dl * 128], lhsT=wv[1][:, 0, :], rhs=vv[1][:, 4 - dl, :], start=True, stop=True, skip_group_check=True), reads=rk, writes=[PK[2]])
                    S.op('pe', lambda e: e.matmul(PS[1][:, 384:512], lhsT=wv[0][:, 3, :], rhs=vn[0], start=True, stop=True, skip_group_check=True), reads=rk, writes=[PK[1]])
                    S.op('pe', lambda e: e.matmul(PS[2][:, 384:512], lhsT=wv[1][:, 0, :], rhs=vn[1], start=True, stop=True, skip_group_check=True), reads=rk, writes=[PK[2]])
                    mk = f"Mg{gb}"
                    S.op('act', lambda e, gb=gb: e.copy(out=Mg[gb][:, 4:7, :].rearrange("p a c -> p (a c)"), in_=PS[1][:, 0:384]), reads=[PK[1]], writes=[mk])
                    for dl in range(1, 4):
                        S.op('act', lambda e, gb=gb, dl=dl: e.copy(out=Mg[gb][:, 3 - dl, :], in_=PS[2][:, (dl - 1) * 128:dl * 128]), reads=[PK[2]], writes=[mk])
                    S.op('dve', lambda e, gb=gb: e.tensor_tensor(out=Mg[gb][:, 3, :], in0=PS[1][:, 384:512], in1=mkf[:], op=ALU.mult), reads=[PK[1], 's5k'], writes=[mk])
                    S.op('dve', lambda e, gb=gb: e.tensor_tensor(out=junk[:, 0:128], in0=PS[2][:, 384:512], in1=mkb[:], op=ALU.mult), reads=[PK[2], 's5k'], writes=['junk'])
                    S.op('dve', lambda e, gb=gb: e.tensor_tensor(out=Mg[gb][:, 3, :], in0=Mg[gb][:, 3, :], in1=junk[:, 0:128], op=ALU.add), reads=[mk, 'junk'], writes=[mk])
                    S.op('dve', lambda e, gb=gb, g=g: e.scalar_tensor_tensor(out=Mg[gb][:, 3, :], in0=ident[:], scalar=Dcol[:, g:g + 1], in1=Mg[gb][:, 3, :], op0=ALU.mult, op1=ALU.add), reads=[mk, 'ident', 's5k'], writes=[mk])
                def main_part(g, fill):
                    gl = g % 4
                    gb = g % 2
                    ub = (g // 4) % 2
                    mk = f"Mg{gb}"
                    vv = [VV[d][gb][:].rearrange("p (h t) q -> p h (t q)", h=4) for d in range(2)]
                    for d in range(2):
                        b0 = 0 if d == 0 else 32
                        for sh_ in range(4):
                            S.op('pe', lambda e, d=d, sh_=sh_, b0=b0, gl=gl: e.matmul(PS[3][:, 0:376], lhsT=WS[d][gb][:, sh_, :], rhs=uf[:, g % 2, b0 + sh_:b0 + sh_ + 1501:4], start=(sh_ == 0), stop=(sh_ == 3)), reads=[f"WS{d}_{gb}", 'uf'], writes=[PK[3]])
                        S.op('act', lambda e, d=d: e.copy(out=Xfar[d][:], in_=PS[3][:, 0:376]), reads=[PK[3]], writes=[f"Xfar{d}"])
                    for j in range(9):
                        sh = 1 << j
                        n_ = 376 - sh
                        for d in range(2):
                            src = Xfar[d][:, 0:n_] if d == 0 else Xfar[d][:, sh:376]
                            dst = Xfar[d][:, sh:376] if d == 0 else Xfar[d][:, 0:n_]
                            S.op('pe', lambda e, d=d, j=j, src=src, n_=n_: e.matmul(PS[4 + d][:, 0:n_], lhsT=Lm[d][gb][:, j, :], rhs=src, start=True, stop=True), reads=[f"Lm{d}_{gb}", f"Xfar{d}"], writes=[PK[4 + d]])
                            S.op('dve', lambda e, d=d, dst=dst, n_=n_: e.tensor_tensor(out=dst, in0=PS[4 + d][:, 0:n_], in1=dst, op=ALU.add), reads=[PK[4 + d], f"Xfar{d}"], writes=[f"Xfar{d}"])
                        fill(6)
                    for d in range(2):
                        S.op('dve', lambda e, d=d: e.scalar_tensor_tensor(out=junk[:], in0=Xfar[d][:], scalar=1.0, in1=ohr[d][:, 0:376], op0=ALU.mult, op1=ALU.mult, accum_out=cvec_[d][:]), reads=[f"Xfar{d}", 'ohr'], writes=['junk', f"cv{d}"])
                    for d in range(2):
                        for sh_ in range(4):
                            S.op('pe', lambda e, d=d, sh_=sh_, gl=gl, ub=ub: e.matmul(PS[3][:, 0:144], lhsT=WS[d][gb][:, sh_, :], rhs=uo[ub][:, gl, sh_:sh_ + 573:4], start=(sh_ == 0), stop=(sh_ == 3)), reads=[f"WS{d}_{gb}", f"uo{ub}"], writes=[PK[3]])
                        if d == 0:
                            S.op('act', lambda e: e.copy(out=XO[0][:, 1:145], in_=PS[3][:, 0:144]), reads=[PK[3]], writes=["XO0"])
                            S.op('dve', lambda e: e.tensor_copy(out=XO[0][:, 0:1], in_=cvec_[0][:]), reads=["cv0"], writes=["XO0"])
                        else:
                            S.op('act', lambda e: e.copy(out=XO[1][:, 0:144], in_=PS[3][:, 0:144]), reads=[PK[3]], writes=["XO1"])
                            S.op('dve', lambda e: e.tensor_copy(out=XO[1][:, 144:145], in_=cvec_[1][:]), reads=["cv1"], writes=["XO1"])
                    for j in range(8):
                        sh = 1 << j
                        n_ = 145 - sh
                        for d in range(2):
                            src = XO[d][:, 0:n_] if d == 0 else XO[d][:, sh:145]
                            dst = XO[d][:, sh:145] if d == 0 else XO[d][:, 0:n_]
                            S.op('pe', lambda e, d=d, j=j, src=src, n_=n_: e.matmul(PS[4 + d][:, 0:n_], lhsT=Lm[d][gb][:, j, :], rhs=src, start=True, stop=True), reads=[f"Lm{d}_{gb}", f"XO{d}"], writes=[PK[4 + d]])
                            S.op('dve', lambda e, d=d, dst=dst, n_=n_: e.tensor_tensor(out=dst, in0=PS[4 + d][:, 0:n_], in1=dst, op=ALU.add), reads=[PK[4 + d], f"XO{d}"], writes=[f"XO{d}"])
                        fill(6)
                    fill(10000)
                    for th in range(4):
                        yb = 6 + th // 2
                        c0 = (th % 2) * 144
                        for sh_ in range(4):
                            S.op('pe', lambda e, th=th, sh_=sh_, yb=yb, c0=c0: e.matmul(PS[yb][:, c0:c0 + 144], lhsT=Mg[gb][:, th - sh_ + 3, :], rhs=uo[ub][:, gl, sh_:sh_ + 573:4], start=(sh_ == 0), stop=False, skip_group_check=True), reads=[mk, f"uo{ub}"], writes=[PK[yb]])
                        for d in range(2):
                            hsrc = XO[0][:, 0:144] if d == 0 else XO[1][:, 1:145]
                            S.op('pe', lambda e, th=th, d=d, yb=yb, c0=c0, hsrc=hsrc: e.matmul(PS[yb][:, c0:c0 + 144], lhsT=vv[d][:, th, :], rhs=hsrc, start=False, stop=(d == 1), skip_group_check=True), reads=[f"VV{d}_{gb}", f"XO{d}"], writes=[PK[yb]])
                    for hh in range(2):
                        yb = 6 + hh
                        S.op('act', lambda e, hh=hh, yb=yb: e.copy(out=ysb[:, gl, :].rearrange("p (j t) -> p t j", t=4)[:, 2 * hh:2 * hh + 2, :], in_=PS[yb][:, 0:288].rearrange("p (t j) -> p t j", t=2)), reads=[PK[yb]], writes=['ysb'])
                    if gl == 3:
                        g0 = g - 3
                        for tl in range(8):
                            S.op('sp', lambda e, tl=tl, g0=g0: e.dma_start(out=Ys[g0 * 16:(g0 + 4) * 16, tl, 8:584].rearrange("(g p) j -> p g j", p=16), in_=ysb[tl * 16:(tl + 1) * 16, :, :]), reads=['ysb'], writes=['Ys'], dma='st_ys')
                S.defer_begin(); gen_part(0); pend = S.defer_end()
                S.replay(pend, 10000)
                for g in range(32):
                    if g + 1 < 32:
                        S.defer_begin(); gen_part(g + 1); pend = S.defer_end()
                    else:
                        pend = []
                    main_part(g, lambda n, pend=pend: S.replay(pend, n))
                emit_casts(castA, 10 ** 6)
                emit_stage()

        if 's5' in stages or 'l0mix' in stages:
            s5b_stage()
        if 'l0mix' in stages or 'l0mix_nos5' in stages:
            l0mix_stage(xe, h1, 'h1', use_s5=('l0mix' in stages))
        if 'l0ffn' in stages:
            ffn_stage(0, h1 if ('l0mix' in stages or 'l0mix_nos5' in stages) else xe, 'h1', h2, 'h2')
        if 'l1mix' in stages:
            l1mix_stage(h2, h3, 'h3')
        if 'l1ffn' in stages:
            ffn_stage(1, h3, 'h3', h1, 'h1')
            final_stage(h1, out)
    return nc


_NC_CACHE = {}


def make_in_maps(inputs):
    x = np.ascontiguousarray(inputs['x'], dtype=np.float32)
    c = np.asarray(inputs['c'], dtype=np.float32)
    ctx = np.asarray(inputs['ctx'], dtype=np.float32)
    c_ctx = np.asarray(inputs['c_ctx'], dtype=np.float32)
    pr = {n: np.ascontiguousarray(np.asarray(inputs[n], dtype=np.float32)) for n, _ in PARAMS}
    maps, offs = [], []
    for k in range(8):
        b, q = k // 4, k % 4
        w0 = min(max(q * 4096 - 256, 0), 16384 - TE)
        offs.append((b, q, q * 4096 - w0))
        nbc = w0 // 32
        ohf = np.zeros((1, 384), np.float32); ohf[0, 7 + nbc] = 1.0
        ohb = np.zeros((1, 384), np.float32); ohb[0, nbc] = 1.0
        m = dict(pr)
        m['xe'] = np.ascontiguousarray(x[b, w0:w0 + TE])
        m['xfar'] = np.ascontiguousarray(np.concatenate([x[b, :w0], x[b, w0 + TE:]], axis=0))
        m['ctxb'] = np.ascontiguousarray(ctx[b])
        m['cvec'] = np.ascontiguousarray(np.stack([c[b], c_ctx], axis=0))
        m['ohf'] = ohf
        m['ohb'] = ohb
        maps.append(m)
    return maps, offs


def kernel(**inputs):
    maps, offs = make_in_maps(inputs)
    if 'nc' not in _NC_CACHE:
        _NC_CACHE['nc'] = build()
    res = run_bass_kernel_spmd(_NC_CACHE['nc'], maps, core_ids=list(range(8)))
    out = np.empty((2, 16384, D), np.float32)
    for k, (b, q, off) in enumerate(offs):
        out[b, q * 4096:(q + 1) * 4096] = res.results[k]['out'][off:off + 4096]
    return out
```
